# Optimizing a Trainium2 kernel written in Bass

```python
import math
import jax
import jax.numpy as jnp
from jax import lax
import numpy as np

D_MODEL = 1024
BATCH = 2
SEQ = 16384
DEPTH = 1
DEC_BATCH = 16
DEC_SEQ = 32
PAST_LEN = 1024

CHUNK = 64
MIX = D_MODEL
DN_WIDTH = MIX // 2
S5_WIDTH = MIX - DN_WIDTH
DN_HEADS = 4
DN_DK = DN_WIDTH // DN_HEADS
DN_DV = DN_WIDTH // DN_HEADS
CONV_K = 4
CONV_CH = 2 * DN_HEADS * DN_DK + DN_HEADS * DN_DV
S5_GROUP = 16
S5_GROUPS = S5_WIDTH // S5_GROUP
S5_STATE = 64
IN_COLS = CONV_CH + 2 * DN_HEADS + DN_WIDTH + 2 * S5_WIDTH
EPS = 1e-6
L2_EPS = 1e-6

kernel_name = "hymba_gdn_s5_stream_step"


def rms_norm(x, w):
    xf = x.astype(jnp.float32)
    return xf * lax.rsqrt(jnp.mean(xf * xf, axis=-1, keepdims=True) + EPS) * w.astype(jnp.float32)


def l2_normalize(x):
    return x * lax.rsqrt(jnp.sum(x * x, axis=-1, keepdims=True) + L2_EPS)


def causal_short_conv(u, buf, w):
    T = u.shape[1]
    up = jnp.concatenate([buf, u], axis=1)
    out = up[:, 0:T] * w[0]
    for j in range(1, CONV_K):
        out = out + up[:, j:j + T] * w[j]
    return jax.nn.silu(out), up[:, T:]


def gated_delta_rule(q, k, v, g, beta, s0):
    bsz, T, H, _ = q.shape
    blk = min(CHUNK, T)
    nb = T // blk

    def to_blocks(t):
        return t.reshape(bsz, nb, blk, H, -1).transpose(1, 0, 3, 2, 4)

    q, k, v = to_blocks(q), to_blocks(k), to_blocks(v)
    g = g.reshape(bsz, nb, blk, H).transpose(1, 0, 3, 2)
    beta = beta.reshape(bsz, nb, blk, H).transpose(1, 0, 3, 2)
    gc = jnp.cumsum(g, axis=-1)
    incl = jnp.tril(jnp.ones((blk, blk), dtype=bool))
    strict = jnp.tril(jnp.ones((blk, blk), dtype=bool), k=-1)
    diff = gc[..., :, None] - gc[..., None, :]
    decay = jnp.where(incl, jnp.exp(jnp.where(incl, diff, 0.0)), 0.0)
    kb = k * beta[..., None]
    eye = jnp.eye(blk, dtype=jnp.float32)
    lower = jnp.where(strict, jnp.einsum('nbhid,nbhjd->nbhij', kb, k) * decay, 0.0) + eye
    u = lax.linalg.triangular_solve(lower, v * beta[..., None], left_side=True, lower=True,
                                    unit_diagonal=True)
    w = lax.linalg.triangular_solve(lower, kb * jnp.exp(gc)[..., None], left_side=True, lower=True,
                                    unit_diagonal=True)
    attn = jnp.where(incl, jnp.einsum('nbhid,nbhjd->nbhij', q, k) * decay, 0.0)

    def step(S, xs):
        q_b, k_b, u_b, w_b, a_b, gc_b = xs
        v_new = u_b - jnp.einsum('bhcd,bhde->bhce', w_b, S)
        o = (jnp.einsum('bhcd,bhde->bhce', q_b * jnp.exp(gc_b)[..., None], S)
             + jnp.einsum('bhij,bhje->bhie', a_b, v_new))
        g_last = gc_b[..., -1]
        k_dec = k_b * jnp.exp(g_last[..., None] - gc_b)[..., None]
        S = S * jnp.exp(g_last)[..., None, None] + jnp.einsum('bhcd,bhce->bhde', k_dec, v_new)
        return S, o

    s_final, o = lax.scan(step, s0, (q, k, u, w, attn, gc))
    o = o.transpose(1, 0, 3, 2, 4).reshape(bsz, T, H, -1)
    return o, s_final


def s5_ssm(u, x0_re, x0_im, A_re, A_im, log_dt, B_re, B_im, C_re, C_im, D):
    bsz, T, _ = u.shape
    ug = u.reshape(bsz, T, S5_GROUPS, S5_GROUP)
    lam_re = jnp.minimum(A_re.astype(jnp.float32), -1e-4)
    lam_im = A_im.astype(jnp.float32)
    dt = jnp.exp(log_dt.astype(jnp.float32))[:, None]
    ldt_re, ldt_im = lam_re * dt, lam_im * dt
    mag = jnp.exp(ldt_re)
    lb_re, lb_im = mag * jnp.cos(ldt_im), mag * jnp.sin(ldt_im)
    den = lam_re * lam_re + lam_im * lam_im
    f_re = ((lb_re - 1.0) * lam_re + lb_im * lam_im) / den
    f_im = (lb_im * lam_re - (lb_re - 1.0) * lam_im) / den
    B_re = B_re.astype(jnp.float32)
    B_im = B_im.astype(jnp.float32)
    bb_re = f_re[..., None] * B_re - f_im[..., None] * B_im
    bb_im = f_re[..., None] * B_im + f_im[..., None] * B_re
    bu_re = jnp.einsum('btgc,gnc->btgn', ug, bb_re)
    bu_im = jnp.einsum('btgc,gnc->btgn', ug, bb_im)
    a_re = jnp.broadcast_to(lb_re, (1, T, S5_GROUPS, S5_STATE))
    a_im = jnp.broadcast_to(lb_im, (1, T, S5_GROUPS, S5_STATE))

    def combine(e1, e2):
        a1r, a1i, b1r, b1i = e1
        a2r, a2i, b2r, b2i = e2
        return (a1r * a2r - a1i * a2i,
                a1r * a2i + a1i * a2r,
                a2r * b1r - a2i * b1i + b2r,
                a2r * b1i + a2i * b1r + b2i)

    _, _, xr, xi = lax.associative_scan(combine, (a_re, a_im, bu_re, bu_im), axis=1)
    kpow = jnp.arange(1, T + 1, dtype=jnp.float32)[:, None, None]
    p_mag = jnp.exp(ldt_re * kpow)
    p_re, p_im = p_mag * jnp.cos(ldt_im * kpow), p_mag * jnp.sin(ldt_im * kpow)
    x0_re = x0_re[:, None]
    x0_im = x0_im[:, None]
    xr = xr + p_re * x0_re - p_im * x0_im
    xi = xi + p_re * x0_im + p_im * x0_re
    y = (jnp.einsum('btgn,gcn->btgc', xr, C_re.astype(jnp.float32))
         - jnp.einsum('btgn,gcn->btgc', xi, C_im.astype(jnp.float32))
         + D.astype(jnp.float32).reshape(S5_GROUPS, S5_GROUP) * ug)
    return y.reshape(bsz, T, S5_WIDTH), xr[:, -1], xi[:, -1]


def hybrid_layer(x, conv_buf, s_dn, s5_re, s5_im, norm_w, w_in, conv_w, dn_A_log, dn_dt_bias,
                 dn_norm_w, s5_A_re, s5_A_im, s5_log_dt, s5_B_re, s5_B_im, s5_C_re, s5_C_im, s5_D,
                 glu_w, glu_b, w_out):
    bsz, T, _ = x.shape
    h = rms_norm(x, norm_w).astype(x.dtype)
    p = jnp.einsum('btd,de->bte', h, w_in).astype(jnp.float32)
    o0 = CONV_CH
    o1 = o0 + DN_HEADS
    o2 = o1 + DN_HEADS
    o3 = o2 + DN_WIDTH
    o4 = o3 + S5_WIDTH
    qkv_raw, a_logit, b_logit = p[..., :o0], p[..., o0:o1], p[..., o1:o2]
    z_dn, u_s5, z_s5 = p[..., o2:o3], p[..., o3:o4], p[..., o4:]

    qkv, new_buf = causal_short_conv(qkv_raw, conv_buf.astype(jnp.float32), conv_w.astype(jnp.float32))
    nqk = DN_HEADS * DN_DK
    q = qkv[..., :nqk].reshape(bsz, T, DN_HEADS, DN_DK)
    k = qkv[..., nqk:2 * nqk].reshape(bsz, T, DN_HEADS, DN_DK)
    v = qkv[..., 2 * nqk:].reshape(bsz, T, DN_HEADS, DN_DV)
    q = l2_normalize(q) * (DN_DK ** -0.5)
    k = l2_normalize(k)
    beta = jax.nn.sigmoid(b_logit)
    g = -jnp.exp(dn_A_log.astype(jnp.float32)) * jax.nn.softplus(a_logit + dn_dt_bias.astype(jnp.float32))
    o_dn, s_dn_new = gated_delta_rule(q, k, v, g, beta, s_dn.astype(jnp.float32))
    o_dn = rms_norm(o_dn, dn_norm_w) * jax.nn.silu(z_dn.reshape(bsz, T, DN_HEADS, DN_DV))

    y_s5, re_new, im_new = s5_ssm(u_s5, s5_re.astype(jnp.float32), s5_im.astype(jnp.float32),
                                  s5_A_re, s5_A_im, s5_log_dt, s5_B_re, s5_B_im, s5_C_re, s5_C_im, s5_D)
    gy = jax.nn.gelu(y_s5)
    o_s5 = gy * jax.nn.sigmoid(gy @ glu_w.astype(jnp.float32) + glu_b.astype(jnp.float32)) * jax.nn.silu(z_s5)

    mixed = jnp.concatenate([o_dn.reshape(bsz, T, DN_WIDTH), o_s5], axis=-1).astype(x.dtype)
    x_new = x + jnp.einsum('bte,ed->btd', mixed, w_out)
    return x_new, new_buf, s_dn_new, re_new, im_new


def setup_inputs(seed: int = 0) -> dict:
    key = jax.random.key(seed)
    ks = jax.random.split(key, 24)
    f32 = jnp.float32
    nrm = lambda k, s, sc: jax.random.normal(k, s, f32) * sc
    x_prompt = nrm(ks[0], (BATCH, SEQ, D_MODEL), 1.0)
    x_sample = nrm(ks[1], (DEC_BATCH, DEC_SEQ, D_MODEL), 1.0)
    cache_conv = nrm(ks[2], (DEPTH, DEC_BATCH, CONV_K - 1, CONV_CH), 1.0)
    state_dn = nrm(ks[3], (DEPTH, DEC_BATCH, DN_HEADS, DN_DK, DN_DV), 0.1)
    state_s5_re = nrm(ks[4], (DEPTH, DEC_BATCH, S5_GROUPS, S5_STATE), 0.1)
    state_s5_im = nrm(ks[5], (DEPTH, DEC_BATCH, S5_GROUPS, S5_STATE), 0.1)
    norm_w = 1.0 + nrm(ks[6], (DEPTH, D_MODEL), 0.02)
    w_in = nrm(ks[7], (DEPTH, D_MODEL, IN_COLS), D_MODEL ** -0.5)
    conv_w = nrm(ks[8], (DEPTH, CONV_K, CONV_CH), CONV_K ** -0.5)
    dn_A_log = jnp.log(jax.random.uniform(ks[9], (DEPTH, DN_HEADS), f32, 1.0, 16.0))
    dt0 = jnp.exp(jax.random.uniform(ks[10], (DEPTH, DN_HEADS), f32, math.log(1e-3), math.log(1e-1)))
    dn_dt_bias = dt0 + jnp.log(-jnp.expm1(-dt0))
    dn_norm_w = 1.0 + nrm(ks[11], (DEPTH, DN_DV), 0.02)
    s5_A_re = -0.5 + nrm(ks[12], (DEPTH, S5_GROUPS, S5_STATE), 0.01)
    s5_A_im = (math.pi * jnp.arange(S5_STATE, dtype=f32))[None, None, :] + nrm(ks[13], (DEPTH, S5_GROUPS, S5_STATE), 0.01)
    s5_log_dt = jax.random.uniform(ks[14], (DEPTH, S5_GROUPS), f32, math.log(1e-3), math.log(1e-1))
    s5_B_re = nrm(ks[15], (DEPTH, S5_GROUPS, S5_STATE, S5_GROUP), (2 * S5_GROUP) ** -0.5)
    s5_B_im = nrm(ks[16], (DEPTH, S5_GROUPS, S5_STATE, S5_GROUP), (2 * S5_GROUP) ** -0.5)
    s5_C_re = nrm(ks[17], (DEPTH, S5_GROUPS, S5_GROUP, S5_STATE), (2 * S5_STATE) ** -0.5)
    s5_C_im = nrm(ks[18], (DEPTH, S5_GROUPS, S5_GROUP, S5_STATE), (2 * S5_STATE) ** -0.5)
    s5_D = nrm(ks[19], (DEPTH, S5_WIDTH), 1.0)
    glu_w = nrm(ks[20], (DEPTH, S5_WIDTH, S5_WIDTH), S5_WIDTH ** -0.5)
    glu_b = nrm(ks[21], (DEPTH, S5_WIDTH), 0.01)
    w_out = nrm(ks[22], (DEPTH, MIX, D_MODEL), MIX ** -0.5)
    final_norm_w = 1.0 + nrm(ks[23], (D_MODEL,), 0.02)
    return {"x_prompt": x_prompt, "x_sample": x_sample, "cache_conv": cache_conv, "state_dn": state_dn,
            "state_s5_re": state_s5_re, "state_s5_im": state_s5_im, "norm_w": norm_w, "w_in": w_in,
            "conv_w": conv_w, "dn_A_log": dn_A_log, "dn_dt_bias": dn_dt_bias, "dn_norm_w": dn_norm_w,
            "s5_A_re": s5_A_re, "s5_A_im": s5_A_im, "s5_log_dt": s5_log_dt, "s5_B_re": s5_B_re,
            "s5_B_im": s5_B_im, "s5_C_re": s5_C_re, "s5_C_im": s5_C_im, "s5_D": s5_D, "glu_w": glu_w,
            "glu_b": glu_b, "w_out": w_out, "final_norm_w": final_norm_w}


def reference(x_prompt, x_sample, cache_conv, state_dn, state_s5_re, state_s5_im, norm_w, w_in, conv_w,
              dn_A_log, dn_dt_bias, dn_norm_w, s5_A_re, s5_A_im, s5_log_dt, s5_B_re, s5_B_im, s5_C_re,
              s5_C_im, s5_D, glu_w, glu_b, w_out, final_norm_w):
    bp = x_prompt.shape[0]
    yp, ys = x_prompt, x_sample
    conv_p, dn_p, re_p, im_p = [], [], [], []
    conv_s, dn_s, re_s, im_s = [], [], [], []
    for l in range(DEPTH):
        lw = (norm_w[l], w_in[l], conv_w[l], dn_A_log[l], dn_dt_bias[l], dn_norm_w[l], s5_A_re[l],
              s5_A_im[l], s5_log_dt[l], s5_B_re[l], s5_B_im[l], s5_C_re[l], s5_C_im[l], s5_D[l],
              glu_w[l], glu_b[l], w_out[l])
        yp, c1, d1, r1, i1 = hybrid_layer(
            yp, jnp.zeros((bp, CONV_K - 1, CONV_CH), jnp.float32),
            jnp.zeros((bp, DN_HEADS, DN_DK, DN_DV), jnp.float32),
            jnp.zeros((bp, S5_GROUPS, S5_STATE), jnp.float32),
            jnp.zeros((bp, S5_GROUPS, S5_STATE), jnp.float32), *lw)
        ys, c2, d2, r2, i2 = hybrid_layer(ys, cache_conv[l], state_dn[l], state_s5_re[l], state_s5_im[l], *lw)
        conv_p.append(c1); dn_p.append(d1); re_p.append(r1); im_p.append(i1)
        conv_s.append(c2); dn_s.append(d2); re_s.append(r2); im_s.append(i2)
    y_prompt = rms_norm(yp, final_norm_w).astype(x_prompt.dtype)
    y_sample = rms_norm(ys, final_norm_w).astype(x_sample.dtype)
    new_conv_prompt = jnp.stack(conv_p).astype(cache_conv.dtype)
    new_dn_prompt = jnp.stack(dn_p).astype(state_dn.dtype)
    new_s5_re_prompt = jnp.stack(re_p).astype(state_s5_re.dtype)
    new_s5_im_prompt = jnp.stack(im_p).astype(state_s5_im.dtype)
    new_conv_sample = jnp.stack(conv_s).astype(cache_conv.dtype)
    new_dn_sample = jnp.stack(dn_s).astype(state_dn.dtype)
    new_s5_re_sample = jnp.stack(re_s).astype(state_s5_re.dtype)
    new_s5_im_sample = jnp.stack(im_s).astype(state_s5_im.dtype)
    return (y_prompt, y_sample, new_conv_prompt, new_dn_prompt, new_s5_re_prompt, new_s5_im_prompt,
            new_conv_sample, new_dn_sample, new_s5_re_sample, new_s5_im_sample)
```

```python
import math
import numpy as np
from contextlib import ExitStack
import concourse.bass as bass
import concourse.mybir as mybir
from concourse.bass_utils import run_bass_kernel_spmd

F32 = mybir.dt.float32
BF16 = mybir.dt.bfloat16
I32 = mybir.dt.int32
ALU = mybir.AluOpType
AF = mybir.ActivationFunctionType

EPOCH = 30000
NDMASEM = 24
PI = math.pi


class Prog:
    ENGS = ["pe", "act", "dve", "pool", "sp"]

    def __init__(self, nc, es):
        self.nc = nc
        self.es = es
        self.ops = {e: [] for e in self.ENGS}
        self.cnt = {e: 0 for e in self.ENGS}
        self.sems = {e: [] for e in self.ENGS}
        self.dma_sems = [es.enter_context(nc.semaphore(f"dq{i}")) for i in range(NDMASEM)]
        self.dma_n = 0
        self.dma_tokens = []
        self.lastw = {}
        self.readers = {}
        self.waited = {e: {} for e in self.ENGS}

    def _sem(self, e, idx):
        ep = idx // EPOCH
        while len(self.sems[e]) <= ep:
            self.sems[e].append(self.es.enter_context(self.nc.semaphore(f"s_{e}_{len(self.sems[e])}")))
        return self.sems[e][ep], idx % EPOCH + 1

    def op(self, eng, fn, r=(), w=(), dma=False):
        deps = []
        for k in r:
            if k in self.lastw:
                deps.append(self.lastw[k])
        for k in w:
            if k in self.lastw:
                deps.append(self.lastw[k])
            deps.extend(self.readers.get(k, []))
        if dma:
            i = self.dma_n
            self.dma_n += 1
            sem = self.dma_sems[i % NDMASEM]
            val = 16 * (i // NDMASEM + 1)
            if i >= NDMASEM:
                deps.append(self.dma_tokens[i - NDMASEM])
            tok = (sem, val, "dma")
            self.dma_tokens.append(tok)
            inc = (sem, 16)
        else:
            idx = self.cnt[eng]
            self.cnt[eng] += 1
            sem, val = self._sem(eng, idx)
            tok = (sem, val, eng)
            inc = (sem, 1)
        waits = []
        wd = self.waited[eng]
        for (s, v, te) in deps:
            if te == "pe" and eng == "pe" and not dma:
                continue
            key = id(s)
            if wd.get(key, 0) >= v:
                continue
            wd[key] = v
            waits.append((s, v))
        self.ops[eng].append((fn, waits, inc))
        for k in r:
            self.readers.setdefault(k, []).append(tok)
        for k in w:
            self.lastw[k] = tok
            self.readers[k] = []
        return tok

    def emit(self):
        nc = self.nc
        final = []
        for t in self.dma_tokens[-NDMASEM:]:
            final.append((t[0], t[1]))
        for e in self.ENGS:
            if self.cnt[e] > 0:
                final.append(self._sem(e, self.cnt[e] - 1))
        ops = self.ops

        def run(engh, lst, extra=None):
            for fn, waits, inc in lst:
                for (s, v) in waits:
                    engh.wait_ge(s, v)
                ins = fn(engh)
                ins.then_inc(inc[0], inc[1])
            if extra:
                for (s, v) in extra:
                    engh.wait_ge(s, v)

        with nc.Block() as block:
            @block.tensor
            def _(e):
                run(e, ops["pe"])

            @block.scalar
            def _(e):
                run(e, ops["act"])

            @block.vector
            def _(e):
                run(e, ops["dve"])

            @block.gpsimd
            def _(e):
                run(e, ops["pool"])

            @block.sync
            def _(e):
                run(e, ops["sp"], final)


def make_consts():
    c = {}
    c["ident"] = np.eye(128, dtype=np.float32)
    c["lst"] = np.tril(np.ones((128, 128), np.float32), -1)
    c["ones"] = np.ones((128, 128), np.float32)
    for name, TT, C in (("p", 128, 64), ("s", 64, 32)):
        idx = np.arange(TT)
        same = (idx[:, None] // C) == (idx[None, :] // C)
        tri = same & (idx[:, None] <= idx[None, :])
        us = same & (idx[None, :] > idx[:, None])
        ui = same & (idx[None, :] >= idx[:, None])
        ls = same & (idx[:, None] > idx[None, :])
        f = lambda a: a.astype(np.float32).copy()
        c["tri_" + name] = f(tri)
        c["blk_" + name] = f(same)
        c["us_" + name] = f(us)
        c["ui_" + name] = f(ui)
        c["ls_" + name] = f(ls)
        c["id_" + name] = f(np.eye(TT))
        c["tw_" + name] = f(2 * np.eye(TT))
        nch = TT // C
        ind = np.zeros((TT, nch, 128), np.float32)
        for ch in range(nch):
            ind[ch * C:(ch + 1) * C, ch, :] = 1.0
        c["ind_" + name] = ind
    c["tv"] = np.tile(np.arange(1, 65, dtype=np.float32)[None, :], (128, 1))
    m64 = np.ones((128, 64), np.float32)
    m64[:, 0] = 0.0
    c["m64"] = m64
    sg = np.ones((128, 4), np.float32)
    sg[:64, 0] = -1.0
    sg[64:, 1] = -1.0
    sg[:, 2] = -1.0
    c["sgn"] = sg
    psw = np.zeros((128, 128), np.float32)
    for n in range(64):
        psw[64 + n, n] = -1.0
        psw[n, 64 + n] = 1.0
    c["psw"] = psw
    rm = np.zeros((128, 8), np.float32)
    for g in range(8):
        rm[g * 16:(g + 1) * 16, g] = 1.0
    c["rowmask"] = rm
    return c


CONSTS = make_consts()

WSHAPES = {
    "norm_w": [1024], "w_in": [1024, 3080], "conv_w": [4, 1536], "dn_A_log": [4], "dn_dt_bias": [4],
    "dn_norm_w": [128], "s5_A_re": [32, 64], "s5_A_im": [32, 64], "s5_log_dt": [32],
    "s5_B_re": [32, 64, 16], "s5_B_im": [32, 64, 16], "s5_C_re": [32, 16, 64], "s5_C_im": [32, 16, 64],
    "s5_D": [512], "glu_w": [512, 512], "glu_b": [512], "w_out": [1024, 1024], "final_norm_w": [1024],
}


def build(TP, NS=2, dbg=(), NPRE=0):
    nc = bass.Bass("TRN2", target_bir_lowering=False)
    din = lambda n, s: nc.dram_tensor(n, list(s), F32, kind="ExternalInput").ap()
    dout = lambda n, s: nc.dram_tensor(n, list(s), F32, kind="ExternalOutput").ap()
    xp = din("xp", [TP, 1024]); xs = din("xs", [NS * 32, 1024])
    xpre = din("xpre", [NPRE, 1024]) if NPRE else None
    cconv = din("cconv", [NS, 3, 1536]); sdn = din("sdn", [NS, 4, 128, 128])
    sre = din("sre", [NS, 32, 64]); sim = din("sim", [NS, 32, 64])
    W = {k: din(k, v) for k, v in WSHAPES.items()}
    CD = {k: din("c_" + k, v.shape) for k, v in CONSTS.items()}
    yp = dout("yp", [TP, 1024]); ys = dout("ys", [NS * 32, 1024])
    o_convp = dout("convp", [3, 1536]); o_dnp = dout("dnp", [4, 128, 128])
    o_rep = dout("rep", [32, 64]); o_imp = dout("imp", [32, 64])
    o_convs = dout("convs", [NS, 3, 1536]); o_dns = dout("dns", [NS, 4, 128, 128])
    o_res = dout("res", [NS, 32, 64]); o_ims = dout("ims", [NS, 32, 64])
    dbg_mix = nc.dram_tensor("dbg_mix", [128, 1024], BF16, kind="ExternalOutput").ap() if dbg else None

    with ExitStack() as es:
        P = Prog(nc, es)
        _sbn = [0]

        def sb(shape, dt=F32, name=None):
            _sbn[0] += 1
            nm = name or f"t{_sbn[0]}"
            t = es.enter_context(nc.sbuf_tensor(nm, list(shape), dt))
            return t, nm

        banks = []
        for i in range(6):
            banks.append((es.enter_context(nc.psum_tensor(f"pb{i}", [128, 512], F32)), f"pb{i}"))
        bbanks = []
        for i in range(2):
            bbanks.append((es.enter_context(nc.psum_tensor(f"pbb{i}", [128, 1024], BF16)), f"pbb{i}"))
        _bi = [0, 0]
        _stream = [None]

        y5bank = banks.pop()

        def bank():
            st_ = _stream[0]
            if st_ is not None:
                st_["bi"] = (st_["bi"] + 1) % len(st_["banks"])
                return st_["banks"][st_["bi"]]
            _bi[0] = (_bi[0] + 1) % len(banks)
            return banks[_bi[0]]

        def begin_stream(bank_ids, bb=None):
            bl_ = [y5bank if i == "y5" else banks[i] for i in bank_ids]
            _stream[0] = dict(ops=[], banks=bl_, bi=0, bb=bb)
            return _stream[0]

        def end_stream():
            _stream[0] = None

        def merge_streams(sts):
            lists = [st_["ops"] for st_ in sts]
            idx = [0] * len(lists)
            total = sum(len(l) for l in lists)
            for _ in range(total):
                best, bv = None, None
                for i, l in enumerate(lists):
                    if idx[i] < len(l):
                        frac = idx[i] / len(l)
                        if bv is None or frac < bv:
                            best, bv = i, frac
                a = lists[best][idx[best]]
                idx[best] += 1
                P.op(a[0], a[1], r=a[2], w=a[3], dma=a[4])

        def bbank():
            st_ = _stream[0]
            if st_ is not None and st_.get("bb") is not None:
                return bbanks[st_["bb"]]
            _bi[1] = (_bi[1] + 1) % len(bbanks)
            return bbanks[_bi[1]]

        def op(eng, fn, r=(), w=(), dma=False):
            if _stream[0] is not None:
                _stream[0]["ops"].append((eng, fn, tuple(r), tuple(w), dma))
                return None
            return P.op(eng, fn, r=r, w=w, dma=dma)

        def dma(out, in_, r=(), w=(), slow=False):
            if slow:
                return op("sp", lambda e: e.dma_start(out=out, in_=in_, allow_slow_non_contiguous=True), r=r, w=w, dma=True)
            return op("sp", lambda e: e.dma_start(out=out, in_=in_), r=r, w=w, dma=True)

        def mm(out, lhsT, rhs, start, stop, r, w):
            return op("pe", lambda e: e.matmul(out, lhsT=lhsT, rhs=rhs, start=start, stop=stop), r=r, w=w)

        def tr(out, in_, ident, r, w):
            return op("pe", lambda e: e.transpose(out=out, in_=in_, identity=ident), r=r, w=w)

        def act(out, in_, func, r, w, **kw):
            return op("act", lambda e: e.activation(out=out, in_=in_, func=func, **kw), r=r, w=w)

        def tt(eng, out, in0, in1, o, r, w):
            return op(eng, lambda e: e.tensor_tensor(out=out, in0=in0, in1=in1, op=o), r=r, w=w)

        def ts(eng, out, in0, s1, s2, o0, o1, r, w):
            if o1 is None:
                return op(eng, lambda e: e.tensor_scalar(out=out, in0=in0, scalar1=s1, scalar2=None, op0=o0), r=r, w=w)
            return op(eng, lambda e: e.tensor_scalar(out=out, in0=in0, scalar1=s1, scalar2=s2, op0=o0, op1=o1), r=r, w=w)

        def stt(eng, out, in0, sc, in1, o0, o1, r, w):
            return op(eng, lambda e: e.scalar_tensor_tensor(out=out, in0=in0, scalar=sc, in1=in1, op0=o0, op1=o1), r=r, w=w)

        def cp(eng, out, in_, r, w):
            if eng == "act":
                return op("act", lambda e: e.copy(out=out, in_=in_), r=r, w=w)
            return op(eng, lambda e: e.tensor_copy(out=out, in_=in_), r=r, w=w)

        TA, TAK = sb([128, 512], F32, "TA")
        TB, TBK = sb([128, 512], F32, "TB")
        TC, TCK = sb([128, 512], F32, "TC")
        TD, TDK = sb([128, 512], F32, "TD")
        TE, TEK = sb([128, 512], F32, "TE")
        TF, TFK = sb([128, 512], F32, "TF")
        xt, xtK = sb([128, 1024], F32, "xt")
        XN, XNK = sb([128, 1024], F32, "XN")
        v4 = lambda t: t[:].rearrange("p (h e) -> p h e", h=4)

        C = {}
        for k, v in CONSTS.items():
            if k in ("ones",):
                dma(XN[:, 0:128], CD[k], w=[XNK])
                C[k] = (XN[:, 0:128], XNK)
                continue
            t, nm = sb(v.shape, F32, "k_" + k)
            dma(t[:], CD[k], w=[nm])
            C[k] = (t, nm)
        ident, identK = C["ident"]
        identb, identbK = sb([128, 128], BF16, "identb")
        cp("dve", identb[:], ident[:], [identK], [identbK])
        onesb, onesbK = sb([128, 128], BF16, "onesb")
        cp("dve", onesb[:], C["ones"][0], [C["ones"][1]], [onesbK])
        sgn, sgnK = C["sgn"]

        nwc, nwcK = sb([128, 8], F32, "nwc")
        for k in range(8):
            dma(nwc[:, k:k + 1], W["norm_w"][k * 128:(k + 1) * 128].rearrange("(p o) -> p o", o=1), w=[nwcK])
        winb, winbK = sb([128, 8, 3080], BF16, "winb")
        n_ = 0
        for k in range(8):
            for pc in range(4):
                c0, c1 = pc * 770, (pc + 1) * 770
                dma(XN[:, 0:770], W["w_in"][k * 128:(k + 1) * 128, c0:c1], w=[XNK])
                ts("dve" if n_ % 2 == 0 else "pool", winb[:, k, c0:c1], XN[:, 0:770], nwc[:, k:k + 1], None, ALU.mult, None, [XNK, nwcK], [winbK])
                n_ += 1
        woutb, woutbK = sb([128, 8, 1024], BF16, "woutb")
        for k in range(8):
            dma(XN[:, :], W["w_out"][k * 128:(k + 1) * 128, :], w=[XNK])
            cp("dve" if k % 2 == 0 else "pool", woutb[:, k, :], XN[:, :], [XNK], [woutbK])
        glub, glubK = sb([128, 4, 512], BF16, "glub")
        for k in range(4):
            dma(XN[:, 0:512], W["glu_w"][k * 128:(k + 1) * 128, :], w=[XNK])
            cp("dve", glub[:, k, :], XN[:, 0:512], [XNK], [glubK])
        fnwb, fnwbK = sb([128, 1024], F32, "fnwb")
        dma(fnwb[:], W["final_norm_w"].partition_broadcast(128), w=[fnwbK])
        glbb, glbbK = sb([128, 512], F32, "glbb")
        dma(glbb[:], W["glu_b"].partition_broadcast(128), w=[glbbK])
        dnwb, dnwbK = sb([128, 128], F32, "dnwb")
        dma(dnwb[:], W["dn_norm_w"].partition_broadcast(128), w=[dnwbK])
        dtb, dtbK = sb([128, 4], F32, "dtb")
        dma(dtb[:], W["dn_dt_bias"].partition_broadcast(128), w=[dtbK])
        negA, negAK = sb([128, 4], F32, "negA")
        dma(negA[:], W["dn_A_log"].partition_broadcast(128), w=[negAK])
        act(negA[:], negA[:], AF.Exp, [negAK], [negAK])
        ts("dve", negA[:], negA[:], -1.0, None, ALU.mult, None, [negAK], [negAK])
        acc, accK = sb([128, 4, 128], F32, "acc")
        accKs = [accK + str(t) for t in range(4)]
        cw4, cw4K = None, None
        CW, CWK = sb([128, 12, 4], F32, "CW")
        for t3 in range(3):
            dma(XN[0:4, 0:512], W["conv_w"][:, t3 * 512:(t3 + 1) * 512], w=[XNK])
            for t4 in range(4):
                t = t3 * 4 + t4
                bk, bkK = bank()
                tr(bk[:, 0:4], XN[0:4, t4 * 128:(t4 + 1) * 128], ident[0:4, 0:4], [XNK, identK], [bkK])
                cp("dve", CW[:, t, :], bk[:, 0:4], [bkK], [CWK])

        PA, PAK = sb([128, 8, 128], F32, "PA")
        PB, PBK = sb([128, 8, 128], F32, "PB")
        KI, KIK = sb([128, 512], I32, "KI")
        COS, COSK = sb([128, 32, 64], F32, "COS")
        SIN, SINK = sb([128, 32, 64], F32, "SIN")

        def reduce_sin(out, arg, tmp, ki, keys, shift):
            (oK, aK, tK, kK) = keys
            ts("dve", tmp, arg, shift + 64 * PI, 1.0 / (2 * PI), ALU.add, ALU.mult, [aK], [tK])
            cp("dve", ki, tmp, [tK], [kK])
            cp("dve", tmp, ki, [kK], [tK])
            ts("dve", tmp, tmp, -2 * PI, shift + 64 * PI, ALU.mult, ALU.add, [tK], [tK])
            tt("dve", tmp, tmp, arg, ALU.add, [tK, aK], [tK])
            ts("dve", tmp, tmp, -3.1415925, 3.1415925, ALU.max, ALU.min, [tK], [tK])
            act(out, tmp, AF.Sin, [tK], [oK])

        ar2, ar2K = sb([32, 128], F32, "ar2")
        ai2, ai2K = sb([32, 128], F32, "ai2")
        for h in range(2):
            dma(ar2[:, h * 64:(h + 1) * 64], W["s5_A_re"], w=[ar2K])
            dma(ai2[:, h * 64:(h + 1) * 64], W["s5_A_im"], w=[ai2K])
        sm = {}
        for nm in ["lre", "lim", "dt", "ldr", "th", "rho", "c0", "s0", "t0", "t1", "lbr", "lbi", "den", "fre", "fim",
                   "FIMS", "FRES", "t2"]:
            sm[nm] = sb([128, 32], F32, "s5_" + nm)
        bk, bkK = bank()
        tr(bk[:, 0:32], ar2[0:32, :], ident[0:32, 0:32], [ar2K, identK], [bkK])
        ts("dve", sm["lre"][0][:], bk[:, 0:32], -1e-4, None, ALU.min, None, [bkK], [sm["lre"][1]])
        bk, bkK = bank()
        tr(bk[:, 0:32], ai2[0:32, :], ident[0:32, 0:32], [ai2K, identK], [bkK])
        cp("dve", sm["lim"][0][:], bk[:, 0:32], [bkK], [sm["lim"][1]])
        dma(sm["dt"][0][:], W["s5_log_dt"].partition_broadcast(128), w=[sm["dt"][1]])
        act(sm["dt"][0][:], sm["dt"][0][:], AF.Exp, [sm["dt"][1]], [sm["dt"][1]])
        S = lambda n: sm[n][0][:]
        K_ = lambda n: sm[n][1]
        tt("dve", S("ldr"), S("lre"), S("dt"), ALU.mult, [K_("lre"), K_("dt")], [K_("ldr")])
        tt("dve", S("th"), S("lim"), S("dt"), ALU.mult, [K_("lim"), K_("dt")], [K_("th")])
        act(S("rho"), S("ldr"), AF.Exp, [K_("ldr")], [K_("rho")])
        reduce_sin(S("s0"), S("th"), S("t0"), KI[:, 0:32], (K_("s0"), K_("th"), K_("t0"), KIK), 0.0)
        reduce_sin(S("c0"), S("th"), S("t0"), KI[:, 0:32], (K_("c0"), K_("th"), K_("t0"), KIK), PI / 2)
        tt("dve", S("lbr"), S("rho"), S("c0"), ALU.mult, [K_("rho"), K_("c0")], [K_("lbr")])
        tt("dve", S("lbi"), S("rho"), S("s0"), ALU.mult, [K_("rho"), K_("s0")], [K_("lbi")])
        tt("dve", S("den"), S("lre"), S("lre"), ALU.mult, [K_("lre")], [K_("den")])
        tt("dve", S("t0"), S("lim"), S("lim"), ALU.mult, [K_("lim")], [K_("t0")])
        tt("dve", S("den"), S("den"), S("t0"), ALU.add, [K_("den"), K_("t0")], [K_("den")])
        op("dve", lambda e: e.reciprocal(out=S("den"), in_=S("den")), r=[K_("den")], w=[K_("den")])
        ts("dve", S("t1"), S("lbr"), -1.0, None, ALU.add, None, [K_("lbr")], [K_("t1")])
        tt("dve", S("t0"), S("t1"), S("lre"), ALU.mult, [K_("t1"), K_("lre")], [K_("t0")])
        tt("dve", S("t2"), S("lbi"), S("lim"), ALU.mult, [K_("lbi"), K_("lim")], [K_("t2")])
        tt("dve", S("t0"), S("t0"), S("t2"), ALU.add, [K_("t0"), K_("t2")], [K_("t0")])
        tt("dve", S("fre"), S("t0"), S("den"), ALU.mult, [K_("t0"), K_("den")], [K_("fre")])
        tt("dve", S("t0"), S("lbi"), S("lre"), ALU.mult, [K_("lbi"), K_("lre")], [K_("t0")])
        tt("dve", S("t2"), S("t1"), S("lim"), ALU.mult, [K_("t1"), K_("lim")], [K_("t2")])
        tt("dve", S("t0"), S("t0"), S("t2"), ALU.subtract, [K_("t0"), K_("t2")], [K_("t0")])
        tt("dve", S("fim"), S("t0"), S("den"), ALU.mult, [K_("t0"), K_("den")], [K_("fim")])
        ts("dve", S("FIMS"), S("fim"), sgn[:, 0:1], None, ALU.mult, None, [K_("fim"), sgnK], [K_("FIMS")])
        ts("dve", S("FRES"), S("fre"), sgn[:, 1:2], None, ALU.mult, None, [K_("fre"), sgnK], [K_("FRES")])
        g16 = lambda t: t[:].rearrange("p (g c) -> p g c", g=32)
        Bst, BstK, Bsw, BswK, bb, bbK, bbs, bbsK, btmp, btmpK = g16(TA), TAK, g16(TB), TBK, g16(TC), TCK, g16(TD), TDK, g16(TE), TEK
        bre = W["s5_B_re"].rearrange("g n c -> n g c")
        bim = W["s5_B_im"].rearrange("g n c -> n g c")
        dma(Bst[0:64], bre, w=[BstK]); dma(Bst[64:128], bim, w=[BstK])
        dma(Bsw[0:64], bim, w=[BswK]); dma(Bsw[64:128], bre, w=[BswK])
        bc = lambda nm: sm[nm][0][:].unsqueeze(2).to_broadcast([128, 32, 16])
        tt("dve", bb, Bst, bc("fre"), ALU.mult, [BstK, K_("fre")], [bbK])
        tt("dve", btmp, Bsw, bc("FIMS"), ALU.mult, [BswK, K_("FIMS")], [btmpK])
        tt("dve", bb, bb, btmp, ALU.add, [bbK, btmpK], [bbK])
        tt("dve", bbs, Bsw, bc("FRES"), ALU.mult, [BswK, K_("FRES")], [bbsK])
        tt("dve", btmp, Bst, bc("fim"), ALU.mult, [BstK, K_("fim")], [btmpK])
        tt("dve", bbs, bbs, btmp, ALU.add, [bbsK, btmpK], [bbsK])
        BBn, BBnK = sb([128, 4, 4, 128], BF16, "BBn")
        BBs, BBsK = sb([128, 4, 4, 128], BF16, "BBs")
        rmask, rmaskK = C["rowmask"]
        for (src, srcK, dst, dstK) in ((bb, bbK, BBn, BBnK), (bbs, bbsK, BBs, BBsK)):
            for q in range(4):
                bk, bkK = bank()
                tr(bk[:, 0:128], src[:, 8 * q:8 * q + 8, :].rearrange("p g c -> p (g c)"), ident[:, :], [srcK, identK], [bkK])
                for g in range(8):
                    hf, j = g // 4, g % 4
                    ts("dve", dst[64 * hf:64 * hf + 64, q, j, :], bk[64 * hf:64 * hf + 64, 0:128], rmask[64 * hf:64 * hf + 64, g:g + 1], None, ALU.mult, None, [bkK, rmaskK], [dstK])
        tv, tvK = C["tv"]
        PAf = PA[:].rearrange("p g t -> p (g t)")
        PBf = PB[:].rearrange("p g t -> p (g t)")
        for q4 in range(4):
            argv = PAf[:, 0:512].rearrange("p (g t) -> p g t", g=8)
            tt("dve", argv, sm["th"][0][:, 8 * q4:8 * q4 + 8].unsqueeze(2).to_broadcast([128, 8, 64]),
               tv[:].unsqueeze(1).to_broadcast([128, 8, 64]), ALU.mult, [K_("th"), tvK], [PAK])
            reduce_sin(SIN[:, 8 * q4:8 * q4 + 8, :].rearrange("p g t -> p (g t)"), PAf[:, 0:512], PBf[:, 0:512], KI[:, :], (SINK, PAK, PBK, KIK), 0.0)
            reduce_sin(COS[:, 8 * q4:8 * q4 + 8, :].rearrange("p g t -> p (g t)"), PAf[:, 0:512], PBf[:, 0:512], KI[:, :], (COSK, PAK, PBK, KIK), PI / 2)
        Md1, Md1K = sb([128, 512], BF16, "Md1")
        Md2, Md2K = sb([128, 512], BF16, "Md2")
        cri, criK = xt[:].rearrange("p (q n) -> p q n", q=8), xtK
        cre = W["s5_C_re"].rearrange("g c n -> (g c) n")
        cim = W["s5_C_im"].rearrange("g c n -> (g c) n")
        for q in range(4):
            dma(cri[:, q, 0:64], cre[q * 128:(q + 1) * 128, :], w=[criK])
            dma(cri[:, q, 64:128], cim[q * 128:(q + 1) * 128, :], w=[criK])
            dma(cri[:, 4 + q, 0:64], cim[q * 128:(q + 1) * 128, :], w=[criK])
            dma(cri[:, 4 + q, 64:128], cre[q * 128:(q + 1) * 128, :], w=[criK])
        for q in range(4):
            bk, bkK = bank()
            tr(bk[:, 0:128], cri[:, q, :], ident[:, :], [criK, identK], [bkK])
            ts("dve", Md1[:, q * 128:(q + 1) * 128], bk[:, 0:128], sgn[:, 1:2], None, ALU.mult, None, [bkK, sgnK], [Md1K])
            bk, bkK = bank()
            tr(bk[:, 0:128], cri[:, 4 + q, :], ident[:, :], [criK, identK], [bkK])
            ts("dve", Md2[:, q * 128:(q + 1) * 128], bk[:, 0:128], -1.0, None, ALU.mult, None, [bkK], [Md2K])
        dcol, dcolK = sb([128, 4], F32, "dcol")
        for q in range(4):
            dma(dcol[:, q:q + 1], W["s5_D"][q * 128:(q + 1) * 128].rearrange("(p o) -> p o", o=1), w=[dcolK])
        diagD, diagDK = sb([128, 4, 128], BF16, "diagD")
        for q in range(4):
            ts("dve", diagD[:, q, :], ident[:, :], dcol[:, q:q + 1], None, ALU.mult, None, [identK, dcolK], [diagDK])
        psw, pswK = C["psw"]
        rho, rhoK = sm["rho"]
        RC, RCK = sb([128, 32], F32, "RC"); RS, RSK = sb([128, 32], F32, "RS")
        tt("dve", RC[:], COS[:, :, 63], rho[:], ALU.mult, [COSK, rhoK], [RCK])
        tt("dve", RS[:], SIN[:, :, 63], rho[:], ALU.mult, [SINK, rhoK], [RSK])
        RHOT, RHOTK = sb([128, 32, 64], F32, "RHOT")
        tt("dve", RHOT[:], rho[:].unsqueeze(2).to_broadcast([128, 32, 64]), C["m64"][0][:].unsqueeze(1).to_broadcast([128, 32, 64]), ALU.mult, [rhoK, C["m64"][1]], [RHOTK])

        NST = 1 + NS
        Sst = [sb([128, 4, 128], F32, f"Sst{i}") for i in range(NST)]
        Sbf = [sb([128, 4, 128], BF16, f"Sbf{i}") for i in range(NST)]
        XS = [sb([128, 32], F32, f"XS{i}") for i in range(NST)]
        op("pool", lambda e: e.memset(Sst[0][0][:], 0.0), w=[Sst[0][1]])
        op("pool", lambda e: e.memset(Sbf[0][0][:], 0.0), w=[Sbf[0][1]])
        XS0K = [f"XS0_{e}" for e in range(8)]
        op("pool", lambda e: e.memset(XS[0][0][:], 0.0), w=XS0K)
        xs2, xs2K = sb([32, 128], F32, "xs2")
        for s in range(NS):
            dma(Sst[1 + s][0][:], sdn[s].rearrange("h d e -> d h e"), w=[Sst[1 + s][1]])
            cp("pool", Sbf[1 + s][0][:], Sst[1 + s][0][:], [Sst[1 + s][1]], [Sbf[1 + s][1]])
            dma(xs2[:, 0:64], sre[s], w=[xs2K]); dma(xs2[:, 64:128], sim[s], w=[xs2K])
            bk, bkK = bank()
            tr(bk[:, 0:32], xs2[0:32, :], ident[0:32, 0:32], [xs2K, identK], [bkK])
            cp("dve", XS[1 + s][0][:], bk[:, 0:32], [bkK], [XS[1 + s][1]])

        sc = {}
        for nm in ["ss", "rstd", "ss2", "r2"]:
            sc[nm] = sb([128, 1], F32, "sc_" + nm)
        hb, hbK = sb([128, 1024], BF16, "hb")
        MIX, MIXK = hb, hbK
        hT, hTK = sb([128, 8, 128], BF16, "hT")
        MIXT, MIXTK = hT, hTK
        RAWp, RAWpK = sb([128, 12, 1, 131], F32, "RAWp")
        RAWs, RAWsK = sb([128, 12, NS, 35], F32, "RAWs")
        op("pool", lambda e: e.memset(RAWp[:], 0.0), w=[RAWpK])
        for s in range(NS):
            for t in range(12):
                dma(RAWs[:, t, s, 0:3], cconv[s, :, t * 128:(t + 1) * 128].rearrange("r c -> c r"), w=[RAWsK], slow=True)
        QKVs = [sb([128, 12, 128], BF16, f"QKV{i}") for i in range(2)]
        uTs = [sb([128, 4, 128], BF16, f"uT{i}") for i in range(2)]
        GT8s = [sb([128, 8], F32, f"GT8_{i}") for i in range(2)]
        SZD, SZDK = sb([128, 512], BF16, "SZD")
        SZS, SZSK = sb([128, 512], BF16, "SZS")
        g4 = {}
        for nm in ["GT8", "BETA", "LNB", "SQB", "RSB", "XA", "G", "GC", "GD", "EGC", "EDEC", "SSK", "RK", "S1", "S2", "S3", "SSO", "RO"]:
            g4[nm] = sb([128, 8 if nm == "GT8" else 4], F32, "g_" + nm)
        EGL, EGLK = sb([128, 2, 4], F32, "EGL")
        Rg, RgK = v4(TA), TAK
        RQ, RQK = v4(TA), TAK
        OT, OTK = v4(TA), TAK
        EU, EUK = v4(TB), TBK
        X32, X32K = v4(TB), TBK
        Y5, Y5K = TB, TBK
        EL, ELK = v4(TC), TCK
        Z32, Z32K = v4(TC), TCK
        G1, G1K = TC, TCK
        EUI, EUIK = v4(TD), TDK
        OO, OOK = v4(TD), TDK
        Mt, MtK = v4(TE), TEK
        G2, G2K = TE, TEK
        VT, VTK = v4(TF), TFK
        GY, GYK = TF, TFK
        QHT, QHTK = sb([128, 4, 128], BF16, "QHT")
        KTk, KTkK = sb([128, 4, 128], BF16, "KTk")
        QSQ, QSQK = KTk, KTkK
        KG, KGK = sb([128, 4, 128], BF16, "KG"); KDEC, KDECK = sb([128, 4, 128], BF16, "KDEC")
        KTT, KTTK = sb([128, 4, 128], BF16, "KTT"); KGT, KGTK = sb([128, 4, 128], BF16, "KGT")
        Xb, XbK = sb([128, 4, 128], BF16, "Xb"); Mtb, MtbK = sb([128, 4, 128], BF16, "Mtb")
        Zb, ZbK = sb([128, 4, 128], BF16, "Zb")
        XF, XFK = sb([128, 4, 128], BF16, "XF")
        ATT, ATTK = sb([128, 4, 128], BF16, "ATT")
        Rr, RrK = sb([128, 4, 128], BF16, "Rr"); VN, VNK = sb([128, 4, 128], BF16, "VN")
        GYb, GYbK = Rr[:].rearrange("p h e -> p (h e)"), RrK
        GYT, GYTK = VN, VNK
        Ab, AbK = sb([128, 8, 128], BF16, "Ab"); Bb, BbK = sb([128, 8, 128], BF16, "Bb")
        AL, ALK = sb([128, 8], F32, "AL"); BL, BLK = sb([128, 8], F32, "BL")
        AL2, _ = sb([128, 4, 4], F32, "AL2"); BL2, _ = sb([128, 4, 4], F32, "BL2"); FX2, _ = sb([128, 4, 4], F32, "FX2")
        xso, xsoK = XN[0:32, 0:128], XNK
        def rsqrt_chain(outk, ink, scale, eps):
            o, oK = outk
            i_, iK = ink
            ts("dve", o, i_, scale, eps, ALU.mult, ALU.add, [iK], [oK])
            act(o, o, AF.Ln, [oK], [oK])
            act(o, o, AF.Exp, [oK], [oK], scale=-0.5)

        def do_tile(kind, ti, xsrc, ydst, mode="full", b=0, part="all"):
            full = mode == "full"
            needq = mode != "state"
            QKV, QKVK = QKVs[b]
            uT, uTK = uTs[b]
            if kind == "p":
                TT, Cc, nseg, L, RAW, RAWK = 128, 64, 1, 128, RAWp, RAWpK
                chunks = [(0, 0), (1, 0)]
                segs = [(0, 0), (1, 0)]
                SL = 64
            else:
                TT, Cc, nseg, L, RAW, RAWK = NS * 32, 32, NS, 32, RAWs, RAWsK
                chunks = [(s, 1 + s) for s in range(NS)]
                segs = [(s, 1 + s) for s in range(NS)]
                SL = 32
            nsg = TT // SL
            cn = lambda n: C[n + "_" + kind]
            m4 = lambda n: cn(n)[0][:TT, :TT].unsqueeze(1).to_broadcast([TT, 4, TT])
            x, xK = xt, xtK
            if part in ("all", "front"):
                x, xK = xt, xtK
                dma(x[:TT, :], xsrc, w=[xK])
                ss, ssK = sc["ss"]; rstd, rstdK = sc["rstd"]
                act(hb[:TT, :], x[:TT, :], AF.Square, [xK], [hbK, ssK], accum_out=ss[:TT, :])
                rsqrt_chain((rstd[:TT, :], rstdK), (ss[:TT, :], ssK), 1.0 / 1024, 1e-6)
                ts("dve", hb[:TT, :], x[:TT, :], rstd[:TT, 0:1], None, ALU.mult, None, [xK, rstdK], [hbK])
                pb, pbK = bbank()
                pbv = pb[:].rearrange("p (k t) -> p k t", k=8)
                for k in range(8):
                    tr(pbv[:, k, :TT], hb[:TT, k * 128:(k + 1) * 128], identb[:TT, :TT], [hbK, identbK], [pbK])
                cp("act", hT[:, :, :TT], pbv[:, :, :TT], [pbK], [hTK])
                for grp in range(4):
                    if grp == 0 and not needq:
                        continue
                    bk, bkK = bank()
                    bv = bk[:].rearrange("p (j t) -> p j t", j=4)
                    c0 = grp * 512 if grp < 3 else 2056
                    for j in range(4):
                        for k in range(8):
                            mm(bv[:, j, :TT], winb[:, k, c0 + j * 128:c0 + (j + 1) * 128], hT[:, k, :TT], k == 0, k == 7, [winbK, hTK], [bkK])
                    if grp < 3:
                        for s in range(nseg):
                            cp("act", RAW[:, grp * 4:(grp + 1) * 4, s, 3:3 + L], bv[:, :, s * L:(s + 1) * L], [bkK], [RAWK])
                    else:
                        cp("act", uT[:, :, :TT], bv[:, :, :TT], [bkK], [uTK])
                if full:
                    bk, bkK = bank()
                    for k in range(8):
                        mm(bk[:TT, :], hT[:, k, :TT], winb[:, k, 1544:2056], k == 0, k == 7, [winbK, hTK], [bkK])
                    act(SZD[:TT, :], bk[:TT, :], AF.Silu, [bkK], [SZDK])
                    bk, bkK = bank()
                    for k in range(8):
                        mm(bk[:TT, :], hT[:, k, :TT], winb[:, k, 2568:3080], k == 0, k == 7, [winbK, hTK], [bkK])
                    act(SZS[:TT, :], bk[:TT, :], AF.Silu, [bkK], [SZSK])
                bk, bkK = bank()
                for k in range(8):
                    mm(bk[:TT, 0:8], hT[:, k, :TT], winb[:, k, 1536:1544], k == 0, k == 7, [winbK, hTK], [bkK])
                GT8, GT8K = GT8s[b]
                cp("dve", GT8[:TT, :], bk[:TT, 0:8], [bkK], [GT8K])
                for t3 in range(3):
                    if t3 == 0 and not full:
                        continue
                    for t4 in range(4):
                        t = t3 * 4 + t4
                        av = acc[:, t4, :TT].rearrange("p (s l) -> p s l", s=nseg)
                        ts("dve", av, RAW[:, t, :, 0:L], CW[:, t, 0:1], None, ALU.mult, None, [RAWK, CWK], [accKs[t4]])
                        for j in range(1, 4):
                            stt("dve", av, RAW[:, t, :, j:j + L], CW[:, t, j:j + 1], av, ALU.mult, ALU.add, [RAWK, CWK, accKs[t4]], [accKs[t4]])
                    act(QKV[:, t3 * 4:(t3 + 1) * 4, :TT], acc[:, :, :TT], AF.Silu, accKs, [QKVK])
                if kind == "p":
                    cp("pool", RAW[:, :, 0, 0:3], RAW[:, :, 0, L:L + 3], [RAWK], [RAWK])
            if part == "front":
                return None
            GT8, GT8K = GT8s[b]
            stG = begin_stream([0, 1, 2], bb=1)
            G_ = lambda n: g4[n][0][:TT, :]
            GK = lambda n: g4[n][1]
            act(G_("BETA"), GT8[:TT, 4:8], AF.Exp, [GT8K], [GK("BETA")], scale=-1.0)
            act(G_("LNB"), G_("BETA"), AF.Ln, [GK("BETA")], [GK("LNB")], bias=1.0)
            act(G_("SQB"), G_("LNB"), AF.Exp, [GK("LNB")], [GK("SQB")], scale=-0.5)
            act(G_("RSB"), G_("LNB"), AF.Exp, [GK("LNB")], [GK("RSB")], scale=0.5)
            tt("dve", G_("XA"), GT8[:TT, 0:4], dtb[:TT, :], ALU.add, [GT8K, dtbK], [GK("XA")])
            act(G_("XA"), G_("XA"), AF.Exp, [GK("XA")], [GK("XA")])
            act(G_("XA"), G_("XA"), AF.Ln, [GK("XA")], [GK("XA")], bias=1.0)
            tt("dve", G_("G"), G_("XA"), negA[:TT, :], ALU.mult, [GK("XA"), negAK], [GK("G")])
            bk, bkK = bank()
            mm(bk[:TT, 0:4], cn("tri")[0][:TT, :TT], G_("G"), True, True, [cn("tri")[1], GK("G")], [bkK])
            mm(bk[:TT, 4:8], cn("blk")[0][:TT, :TT], G_("G"), True, True, [cn("blk")[1], GK("G")], [bkK])
            nch = len(chunks)
            for c in range(nch):
                mm(bk[:, 8 + 4 * c:12 + 4 * c], cn("ind")[0][:TT, c, :], G_("G"), True, True, [cn("ind")[1], GK("G")], [bkK])
            cp("dve", G_("GC"), bk[:TT, 0:4], [bkK], [GK("GC")])
            tt("dve", G_("GD"), bk[:TT, 4:8], G_("GC"), ALU.subtract, [bkK, GK("GC")], [GK("GD")])
            act(G_("EGC"), G_("GC"), AF.Exp, [GK("GC")], [GK("EGC")])
            act(G_("EDEC"), G_("GD"), AF.Exp, [GK("GD")], [GK("EDEC")])
            act(EGL[:, 0:nch, :], bk[:, 8:8 + 4 * nch].rearrange("p (c h) -> p c h", c=nch), AF.Exp, [bkK], [EGLK])
            tt("dve", Rg[:TT, :, :TT], m4("tri"), G_("G").unsqueeze(2).to_broadcast([TT, 4, TT]), ALU.mult, [cn("tri")[1], GK("G")], [RgK])
            bu, buK = bank()
            bl, blK = bank()
            buv = bu[:].rearrange("p (h t) -> p h t", h=4)
            blv = bl[:].rearrange("p (h t) -> p h t", h=4)
            lst, lstK = C["lst"]
            for h in range(4):
                mm(buv[:TT, h, :TT], lst[:TT, :TT], Rg[:TT, h, :TT], True, True, [lstK, RgK], [buK])
                mm(blv[:TT, h, :TT], Rg[:TT, h, :TT], lst[:TT, :TT], True, True, [lstK, RgK], [blK])
            act(EU[:TT, :, :TT], buv[:TT, :, :TT], AF.Exp, [buK], [EUK])
            act(EL[:TT, :, :TT], blv[:TT, :, :TT], AF.Exp, [blK], [ELK])
            if full:
                tt("pool", EUI[:TT, :, :TT], EU[:TT, :, :TT], m4("ui"), ALU.mult, [EUK, cn("ui")[1]], [EUIK])
            tt("pool", EU[:TT, :, :TT], EU[:TT, :, :TT], m4("us"), ALU.mult, [EUK, cn("us")[1]], [EUK])
            tt("pool", EL[:TT, :, :TT], EL[:TT, :, :TT], m4("ls"), ALU.mult, [ELK, cn("ls")[1]], [ELK])
            if full:
                act(QSQ[:, :, :TT], QKV[:, 0:4, :TT], AF.Square, [QKVK], [QSQK])
                bk, bkK = bank()
                bv = bk[:].rearrange("p (h t) -> p h t", h=4)
                for h in range(4):
                    mm(bv[:, h, :TT], onesb[:, :], QSQ[:, h, :TT], True, True, [onesbK, QSQK], [bkK])
                ts("dve", RQ[:, :, :TT], bv[:, :, :TT], 1e-6, None, ALU.add, None, [bkK], [RQK])
                act(RQ[:, :, :TT], RQ[:, :, :TT], AF.Ln, [RQK], [RQK])
                act(RQ[:, :, :TT], RQ[:, :, :TT], AF.Exp, [RQK], [RQK], scale=-0.5)
                stt("dve", QHT[:, :, :TT], QKV[:, 0:4, :TT], 128.0 ** -0.5, RQ[:, :, :TT], ALU.mult, ALU.mult, [QKVK, RQK], [QHTK])
            pk, pkK = bbank()
            pkv = pk[:].rearrange("p (h d) -> p h d", h=8)
            for h in range(4):
                tr(pkv[:TT, h, :], QKV[:, 4 + h, :TT], identb[:, :], [QKVK, identbK], [pkK])
                tr(pkv[:TT, 4 + h, :], QKV[:, 8 + h, :TT], identb[:, :], [QKVK, identbK], [pkK])
            SSK, SSKK = g4["SSK"]
            for h in range(4):
                act(Rr[:TT, 0, :], pkv[:TT, h, :], AF.Square, [pkK], [RrK, SSKK], accum_out=SSK[:TT, h:h + 1])
            rsqrt_chain((G_("RK"), GK("RK")), (G_("SSK"), SSKK), 1.0, 1e-6)
            tt("dve", G_("S1"), G_("RK"), G_("SQB"), ALU.mult, [GK("RK"), GK("SQB")], [GK("S1")])
            tt("dve", G_("S2"), G_("S1"), G_("EGC"), ALU.mult, [GK("S1"), GK("EGC")], [GK("S2")])
            tt("dve", G_("S3"), G_("RK"), G_("EDEC"), ALU.mult, [GK("RK"), GK("EDEC")], [GK("S3")])
            b4 = lambda n: g4[n][0][:TT, :].unsqueeze(2).to_broadcast([TT, 4, 128])
            tt("dve", KTk[:TT], pkv[:TT, 0:4, :], b4("S1"), ALU.mult, [pkK, GK("S1")], [KTkK])
            tt("dve", KG[:TT], pkv[:TT, 0:4, :], b4("S2"), ALU.mult, [pkK, GK("S2")], [KGK])
            tt("dve", KDEC[:TT], pkv[:TT, 0:4, :], b4("S3"), ALU.mult, [pkK, GK("S3")], [KDECK])
            tt("dve", VT[:TT], pkv[:TT, 4:8, :], b4("SQB"), ALU.mult, [pkK, GK("SQB")], [VTK])
            pk2, pk2K = bbank()
            pk2v = pk2[:].rearrange("p (h t) -> p h t", h=8)
            for h in range(4):
                tr(pk2v[:, h, :TT], KTk[:TT, h, :], identb[:TT, :TT], [KTkK, identbK], [pk2K])
                tr(pk2v[:, 4 + h, :TT], KG[:TT, h, :], identb[:TT, :TT], [KGK, identbK], [pk2K])
            cp("act", KTT[:, :, :TT], pk2v[:, 0:4, :TT], [pk2K], [KTTK])
            cp("act", KGT[:, :, :TT], pk2v[:, 4:8, :TT], [pk2K], [KGTK])
            bkk, bkkK = bank()
            bqk, bqkK = bank()
            kkv = bkk[:].rearrange("p (h t) -> p h t", h=4)
            qkv_ = bqk[:].rearrange("p (h t) -> p h t", h=4)
            for h in range(4):
                mm(kkv[:TT, h, :TT], KTT[:, h, :TT], KTT[:, h, :TT], True, True, [KTTK], [bkkK])
                if full:
                    mm(qkv_[:TT, h, :TT], KTT[:, h, :TT], QHT[:, h, :TT], True, True, [KTTK, QHTK], [bqkK])
            tt("dve", Mt[:TT, :, :TT], kkv[:TT, :, :TT], EL[:TT, :, :TT], ALU.mult, [bkkK, ELK], [MtK])
            tt("pool", Mt[:TT, :, :TT], Mt[:TT, :, :TT], m4("id"), ALU.add, [MtK, cn("id")[1]], [MtK])
            cp("pool", Mtb[:TT, :, :TT], Mt[:TT, :, :TT], [MtK], [MtbK])
            tt("dve", EU[:TT, :, :TT], kkv[:TT, :, :TT], EU[:TT, :, :TT], ALU.mult, [bkkK, EUK], [EUK])
            tt("pool", Xb[:TT, :, :TT], m4("id"), EU[:TT, :, :TT], ALU.subtract, [cn("id")[1], EUK], [XbK])
            RSB, RSBK = g4["RSB"]
            for h in range(4 if full else 0):
                stt("dve", ATT[:TT, h, :TT], qkv_[:TT, h, :TT], RSB[:TT, h:h + 1], EUI[:TT, h, :TT], ALU.mult, ALU.mult, [bqkK, RSBK, EUIK], [ATTK])
            for it in range(5):
                last = it == 4
                Xo, XoK, Mo, MoK = (X32, X32K, Mt, MtK) if last else (Xb, XbK, Mtb, MtbK)
                Zo, ZoK = (Z32, Z32K) if last else (Zb, ZbK)
                by, byK = bank()
                byv = by[:].rearrange("p (h t) -> p h t", h=4)
                for h in range(4):
                    mm(byv[:TT, h, :TT], Xo[:TT, h, :TT], Mo[:TT, h, :TT], True, True, [XoK, MoK], [byK])
                stt("dve", Zo[:TT, :, :TT], byv[:TT, :, :TT], -1.0, cn("tw")[0][:TT, :TT].unsqueeze(1).to_broadcast([TT, 4, TT]), ALU.mult, ALU.add, [byK, cn("tw")[1]], [ZoK])
                bx, bxK = bank()
                bxv = bx[:].rearrange("p (h t) -> p h t", h=4)
                for h in range(4):
                    mm(bxv[:TT, h, :TT], Zo[:TT, h, :TT], Xo[:TT, h, :TT], True, True, [ZoK, XoK], [bxK])
                if it < 3:
                    cp("act", Xb[:TT, :, :TT], bxv[:TT, :, :TT], [bxK], [XbK])
                elif it == 3:
                    cp("act", X32[:TT, :, :TT], bxv[:TT, :, :TT], [bxK], [X32K])
                else:
                    cp("act", XF[:TT, :, :TT], bxv[:TT, :, :TT], [bxK], [XFK])
            for (c, st) in chunks:
                r0, r1 = c * Cc, (c + 1) * Cc
                Sm, SmK = Sst[st]
                Sb_, SbK = Sbf[st]
                ba, baK = bank(); bq, bqK = bank()
                bav = ba[:].rearrange("p (h e) -> p h e", h=4)
                bqv = bq[:].rearrange("p (h e) -> p h e", h=4)
                for h in range(4):
                    mm(bav[:TT, h, :], KGT[:, h, :TT], Sb_[:, h, :], True, True, [KGTK, SbK], [baK])
                    if full:
                        mm(bqv[:TT, h, :], QHT[:, h, :TT], Sb_[:, h, :], True, True, [QHTK, SbK], [bqK])
                tt("dve", Rr[r0:r1], VT[r0:r1], bav[r0:r1], ALU.subtract, [VTK, baK], [RrK])
                if full:
                    tt("dve", OO[r0:r1], bqv[r0:r1], g4["EGC"][0][r0:r1, :].unsqueeze(2).to_broadcast([Cc, 4, 128]), ALU.mult, [bqK, GK("EGC")], [OOK])
                bc_, bcK = bank()
                bcv = bc_[:].rearrange("p (h e) -> p h e", h=4)
                for h in range(4):
                    mm(bcv[:TT, h, :], XF[r0:r1, h, :TT], Rr[r0:r1, h, :], True, True, [XFK, RrK], [bcK])
                tt("dve", VN[r0:r1], bcv[r0:r1], g4["SQB"][0][r0:r1, :].unsqueeze(2).to_broadcast([Cc, 4, 128]), ALU.mult, [bcK, GK("SQB")], [VNK])
                bd, bdK = bank(); be, beK = bank()
                bdv = bd[:].rearrange("p (h e) -> p h e", h=4)
                bev = be[:].rearrange("p (h e) -> p h e", h=4)
                for h in range(4):
                    if full:
                        mm(bdv[:TT, h, :], ATT[r0:r1, h, :TT], VN[r0:r1, h, :], True, True, [ATTK, VNK], [bdK])
                    mm(bev[:, h, :], KDEC[r0:r1, h, :], VN[r0:r1, h, :], True, True, [KDECK, VNK], [beK])
                if full:
                    tt("dve", OO[r0:r1], OO[r0:r1], bdv[r0:r1], ALU.add, [OOK, bdK], [OOK])
                tt("dve", Sm[:], Sm[:], EGL[:, c, :].unsqueeze(2).to_broadcast([128, 4, 128]), ALU.mult, [SmK, EGLK], [SmK])
                tt("dve", Sm[:], Sm[:], bev[:, :, :], ALU.add, [SmK, beK], [SmK])
                cp("act", Sb_[:], Sm[:], [SmK], [SbK])
            if full:
                SSO, SSOK = g4["SSO"]
                for h in range(4):
                    act(Rr[:TT, 0, :], OO[:TT, h, :], AF.Square, [OOK], [RrK, SSOK], accum_out=SSO[:TT, h:h + 1])
                rsqrt_chain((G_("RO"), GK("RO")), (G_("SSO"), SSOK), 1.0 / 128, 1e-6)
                tt("dve", OO[:TT], OO[:TT], b4("RO"), ALU.mult, [OOK, GK("RO")], [OOK])
                tt("pool", OO[:TT], OO[:TT], dnwb[:TT, :].unsqueeze(1).to_broadcast([TT, 4, 128]), ALU.mult, [OOK, dnwbK], [OOK])
                tt("dve", MIX[:TT, 0:512].rearrange("p (h e) -> p h e", h=4), OO[:TT], SZD[:TT, :].rearrange("p (h e) -> p h e", h=4), ALU.mult, [OOK, SZDK], [MIXK])
            end_stream()
            stS = begin_stream([3, 4])
            by5, by5K = y5bank
            PAKs = [PAK + "0", PAK + "1"]; PBKs = [PBK + "0", PBK + "1"]
            AbKs = [AbK + "0", AbK + "1"]; BbKs = [BbK + "0", BbK + "1"]
            if kind == "p":
                PAf = PA[:].rearrange("p g t -> p (g t)"); PBf = PB[:].rearrange("p g t -> p (g t)")
                xs_ = XS[0][0]

                def s1(e):
                    i, q, hf, gs = e % 2, e // 2, e % 2, 4 * e
                    hs = slice(64 * hf, 64 * hf + 64)
                    b1, b1K = bank(); b2, b2K = bank()
                    b1v = b1[:].rearrange("p (g t) -> p g t", g=4)
                    b2v = b2[:].rearrange("p (g t) -> p g t", g=4)
                    for j in range(4):
                        mm(b1v[:, j, :], BBn[hs, q, j, :], uT[hs, q, :], True, True, [BBnK, uTK], [b1K])
                        mm(b2v[:, j, :], BBs[hs, q, j, :], uT[hs, q, :], True, True, [BBsK, uTK], [b2K])
                    pa = PAf[:, i * 512:(i + 1) * 512].rearrange("p (h g t) -> p g h t", h=2, g=4)
                    pb = PBf[:, i * 512:(i + 1) * 512].rearrange("p (h g t) -> p g h t", h=2, g=4)
                    cosb = COS[:, gs:gs + 4, :].unsqueeze(2).to_broadcast([128, 4, 2, 64])
                    sinb = SIN[:, gs:gs + 4, :].unsqueeze(2).to_broadcast([128, 4, 2, 64])
                    tt("dve", pa, b1v[:, :, :].rearrange("p g (h t) -> p g h t", h=2), cosb, ALU.mult, [b1K, COSK], [PAKs[i]])
                    tt("dve", pb, b2v[:, :, :].rearrange("p g (h t) -> p g h t", h=2), sinb, ALU.mult, [b2K, SINK], [PBKs[i]])
                    tt("pool", PAf[:, i * 512:(i + 1) * 512], PAf[:, i * 512:(i + 1) * 512], PBf[:, i * 512:(i + 1) * 512], ALU.add, [PAKs[i], PBKs[i]], [PAKs[i]])

                def s2(e, h):
                    i, gs = e % 2, 4 * e
                    k2 = (e % 2) * 2 + h
                    kk = f"s5sm{k2}"
                    pa3 = PAf[:, i * 512:(i + 1) * 512].rearrange("p (h g t) -> p h g t", h=2, g=4)
                    pb3 = PBf[:, i * 512:(i + 1) * 512].rearrange("p (h g t) -> p h g t", h=2, g=4)
                    if h == 0:
                        tt("dve", pa3[:, h, :, 0], pa3[:, h, :, 0], xs_[:, gs:gs + 4], ALU.add, [PAKs[i], XS0K[e]], [PAKs[i]])
                    c0 = i * 512 + h * 256
                    op("dve", lambda en, c0=c0, gs=gs: en.tensor_tensor_scan(
                        out=PBf[:, c0:c0 + 256], data0=RHOT[:, gs:gs + 4, :].rearrange("p g t -> p (g t)"),
                        data1=PAf[:, c0:c0 + 256], initial=0.0, op0=ALU.mult, op1=ALU.add),
                       r=[PAKs[i], RHOTK], w=[PBKs[i]])
                    tt("dve", AL2[:, k2, :], pb3[:, h, :, 63], RC[:, gs:gs + 4], ALU.mult, [PBKs[i], RCK], [kk + "a"])
                    tt("dve", BL2[:, k2, :], pb3[:, h, :, 63], RS[:, gs:gs + 4], ALU.mult, [PBKs[i], RSK], [kk + "b"])
                    bk, bkK = bank()
                    mm(bk[:, 0:4], psw[:, :], BL2[:, k2, :], True, True, [pswK, kk + "b"], [bkK])
                    if h == 0:
                        tt("dve", pa3[:, 1, :, 0], pa3[:, 1, :, 0], AL2[:, k2, :], ALU.add, [PAKs[i], kk + "a"], [PAKs[i]])
                        tt("dve", pa3[:, 1, :, 0], pa3[:, 1, :, 0], bk[:, 0:4], ALU.add, [PAKs[i], bkK], [PAKs[i]])
                    else:
                        tt("dve", xs_[:, gs:gs + 4], bk[:, 0:4], AL2[:, k2, :], ALU.add, [bkK, kk + "a"], [XS0K[e]])

                def s3(e):
                    i, q, gs = e % 2, e // 2, 4 * e
                    pb = PBf[:, i * 512:(i + 1) * 512].rearrange("p (h g t) -> p g h t", h=2, g=4)
                    cosb = COS[:, gs:gs + 4, :].unsqueeze(2).to_broadcast([128, 4, 2, 64])
                    sinb = SIN[:, gs:gs + 4, :].unsqueeze(2).to_broadcast([128, 4, 2, 64])
                    ab = Ab[:, 4 * i:4 * i + 4, :].rearrange("p g (h t) -> p g h t", h=2)
                    bbv = Bb[:, 4 * i:4 * i + 4, :].rearrange("p g (h t) -> p g h t", h=2)
                    tt("pool", ab, pb, cosb, ALU.mult, [PBKs[i], COSK], [AbKs[i]])
                    tt("dve", bbv, pb, sinb, ALU.mult, [PBKs[i], SINK], [BbKs[i]])
                    for g4_ in range(4):
                        g = gs + g4_
                        g8 = g % 8
                        mm(by5[:TT, g * 16:(g + 1) * 16], uT[:, q, :TT], diagD[:, q, g8 * 16:(g8 + 1) * 16], True, False, [uTK, diagDK], [by5K])
                        mm(by5[:TT, g * 16:(g + 1) * 16], Ab[:, 4 * i + g4_, :TT], Md1[:, g * 16:(g + 1) * 16], False, False, [AbKs[i], Md1K], [by5K])
                        mm(by5[:TT, g * 16:(g + 1) * 16], Bb[:, 4 * i + g4_, :TT], Md2[:, g * 16:(g + 1) * 16], False, True, [BbKs[i], Md2K], [by5K])

                s1(0)
                for e in range(8):
                    s2(e, 0)
                    if e + 1 < 8:
                        s1(e + 1)
                    s2(e, 1)
                    if full:
                        s3(e)
            else:
                for q in range(4):
                    PAv = PA[:, :, :TT].rearrange("p g (s l) -> p g s l", s=nsg)
                    PBv = PB[:, :, :TT].rearrange("p g (s l) -> p g s l", s=nsg)
                    for half in range(2):
                        b1, b1K = bank(); b2, b2K = bank()
                        b1v = b1[:].rearrange("p (g t) -> p g t", g=4)
                        b2v = b2[:].rearrange("p (g t) -> p g t", g=4)
                        hs = slice(64 * half, 64 * half + 64)
                        for j in range(4):
                            mm(b1v[:, j, :TT], BBn[hs, q, j, :], uT[hs, q, :TT], True, True, [BBnK, uTK], [b1K])
                            mm(b2v[:, j, :TT], BBs[hs, q, j, :], uT[hs, q, :TT], True, True, [BBsK, uTK], [b2K])
                        gs = 8 * q + 4 * half
                        cosb = COS[:, gs:gs + 4, 0:SL].unsqueeze(2).to_broadcast([128, 4, nsg, SL])
                        sinb = SIN[:, gs:gs + 4, 0:SL].unsqueeze(2).to_broadcast([128, 4, nsg, SL])
                        tt("dve", PAv[:, 4 * half:4 * half + 4], b1v[:, :, :TT].rearrange("p g (s l) -> p g s l", s=nsg), cosb, ALU.mult, [b1K, COSK], PAKs)
                        tt("dve", PBv[:, 4 * half:4 * half + 4], b2v[:, :, :TT].rearrange("p g (s l) -> p g s l", s=nsg), sinb, ALU.mult, [b2K, SINK], PBKs)
                    tt("pool", PA[:, :, :TT], PA[:, :, :TT], PB[:, :, :TT], ALU.add, PAKs + PBKs, PAKs)
                    for (s, st) in segs:
                        xs_, xsK = XS[st]
                        for g in range(8):
                            op("dve", lambda e, g=g, s=s, xs_=xs_, q=q: e.tensor_tensor_scan(
                                out=PB[:, g, s * SL:(s + 1) * SL], data0=rho[:, 8 * q + g:8 * q + g + 1].to_broadcast([128, SL]),
                                data1=PA[:, g, s * SL:(s + 1) * SL], initial=xs_[:, 8 * q + g:8 * q + g + 1], op0=ALU.mult, op1=ALU.add),
                               r=PAKs + [rhoK, xsK], w=PBKs)
                        e_ = (s + 1) * SL - 1
                        tt("dve", AL[:], PB[:, :, e_], COS[:, 8 * q:8 * q + 8, SL - 1], ALU.mult, PBKs + [COSK], [ALK])
                        tt("dve", BL[:], PB[:, :, e_], SIN[:, 8 * q:8 * q + 8, SL - 1], ALU.mult, PBKs + [SINK], [BLK])
                        bk, bkK = bank()
                        mm(bk[:, 0:8], ident[:, :], AL[:], True, False, [identK, ALK], [bkK])
                        mm(bk[:, 0:8], psw[:, :], BL[:], False, True, [pswK, BLK], [bkK])
                        cp("act", xs_[:, 8 * q:8 * q + 8], bk[:, 0:8], [bkK], [xsK])
                    if not full:
                        continue
                    cosf = COS[:, 8 * q:8 * q + 8, 0:SL].unsqueeze(2).to_broadcast([128, 8, nsg, SL])
                    sinf = SIN[:, 8 * q:8 * q + 8, 0:SL].unsqueeze(2).to_broadcast([128, 8, nsg, SL])
                    tt("pool", Ab[:, :, :TT].rearrange("p g (s l) -> p g s l", s=nsg), PBv, cosf, ALU.mult, PBKs + [COSK], AbKs)
                    tt("dve", Bb[:, :, :TT].rearrange("p g (s l) -> p g s l", s=nsg), PBv, sinf, ALU.mult, PBKs + [SINK], BbKs)
                    for g8 in range(8):
                        g = 8 * q + g8
                        mm(by5[:TT, g * 16:(g + 1) * 16], uT[:, q, :TT], diagD[:, q, g8 * 16:(g8 + 1) * 16], True, False, [uTK, diagDK], [by5K])
                        mm(by5[:TT, g * 16:(g + 1) * 16], Ab[:, g8, :TT], Md1[:, g * 16:(g + 1) * 16], False, False, AbKs + [Md1K], [by5K])
                        mm(by5[:TT, g * 16:(g + 1) * 16], Bb[:, g8, :TT], Md2[:, g * 16:(g + 1) * 16], False, True, BbKs + [Md2K], [by5K])
            end_stream()
            if part == "rest_nomerge":
                return [stG, stS]
            merge_streams([stG, stS])
            if not full:
                return None
            cp("act", Y5[:TT, :], by5[:TT, :], [by5K], [Y5K])
            tt("dve", G1[:TT, :], Y5[:TT, :], Y5[:TT, :], ALU.mult, [Y5K], [G1K])
            ts("dve", G1[:TT, :], G1[:TT, :], 0.044715, 1.0, ALU.mult, ALU.add, [G1K], [G1K])
            tt("dve", G1[:TT, :], G1[:TT, :], Y5[:TT, :], ALU.mult, [G1K, Y5K], [G1K])
            act(G2[:TT, :], G1[:TT, :], AF.Sigmoid, [G1K], [G2K], scale=2.0 * math.sqrt(2.0 / PI))
            tt("dve", GY[:TT, :], Y5[:TT, :], G2[:TT, :], ALU.mult, [Y5K, G2K], [GYK])
            cp("pool", GYb[:TT, :], GY[:TT, :], [GYK], [GYbK])
            pg, pgK = bbank()
            pgv = pg[:].rearrange("p (k t) -> p k t", k=8)
            for q in range(4):
                tr(pgv[:, q, :TT], GYb[:TT, q * 128:(q + 1) * 128], identb[:TT, :TT], [GYbK, identbK], [pgK])
            cp("act", GYT[:, :, :TT], pgv[:, 0:4, :TT], [pgK], [GYTK])
            bg, bgK = bank()
            for q in range(4):
                mm(bg[:TT, :], GYT[:, q, :TT], glub[:, q, :], q == 0, q == 3, [GYTK, glubK], [bgK])
            tt("dve", G1[:TT, :], bg[:TT, :], glbb[:TT, :], ALU.add, [bgK, glbbK], [G1K])
            act(G2[:TT, :], G1[:TT, :], AF.Sigmoid, [G1K], [G2K])
            tt("dve", G1[:TT, :], GY[:TT, :], G2[:TT, :], ALU.mult, [GYK, G2K], [G1K])
            tt("dve", MIX[:TT, 512:1024], G1[:TT, :], SZS[:TT, :], ALU.mult, [G1K, SZSK], [MIXK])
            if dbg and kind == "p" and ti == TP // 128 - 1:
                dma(dbg_mix[:, :], MIX[:TT, :], r=[MIXK])
            pm, pmK = bbank()
            pmv = pm[:].rearrange("p (k t) -> p k t", k=8)
            for k in range(8):
                tr(pmv[:, k, :TT], MIX[:TT, k * 128:(k + 1) * 128], identb[:TT, :TT], [MIXK, identbK], [pmK])
            cp("act", MIXT[:, :, :TT], pmv[:, :, :TT], [pmK], [MIXTK])
            for half in range(2):
                bo, boK = bank()
                for k in range(8):
                    mm(bo[:TT, :], MIXT[:, k, :TT], woutb[:, k, half * 512:(half + 1) * 512], k == 0, k == 7, [MIXTK, woutbK], [boK])
                tt("dve", XN[:TT, half * 512:(half + 1) * 512], x[:TT, half * 512:(half + 1) * 512], bo[:TT, :], ALU.add, [xK, boK], [XNK])
            ss2, ss2K = sc["ss2"]; r2, r2K = sc["r2"]
            act(MIX[:TT, :], XN[:TT, :], AF.Square, [XNK], [MIXK, ss2K], accum_out=ss2[:TT, :])
            rsqrt_chain((r2[:TT, :], r2K), (ss2[:TT, :], ss2K), 1.0 / 1024, 1e-6)
            ts("dve", XN[:TT, :], XN[:TT, :], r2[:TT, 0:1], None, ALU.mult, None, [XNK, r2K], [XNK])
            tt("pool", XN[:TT, :], XN[:TT, :], fnwb[:TT, :], ALU.mult, [XNK, fnwbK], [XNK])
            dma(ydst, XN[:TT, :], r=[XNK])

        def write_states(st, conv_dst, dn_dst, re_dst, im_dst, RAW, RAWK, s, L):
            for t in range(12):
                dma(conv_dst[:, t * 128:(t + 1) * 128].rearrange("r c -> c r"), RAW[:, t, s, L:L + 3], r=[RAWK], slow=True)
            dma(dn_dst.rearrange("h d e -> d h e"), Sst[st][0][:], r=[Sst[st][1]])
            if st == 0:
                op("dve", lambda e: e.reciprocal(out=RC[:], in_=rho[:]), r=[rhoK, RCK], w=[RCK])
                tt("dve", XS[0][0][:], XS[0][0][:], RC[:], ALU.mult, XS0K + [RCK], XS0K)
            bk, bkK = bank()
            tr(bk[0:32, 0:128], XS[st][0][:, :], ident[:, :], (XS0K if st == 0 else [XS[st][1]]) + [identK], [bkK])
            cp("dve", xso, bk[0:32, 0:128], [bkK], [xsoK])
            dma(re_dst, xso[:, 0:64], r=[xsoK])
            dma(im_dst, xso[:, 64:128], r=[xsoK])

        do_tile("s", 0, xs[:, :], ys[:, :])
        for s in range(NS):
            write_states(1 + s, o_convs[s], o_dns[s], o_res[s], o_ims[s], RAWs, RAWsK, s, 32)
        npre = NPRE // 128
        tiles = []
        for ti in range(npre):
            tiles.append((xpre[ti * 128:(ti + 1) * 128, :], None, ("stateq" if ti == npre - 1 else "state")))
        for ti in range(TP // 128):
            tiles.append((xp[ti * 128:(ti + 1) * 128, :], yp[ti * 128:(ti + 1) * 128, :], "full"))
        front_done = False
        for t, (xsrc_, ydst_, mode_) in enumerate(tiles):
            b_ = t % 2
            if not front_done:
                do_tile("p", t, xsrc_, ydst_, mode=mode_, b=b_, part="front")
            front_done = False
            if mode_ != "full" and t + 1 < len(tiles):
                sts = do_tile("p", t, xsrc_, ydst_, mode=mode_, b=b_, part="rest_nomerge")
                nx = tiles[t + 1]
                stF = begin_stream(["y5"], bb=0)
                do_tile("p", t + 1, nx[0], nx[1], mode=nx[2], b=(t + 1) % 2, part="front")
                end_stream()
                merge_streams(sts + [stF])
                front_done = True
            else:
                do_tile("p", t, xsrc_, ydst_, mode=mode_, b=b_, part="rest")
        write_states(0, o_convp, o_dnp, o_rep, o_imp, RAWp, RAWpK, 0, 128)
        P.emit()
    return nc


_CACHE = {}


DBG = ()
LAST = {}
_CACHE = {}


def run(inputs, T, n_cores=8):
    NS = 2
    f = lambda a: np.ascontiguousarray(np.asarray(a, dtype=np.float32))
    xpf = f(inputs["x_prompt"])[:, :T]
    xsf = f(inputs["x_sample"])
    nb = xpf.shape[0]
    nseg = max(1, n_cores // nb)
    TP = T // nseg
    NPRE = (nseg - 1) * TP
    key = (TP, NPRE)
    if key not in _CACHE:
        _CACHE[key] = build(TP, NS, dbg=DBG, NPRE=NPRE)
    nc = _CACHE[key]
    in_maps = []
    wmap = {k: f(inputs[k]).reshape(WSHAPES[k]) for k in WSHAPES}
    cmap = {"c_" + k: v for k, v in CONSTS.items()}
    for c in range(n_cores):
        b, k = c // nseg, c % nseg
        m = {}
        if b < nb:
            m["xp"] = f(xpf[b, k * TP:(k + 1) * TP])
            if NPRE:
                pre = np.zeros((NPRE, 1024), np.float32)
                if k > 0:
                    pre[NPRE - k * TP:] = xpf[b, :k * TP]
                m["xpre"] = pre
        else:
            m["xp"] = np.zeros((TP, 1024), np.float32)
            if NPRE:
                m["xpre"] = np.zeros((NPRE, 1024), np.float32)
        sl = slice(NS * c, NS * c + NS)
        m["xs"] = f(xsf[sl]).reshape(NS * 32, 1024)
        m["cconv"] = f(inputs["cache_conv"][0, sl])
        m["sdn"] = f(inputs["state_dn"][0, sl])
        m["sre"] = f(inputs["state_s5_re"][0, sl])
        m["sim"] = f(inputs["state_s5_im"][0, sl])
        m.update(wmap); m.update(cmap)
        in_maps.append(m)
    res = run_bass_kernel_spmd(nc, in_maps, core_ids=list(range(n_cores)))
    R = res.results
    LAST['R'] = R
    y_prompt = np.stack([np.concatenate([R[b * nseg + k]["yp"] for k in range(nseg)]) for b in range(nb)])
    y_sample = np.concatenate([R[c]["ys"].reshape(NS, 32, 1024) for c in range(n_cores)])
    last = [b * nseg + nseg - 1 for b in range(nb)]
    convp = np.stack([R[c]["convp"] for c in last])[None]
    dnp = np.stack([R[c]["dnp"] for c in last])[None]
    rep = np.stack([R[c]["rep"] for c in last])[None]
    imp = np.stack([R[c]["imp"] for c in last])[None]
    convs = np.concatenate([R[c]["convs"] for c in range(n_cores)])[None]
    dns = np.concatenate([R[c]["dns"] for c in range(n_cores)])[None]
    res_ = np.concatenate([R[c]["res"] for c in range(n_cores)])[None]
    ims = np.concatenate([R[c]["ims"] for c in range(n_cores)])[None]
    outs = (y_prompt, y_sample, convp, dnp, rep, imp, convs, dns, res_, ims)
    return tuple(np.ascontiguousarray(o, dtype=np.float32) for o in outs)


def kernel(**inputs):
    return run(inputs, T=16384, n_cores=8)
```

```python
import math
import numpy as np
from contextlib import ExitStack
import concourse.bass as bass
import concourse.mybir as mybir
from concourse.bass_utils import run_bass_kernel_spmd

F32 = mybir.dt.float32
BF16 = mybir.dt.bfloat16
I32 = mybir.dt.int32
ALU = mybir.AluOpType
AF = mybir.ActivationFunctionType

EPOCH = 30000
NDMASEM = 24
PI = math.pi


class Prog:
    ENGS = ["pe", "act", "dve", "pool", "sp"]

    def __init__(self, nc, es):
        self.nc = nc
        self.es = es
        self.ops = {e: [] for e in self.ENGS}
        self.cnt = {e: 0 for e in self.ENGS}
        self.sems = {e: [] for e in self.ENGS}
        self.dma_sems = [es.enter_context(nc.semaphore(f"dq{i}")) for i in range(NDMASEM)]
        self.dma_n = 0
        self.dma_tokens = []
        self.lastw = {}
        self.readers = {}
        self.waited = {e: {} for e in self.ENGS}

    def _sem(self, e, idx):
        ep = idx // EPOCH
        while len(self.sems[e]) <= ep:
            self.sems[e].append(self.es.enter_context(self.nc.semaphore(f"s_{e}_{len(self.sems[e])}")))
        return self.sems[e][ep], idx % EPOCH + 1

    def op(self, eng, fn, r=(), w=(), dma=False):
        deps = []
        for k in r:
            if k in self.lastw:
                deps.append(self.lastw[k])
        for k in w:
            if k in self.lastw:
                deps.append(self.lastw[k])
            deps.extend(self.readers.get(k, []))
        if dma:
            i = self.dma_n
            self.dma_n += 1
            sem = self.dma_sems[i % NDMASEM]
            val = 16 * (i // NDMASEM + 1)
            if i >= NDMASEM:
                deps.append(self.dma_tokens[i - NDMASEM])
            tok = (sem, val, "dma")
            self.dma_tokens.append(tok)
            inc = (sem, 16)
        else:
            idx = self.cnt[eng]
            self.cnt[eng] += 1
            sem, val = self._sem(eng, idx)
            tok = (sem, val, eng)
            inc = (sem, 1)
        waits = []
        wd = self.waited[eng]
        for (s, v, te) in deps:
            if te == "pe" and eng == "pe" and not dma:
                continue
            key = id(s)
            if wd.get(key, 0) >= v:
                continue
            wd[key] = v
            waits.append((s, v))
        self.ops[eng].append((fn, waits, inc))
        for k in r:
            self.readers.setdefault(k, []).append(tok)
        for k in w:
            self.lastw[k] = tok
            self.readers[k] = []
        return tok

    def emit(self):
        nc = self.nc
        final = []
        for t in self.dma_tokens[-NDMASEM:]:
            final.append((t[0], t[1]))
        for e in self.ENGS:
            if self.cnt[e] > 0:
                final.append(self._sem(e, self.cnt[e] - 1))
        ops = self.ops

        def run(engh, lst, extra=None):
            for fn, waits, inc in lst:
                for (s, v) in waits:
                    engh.wait_ge(s, v)
                ins = fn(engh)
                ins.then_inc(inc[0], inc[1])
            if extra:
                for (s, v) in extra:
                    engh.wait_ge(s, v)

        with nc.Block() as block:
            @block.tensor
            def _(e):
                run(e, ops["pe"])

            @block.scalar
            def _(e):
                run(e, ops["act"])

            @block.vector
            def _(e):
                run(e, ops["dve"])

            @block.gpsimd
            def _(e):
                run(e, ops["pool"])

            @block.sync
            def _(e):
                run(e, ops["sp"], final)


def make_consts():
    c = {}
    c["ident"] = np.eye(128, dtype=np.float32)
    c["lst"] = np.tril(np.ones((128, 128), np.float32), -1)
    c["ones"] = np.ones((128, 128), np.float32)
    for name, TT, C in (("p", 128, 64), ("s", 64, 32)):
        idx = np.arange(TT)
        same = (idx[:, None] // C) == (idx[None, :] // C)
        tri = same & (idx[:, None] <= idx[None, :])
        us = same & (idx[None, :] > idx[:, None])
        ui = same & (idx[None, :] >= idx[:, None])
        ls = same & (idx[:, None] > idx[None, :])
        f = lambda a: a.astype(np.float32).copy()
        c["tri_" + name] = f(tri)
        c["blk_" + name] = f(same)
        c["us_" + name] = f(us)
        c["ui_" + name] = f(ui)
        c["ls_" + name] = f(ls)
        c["id_" + name] = f(np.eye(TT))
        c["tw_" + name] = f(2 * np.eye(TT))
        nch = TT // C
        ind = np.zeros((TT, nch, 128), np.float32)
        for ch in range(nch):
            ind[ch * C:(ch + 1) * C, ch, :] = 1.0
        c["ind_" + name] = ind
    c["tv"] = np.tile(np.arange(1, 65, dtype=np.float32)[None, :], (128, 1))
    m64 = np.ones((128, 64), np.float32)
    m64[:, 0] = 0.0
    c["m64"] = m64
    sg = np.ones((128, 4), np.float32)
    sg[:64, 0] = -1.0
    sg[64:, 1] = -1.0
    sg[:, 2] = -1.0
    c["sgn"] = sg
    psw = np.zeros((128, 128), np.float32)
    for n in range(64):
        psw[64 + n, n] = -1.0
        psw[n, 64 + n] = 1.0
    c["psw"] = psw
    rm = np.zeros((128, 8), np.float32)
    for g in range(8):
        rm[g * 16:(g + 1) * 16, g] = 1.0
    c["rowmask"] = rm
    return c


CONSTS = make_consts()

WSHAPES = {
    "norm_w": [1024], "w_in": [1024, 3080], "conv_w": [4, 1536], "dn_A_log": [4], "dn_dt_bias": [4],
    "dn_norm_w": [128], "s5_A_re": [32, 64], "s5_A_im": [32, 64], "s5_log_dt": [32],
    "s5_B_re": [32, 64, 16], "s5_B_im": [32, 64, 16], "s5_C_re": [32, 16, 64], "s5_C_im": [32, 16, 64],
    "s5_D": [512], "glu_w": [512, 512], "glu_b": [512], "w_out": [1024, 1024], "final_norm_w": [1024],
}


def build(TP, NS=2, dbg=(), NPRE=0):
    nc = bass.Bass("TRN2", target_bir_lowering=False)
    din = lambda n, s: nc.dram_tensor(n, list(s), F32, kind="ExternalInput").ap()
    dout = lambda n, s: nc.dram_tensor(n, list(s), F32, kind="ExternalOutput").ap()
    xp = din("xp", [TP, 1024]); xs = din("xs", [NS * 32, 1024])
    xpre = din("xpre", [NPRE, 1024]) if NPRE else None
    cconv = din("cconv", [NS, 3, 1536]); sdn = din("sdn", [NS, 4, 128, 128])
    sre = din("sre", [NS, 32, 64]); sim = din("sim", [NS, 32, 64])
    W = {k: din(k, v) for k, v in WSHAPES.items()}
    CD = {k: din("c_" + k, v.shape) for k, v in CONSTS.items()}
    yp = dout("yp", [TP, 1024]); ys = dout("ys", [NS * 32, 1024])
    o_convp = dout("convp", [3, 1536]); o_dnp = dout("dnp", [4, 128, 128])
    o_rep = dout("rep", [32, 64]); o_imp = dout("imp", [32, 64])
    o_convs = dout("convs", [NS, 3, 1536]); o_dns = dout("dns", [NS, 4, 128, 128])
    o_res = dout("res", [NS, 32, 64]); o_ims = dout("ims", [NS, 32, 64])
    dbg_mix = nc.dram_tensor("dbg_mix", [128, 1024], BF16, kind="ExternalOutput").ap() if dbg else None

    with ExitStack() as es:
        P = Prog(nc, es)
        _sbn = [0]

        def sb(shape, dt=F32, name=None):
            _sbn[0] += 1
            nm = name or f"t{_sbn[0]}"
            t = es.enter_context(nc.sbuf_tensor(nm, list(shape), dt))
            return t, nm

        banks = []
        for i in range(6):
            banks.append((es.enter_context(nc.psum_tensor(f"pb{i}", [128, 512], F32)), f"pb{i}"))
        bbanks = []
        for i in range(2):
            bbanks.append((es.enter_context(nc.psum_tensor(f"pbb{i}", [128, 1024], BF16)), f"pbb{i}"))
        _bi = [0, 0]
        _stream = [None]

        y5bank = banks.pop()

        def bank():
            st_ = _stream[0]
            if st_ is not None:
                st_["bi"] = (st_["bi"] + 1) % len(st_["banks"])
                return st_["banks"][st_["bi"]]
            _bi[0] = (_bi[0] + 1) % len(banks)
            return banks[_bi[0]]

        def begin_stream(bank_ids, bb=None):
            bl_ = [y5bank if i == "y5" else banks[i] for i in bank_ids]
            _stream[0] = dict(ops=[], banks=bl_, bi=0, bb=bb)
            return _stream[0]

        def end_stream():
            _stream[0] = None

        def merge_streams(sts):
            lists = [st_["ops"] for st_ in sts]
            idx = [0] * len(lists)
            total = sum(len(l) for l in lists)
            for _ in range(total):
                best, bv = None, None
                for i, l in enumerate(lists):
                    if idx[i] < len(l):
                        frac = idx[i] / len(l)
                        if bv is None or frac < bv:
                            best, bv = i, frac
                a = lists[best][idx[best]]
                idx[best] += 1
                P.op(a[0], a[1], r=a[2], w=a[3], dma=a[4])

        def bbank():
            st_ = _stream[0]
            if st_ is not None and st_.get("bb") is not None:
                return bbanks[st_["bb"]]
            _bi[1] = (_bi[1] + 1) % len(bbanks)
            return bbanks[_bi[1]]

        def op(eng, fn, r=(), w=(), dma=False):
            if _stream[0] is not None:
                _stream[0]["ops"].append((eng, fn, tuple(r), tuple(w), dma))
                return None
            return P.op(eng, fn, r=r, w=w, dma=dma)

        def dma(out, in_, r=(), w=(), slow=False):
            if slow:
                return op("sp", lambda e: e.dma_start(out=out, in_=in_, allow_slow_non_contiguous=True), r=r, w=w, dma=True)
            return op("sp", lambda e: e.dma_start(out=out, in_=in_), r=r, w=w, dma=True)

        def mm(out, lhsT, rhs, start, stop, r, w):
            return op("pe", lambda e: e.matmul(out, lhsT=lhsT, rhs=rhs, start=start, stop=stop), r=r, w=w)

        def tr(out, in_, ident, r, w):
            return op("pe", lambda e: e.transpose(out=out, in_=in_, identity=ident), r=r, w=w)

        def act(out, in_, func, r, w, **kw):
            return op("act", lambda e: e.activation(out=out, in_=in_, func=func, **kw), r=r, w=w)

        def tt(eng, out, in0, in1, o, r, w):
            return op(eng, lambda e: e.tensor_tensor(out=out, in0=in0, in1=in1, op=o), r=r, w=w)

        def ts(eng, out, in0, s1, s2, o0, o1, r, w):
            if o1 is None:
                return op(eng, lambda e: e.tensor_scalar(out=out, in0=in0, scalar1=s1, scalar2=None, op0=o0), r=r, w=w)
            return op(eng, lambda e: e.tensor_scalar(out=out, in0=in0, scalar1=s1, scalar2=s2, op0=o0, op1=o1), r=r, w=w)

        def stt(eng, out, in0, sc, in1, o0, o1, r, w):
            return op(eng, lambda e: e.scalar_tensor_tensor(out=out, in0=in0, scalar=sc, in1=in1, op0=o0, op1=o1), r=r, w=w)

        def cp(eng, out, in_, r, w):
            if eng == "act":
                return op("act", lambda e: e.copy(out=out, in_=in_), r=r, w=w)
            return op(eng, lambda e: e.tensor_copy(out=out, in_=in_), r=r, w=w)

        TA, TAK = sb([128, 512], F32, "TA")
        TB, TBK = sb([128, 512], F32, "TB")
        TC, TCK = sb([128, 512], F32, "TC")
        TD, TDK = sb([128, 512], F32, "TD")
        TE, TEK = sb([128, 512], F32, "TE")
        TF, TFK = sb([128, 512], F32, "TF")
        xt, xtK = sb([128, 1024], F32, "xt")
        XN, XNK = sb([128, 1024], F32, "XN")
        v4 = lambda t: t[:].rearrange("p (h e) -> p h e", h=4)

        C = {}
        for k, v in CONSTS.items():
            if k in ("ones",):
                dma(XN[:, 0:128], CD[k], w=[XNK])
                C[k] = (XN[:, 0:128], XNK)
                continue
            t, nm = sb(v.shape, F32, "k_" + k)
            dma(t[:], CD[k], w=[nm])
            C[k] = (t, nm)
        ident, identK = C["ident"]
        identb, identbK = sb([128, 128], BF16, "identb")
        cp("dve", identb[:], ident[:], [identK], [identbK])
        onesb, onesbK = sb([128, 128], BF16, "onesb")
        cp("dve", onesb[:], C["ones"][0], [C["ones"][1]], [onesbK])
        sgn, sgnK = C["sgn"]

        nwc, nwcK = sb([128, 8], F32, "nwc")
        for k in range(8):
            dma(nwc[:, k:k + 1], W["norm_w"][k * 128:(k + 1) * 128].rearrange("(p o) -> p o", o=1), w=[nwcK])
        winb, winbK = sb([128, 8, 3080], BF16, "winb")
        n_ = 0
        for k in range(8):
            for pc in range(4):
                c0, c1 = pc * 770, (pc + 1) * 770
                dma(XN[:, 0:770], W["w_in"][k * 128:(k + 1) * 128, c0:c1], w=[XNK])
                ts("dve" if n_ % 2 == 0 else "pool", winb[:, k, c0:c1], XN[:, 0:770], nwc[:, k:k + 1], None, ALU.mult, None, [XNK, nwcK], [winbK])
                n_ += 1
        woutb, woutbK = sb([128, 8, 1024], BF16, "woutb")
        for k in range(8):
            dma(XN[:, :], W["w_out"][k * 128:(k + 1) * 128, :], w=[XNK])
            cp("dve" if k % 2 == 0 else "pool", woutb[:, k, :], XN[:, :], [XNK], [woutbK])
        glub, glubK = sb([128, 4, 512], BF16, "glub")
        for k in range(4):
            dma(XN[:, 0:512], W["glu_w"][k * 128:(k + 1) * 128, :], w=[XNK])
            cp("dve", glub[:, k, :], XN[:, 0:512], [XNK], [glubK])
        fnwb, fnwbK = sb([128, 1024], F32, "fnwb")
        dma(fnwb[:], W["final_norm_w"].partition_broadcast(128), w=[fnwbK])
        glbb, glbbK = sb([128, 512], BF16, "glbb")
        dma(XN[:, 0:512], W["glu_b"].partition_broadcast(128), w=[XNK])
        cp("dve", glbb[:], XN[:, 0:512], [XNK], [glbbK])
        dnwb, dnwbK = sb([128, 128], F32, "dnwb")
        dma(dnwb[:], W["dn_norm_w"].partition_broadcast(128), w=[dnwbK])
        dtb, dtbK = sb([128, 4], F32, "dtb")
        dma(dtb[:], W["dn_dt_bias"].partition_broadcast(128), w=[dtbK])
        negA, negAK = sb([128, 4], F32, "negA")
        dma(negA[:], W["dn_A_log"].partition_broadcast(128), w=[negAK])
        act(negA[:], negA[:], AF.Exp, [negAK], [negAK])
        ts("dve", negA[:], negA[:], -1.0, None, ALU.mult, None, [negAK], [negAK])
        acc, accK = sb([128, 4, 128], F32, "acc")
        accKs = [accK + str(t) for t in range(4)]
        cw4, cw4K = None, None
        CW, CWK = sb([128, 12, 4], F32, "CW")
        for t3 in range(3):
            dma(XN[0:4, 0:512], W["conv_w"][:, t3 * 512:(t3 + 1) * 512], w=[XNK])
            for t4 in range(4):
                t = t3 * 4 + t4
                bk, bkK = bank()
                tr(bk[:, 0:4], XN[0:4, t4 * 128:(t4 + 1) * 128], ident[0:4, 0:4], [XNK, identK], [bkK])
                cp("dve", CW[:, t, :], bk[:, 0:4], [bkK], [CWK])

        PA, PAK = sb([128, 8, 128], F32, "PA")
        PB, PBK = sb([128, 8, 128], F32, "PB")
        KI, KIK = sb([128, 512], I32, "KI")
        COS, COSK = sb([128, 32, 64], F32, "COS")
        SIN, SINK = sb([128, 32, 64], F32, "SIN")

        def reduce_sin(out, arg, tmp, ki, keys, shift):
            (oK, aK, tK, kK) = keys
            ts("dve", tmp, arg, shift + 64 * PI, 1.0 / (2 * PI), ALU.add, ALU.mult, [aK], [tK])
            cp("dve", ki, tmp, [tK], [kK])
            cp("dve", tmp, ki, [kK], [tK])
            ts("dve", tmp, tmp, -2 * PI, shift + 64 * PI, ALU.mult, ALU.add, [tK], [tK])
            tt("dve", tmp, tmp, arg, ALU.add, [tK, aK], [tK])
            ts("dve", tmp, tmp, -3.1415925, 3.1415925, ALU.max, ALU.min, [tK], [tK])
            act(out, tmp, AF.Sin, [tK], [oK])

        ar2, ar2K = sb([32, 128], F32, "ar2")
        ai2, ai2K = sb([32, 128], F32, "ai2")
        for h in range(2):
            dma(ar2[:, h * 64:(h + 1) * 64], W["s5_A_re"], w=[ar2K])
            dma(ai2[:, h * 64:(h + 1) * 64], W["s5_A_im"], w=[ai2K])
        sm = {}
        for nm in ["lre", "lim", "dt", "ldr", "th", "rho", "c0", "s0", "t0", "t1", "lbr", "lbi", "den", "fre", "fim",
                   "FIMS", "FRES", "t2"]:
            sm[nm] = sb([128, 32], F32, "s5_" + nm)
        bk, bkK = bank()
        tr(bk[:, 0:32], ar2[0:32, :], ident[0:32, 0:32], [ar2K, identK], [bkK])
        ts("dve", sm["lre"][0][:], bk[:, 0:32], -1e-4, None, ALU.min, None, [bkK], [sm["lre"][1]])
        bk, bkK = bank()
        tr(bk[:, 0:32], ai2[0:32, :], ident[0:32, 0:32], [ai2K, identK], [bkK])
        cp("dve", sm["lim"][0][:], bk[:, 0:32], [bkK], [sm["lim"][1]])
        dma(sm["dt"][0][:], W["s5_log_dt"].partition_broadcast(128), w=[sm["dt"][1]])
        act(sm["dt"][0][:], sm["dt"][0][:], AF.Exp, [sm["dt"][1]], [sm["dt"][1]])
        S = lambda n: sm[n][0][:]
        K_ = lambda n: sm[n][1]
        tt("dve", S("ldr"), S("lre"), S("dt"), ALU.mult, [K_("lre"), K_("dt")], [K_("ldr")])
        tt("dve", S("th"), S("lim"), S("dt"), ALU.mult, [K_("lim"), K_("dt")], [K_("th")])
        act(S("rho"), S("ldr"), AF.Exp, [K_("ldr")], [K_("rho")])
        reduce_sin(S("s0"), S("th"), S("t0"), KI[:, 0:32], (K_("s0"), K_("th"), K_("t0"), KIK), 0.0)
        reduce_sin(S("c0"), S("th"), S("t0"), KI[:, 0:32], (K_("c0"), K_("th"), K_("t0"), KIK), PI / 2)
        tt("dve", S("lbr"), S("rho"), S("c0"), ALU.mult, [K_("rho"), K_("c0")], [K_("lbr")])
        tt("dve", S("lbi"), S("rho"), S("s0"), ALU.mult, [K_("rho"), K_("s0")], [K_("lbi")])
        tt("dve", S("den"), S("lre"), S("lre"), ALU.mult, [K_("lre")], [K_("den")])
        tt("dve", S("t0"), S("lim"), S("lim"), ALU.mult, [K_("lim")], [K_("t0")])
        tt("dve", S("den"), S("den"), S("t0"), ALU.add, [K_("den"), K_("t0")], [K_("den")])
        op("dve", lambda e: e.reciprocal(out=S("den"), in_=S("den")), r=[K_("den")], w=[K_("den")])
        ts("dve", S("t1"), S("lbr"), -1.0, None, ALU.add, None, [K_("lbr")], [K_("t1")])
        tt("dve", S("t0"), S("t1"), S("lre"), ALU.mult, [K_("t1"), K_("lre")], [K_("t0")])
        tt("dve", S("t2"), S("lbi"), S("lim"), ALU.mult, [K_("lbi"), K_("lim")], [K_("t2")])
        tt("dve", S("t0"), S("t0"), S("t2"), ALU.add, [K_("t0"), K_("t2")], [K_("t0")])
        tt("dve", S("fre"), S("t0"), S("den"), ALU.mult, [K_("t0"), K_("den")], [K_("fre")])
        tt("dve", S("t0"), S("lbi"), S("lre"), ALU.mult, [K_("lbi"), K_("lre")], [K_("t0")])
        tt("dve", S("t2"), S("t1"), S("lim"), ALU.mult, [K_("t1"), K_("lim")], [K_("t2")])
        tt("dve", S("t0"), S("t0"), S("t2"), ALU.subtract, [K_("t0"), K_("t2")], [K_("t0")])
        tt("dve", S("fim"), S("t0"), S("den"), ALU.mult, [K_("t0"), K_("den")], [K_("fim")])
        ts("dve", S("FIMS"), S("fim"), sgn[:, 0:1], None, ALU.mult, None, [K_("fim"), sgnK], [K_("FIMS")])
        ts("dve", S("FRES"), S("fre"), sgn[:, 1:2], None, ALU.mult, None, [K_("fre"), sgnK], [K_("FRES")])
        g16 = lambda t: t[:].rearrange("p (g c) -> p g c", g=32)
        Bst, BstK, Bsw, BswK, bb, bbK, bbs, bbsK, btmp, btmpK = g16(TA), TAK, g16(TB), TBK, g16(TC), TCK, g16(TD), TDK, g16(TE), TEK
        bre = W["s5_B_re"].rearrange("g n c -> n g c")
        bim = W["s5_B_im"].rearrange("g n c -> n g c")
        dma(Bst[0:64], bre, w=[BstK]); dma(Bst[64:128], bim, w=[BstK])
        dma(Bsw[0:64], bim, w=[BswK]); dma(Bsw[64:128], bre, w=[BswK])
        bc = lambda nm: sm[nm][0][:].unsqueeze(2).to_broadcast([128, 32, 16])
        tt("dve", bb, Bst, bc("fre"), ALU.mult, [BstK, K_("fre")], [bbK])
        tt("dve", btmp, Bsw, bc("FIMS"), ALU.mult, [BswK, K_("FIMS")], [btmpK])
        tt("dve", bb, bb, btmp, ALU.add, [bbK, btmpK], [bbK])
        tt("dve", bbs, Bsw, bc("FRES"), ALU.mult, [BswK, K_("FRES")], [bbsK])
        tt("dve", btmp, Bst, bc("fim"), ALU.mult, [BstK, K_("fim")], [btmpK])
        tt("dve", bbs, bbs, btmp, ALU.add, [bbsK, btmpK], [bbsK])
        BBn, BBnK = sb([128, 4, 4, 128], BF16, "BBn")
        BBs, BBsK = sb([128, 4, 4, 128], BF16, "BBs")
        rmask, rmaskK = C["rowmask"]
        for (src, srcK, dst, dstK) in ((bb, bbK, BBn, BBnK), (bbs, bbsK, BBs, BBsK)):
            for q in range(4):
                bk, bkK = bank()
                tr(bk[:, 0:128], src[:, 8 * q:8 * q + 8, :].rearrange("p g c -> p (g c)"), ident[:, :], [srcK, identK], [bkK])
                for g in range(8):
                    hf, j = g // 4, g % 4
                    ts("dve", dst[64 * hf:64 * hf + 64, q, j, :], bk[64 * hf:64 * hf + 64, 0:128], rmask[64 * hf:64 * hf + 64, g:g + 1], None, ALU.mult, None, [bkK, rmaskK], [dstK])
        tv, tvK = C["tv"]
        PAf = PA[:].rearrange("p g t -> p (g t)")
        PBf = PB[:].rearrange("p g t -> p (g t)")
        for q4 in range(4):
            argv = PAf[:, 0:512].rearrange("p (g t) -> p g t", g=8)
            tt("dve", argv, sm["th"][0][:, 8 * q4:8 * q4 + 8].unsqueeze(2).to_broadcast([128, 8, 64]),
               tv[:].unsqueeze(1).to_broadcast([128, 8, 64]), ALU.mult, [K_("th"), tvK], [PAK])
            reduce_sin(SIN[:, 8 * q4:8 * q4 + 8, :].rearrange("p g t -> p (g t)"), PAf[:, 0:512], PBf[:, 0:512], KI[:, :], (SINK, PAK, PBK, KIK), 0.0)
            reduce_sin(COS[:, 8 * q4:8 * q4 + 8, :].rearrange("p g t -> p (g t)"), PAf[:, 0:512], PBf[:, 0:512], KI[:, :], (COSK, PAK, PBK, KIK), PI / 2)
        Md1, Md1K = sb([128, 512], BF16, "Md1")
        Md2, Md2K = sb([128, 512], BF16, "Md2")
        cri, criK = xt[:].rearrange("p (q n) -> p q n", q=8), xtK
        cre = W["s5_C_re"].rearrange("g c n -> (g c) n")
        cim = W["s5_C_im"].rearrange("g c n -> (g c) n")
        for q in range(4):
            dma(cri[:, q, 0:64], cre[q * 128:(q + 1) * 128, :], w=[criK])
            dma(cri[:, q, 64:128], cim[q * 128:(q + 1) * 128, :], w=[criK])
            dma(cri[:, 4 + q, 0:64], cim[q * 128:(q + 1) * 128, :], w=[criK])
            dma(cri[:, 4 + q, 64:128], cre[q * 128:(q + 1) * 128, :], w=[criK])
        for q in range(4):
            bk, bkK = bank()
            tr(bk[:, 0:128], cri[:, q, :], ident[:, :], [criK, identK], [bkK])
            ts("dve", Md1[:, q * 128:(q + 1) * 128], bk[:, 0:128], sgn[:, 1:2], None, ALU.mult, None, [bkK, sgnK], [Md1K])
            bk, bkK = bank()
            tr(bk[:, 0:128], cri[:, 4 + q, :], ident[:, :], [criK, identK], [bkK])
            ts("dve", Md2[:, q * 128:(q + 1) * 128], bk[:, 0:128], -1.0, None, ALU.mult, None, [bkK], [Md2K])
        dcol, dcolK = sb([128, 4], F32, "dcol")
        for q in range(4):
            dma(dcol[:, q:q + 1], W["s5_D"][q * 128:(q + 1) * 128].rearrange("(p o) -> p o", o=1), w=[dcolK])
        diagD, diagDK = sb([128, 4, 128], BF16, "diagD")
        for q in range(4):
            ts("dve", diagD[:, q, :], ident[:, :], dcol[:, q:q + 1], None, ALU.mult, None, [identK, dcolK], [diagDK])
        psw, pswK = C["psw"]
        rho, rhoK = sm["rho"]
        RC, RCK = sb([128, 32], F32, "RC"); RS, RSK = sb([128, 32], F32, "RS")
        tt("dve", RC[:], COS[:, :, 63], rho[:], ALU.mult, [COSK, rhoK], [RCK])
        tt("dve", RS[:], SIN[:, :, 63], rho[:], ALU.mult, [SINK, rhoK], [RSK])
        RHOT, RHOTK = sb([128, 32, 64], F32, "RHOT")
        tt("dve", RHOT[:], rho[:].unsqueeze(2).to_broadcast([128, 32, 64]), C["m64"][0][:].unsqueeze(1).to_broadcast([128, 32, 64]), ALU.mult, [rhoK, C["m64"][1]], [RHOTK])

        NST = 1 + NS
        Sst = [sb([128, 4, 128], F32, f"Sst{i}") for i in range(NST)]
        Sbf = [sb([128, 4, 128], BF16, f"Sbf{i}") for i in range(NST)]
        XS = [sb([128, 32], F32, f"XS{i}") for i in range(NST)]
        op("pool", lambda e: e.memset(Sst[0][0][:], 0.0), w=[Sst[0][1]])
        op("pool", lambda e: e.memset(Sbf[0][0][:], 0.0), w=[Sbf[0][1]])
        XS0K = [f"XS0_{e}" for e in range(8)]
        op("pool", lambda e: e.memset(XS[0][0][:], 0.0), w=XS0K)
        xs2, xs2K = sb([32, 128], F32, "xs2")
        for s in range(NS):
            dma(Sst[1 + s][0][:], sdn[s].rearrange("h d e -> d h e"), w=[Sst[1 + s][1]])
            cp("pool", Sbf[1 + s][0][:], Sst[1 + s][0][:], [Sst[1 + s][1]], [Sbf[1 + s][1]])
            dma(xs2[:, 0:64], sre[s], w=[xs2K]); dma(xs2[:, 64:128], sim[s], w=[xs2K])
            bk, bkK = bank()
            tr(bk[:, 0:32], xs2[0:32, :], ident[0:32, 0:32], [xs2K, identK], [bkK])
            cp("dve", XS[1 + s][0][:], bk[:, 0:32], [bkK], [XS[1 + s][1]])

        sc = {}
        for nm in ["ss", "rstd", "ss2", "r2"]:
            sc[nm] = sb([128, 1], F32, "sc_" + nm)
        hb, hbK = sb([128, 1024], BF16, "hb")
        MIX, MIXK = KI[:].bitcast(BF16), KIK
        hT, hTK = sb([128, 8, 128], BF16, "hT")
        MIXT, MIXTK = sb([128, 8, 128], BF16, "MIXT")
        RAWp, RAWpK = sb([128, 12, 1, 131], F32, "RAWp")
        RAWs = RAWp[:].rearrange("p a b c -> p (a b c)")[:, 0:12 * NS * 35].rearrange("p (a b c) -> p a b c", a=12, b=NS)
        RAWsK = RAWpK
        for s in range(NS):
            for t in range(12):
                dma(RAWs[:, t, s, 0:3], cconv[s, :, t * 128:(t + 1) * 128].rearrange("r c -> c r"), w=[RAWsK], slow=True)
        QKVs = [sb([128, 12, 128], BF16, f"QKV{i}") for i in range(2)]
        uTs = [sb([128, 4, 128], BF16, f"uT{i}") for i in range(2)]
        GT8s = [sb([128, 8], F32, f"GT8_{i}") for i in range(2)]
        SZDs = [sb([128, 512], BF16, f"SZD{i}") for i in range(2)]
        SZSs = [sb([128, 512], BF16, f"SZS{i}") for i in range(2)]
        g4 = {}
        for nm in ["GT8", "BETA", "LNB", "SQB", "RSB", "XA", "G", "GC", "GD", "EGC", "EDEC", "SSK", "RK", "S1", "S2", "S3", "SSO", "RO"]:
            g4[nm] = sb([128, 8 if nm == "GT8" else 4], F32, "g_" + nm)
        EGL, EGLK = sb([128, 2, 4], F32, "EGL")
        Rg, RgK = v4(TA), TAK
        RQ, RQK = v4(TA), TAK
        OT, OTK = v4(TA), TAK
        EU, EUK = v4(TB), TBK
        X32, X32K = v4(TB), TBK
        Y5, Y5K = TB, TBK
        EL, ELK = v4(TC), TCK
        Z32, Z32K = v4(TC), TCK
        G1, G1K = TC, TCK
        EUI, EUIK = v4(TD), TDK
        OO, OOK = v4(TD), TDK
        Mt, MtK = v4(TE), TEK
        G2, G2K = TE, TEK
        VT, VTK = v4(TF), TFK
        GY, GYK = TF, TFK
        QHT, QHTK = sb([128, 4, 128], BF16, "QHT")
        KTk, KTkK = sb([128, 4, 128], BF16, "KTk")
        QSQ, QSQK = KTk, KTkK
        KG, KGK = sb([128, 4, 128], BF16, "KG"); KDEC, KDECK = sb([128, 4, 128], BF16, "KDEC")
        KTT, KTTK = sb([128, 4, 128], BF16, "KTT"); KGT, KGTK = sb([128, 4, 128], BF16, "KGT")
        Xb, XbK = sb([128, 4, 128], BF16, "Xb"); Mtb, MtbK = sb([128, 4, 128], BF16, "Mtb")
        Zb, ZbK = sb([128, 4, 128], BF16, "Zb")
        XF, XFK = sb([128, 4, 128], BF16, "XF")
        ATT, ATTK = sb([128, 4, 128], BF16, "ATT")
        Rr, RrK = sb([128, 4, 128], BF16, "Rr"); VN, VNK = sb([128, 4, 128], BF16, "VN")
        GYb, GYbK = Rr[:].rearrange("p h e -> p (h e)"), RrK
        GYT, GYTK = VN, VNK
        Ab, AbK = sb([128, 8, 128], BF16, "Ab"); Bb, BbK = sb([128, 8, 128], BF16, "Bb")
        AL, ALK = sb([128, 8], F32, "AL"); BL, BLK = sb([128, 8], F32, "BL")
        AL2, _ = sb([128, 4, 4], F32, "AL2"); BL2, _ = sb([128, 4, 4], F32, "BL2"); FX2, _ = sb([128, 4, 4], F32, "FX2")
        xso, xsoK = XN[0:32, 0:128], XNK
        def rsqrt_chain(outk, ink, scale, eps):
            o, oK = outk
            i_, iK = ink
            ts("dve", o, i_, scale, eps, ALU.mult, ALU.add, [iK], [oK])
            act(o, o, AF.Ln, [oK], [oK])
            act(o, o, AF.Exp, [oK], [oK], scale=-0.5)

        def do_tile(kind, ti, xsrc, ydst, mode="full", b=0, part="all"):
            if part == "tail":
                tile_tail(kind, ti, ydst, b, 128 if kind == "p" else NS * 32)
                return None
            full = mode == "full"
            needq = mode != "state"
            QKV, QKVK = QKVs[b]
            uT, uTK = uTs[b]
            SZD, SZDK = SZDs[b]
            SZS, SZSK = SZSs[b]
            if kind == "p":
                TT, Cc, nseg, L, RAW, RAWK = 128, 64, 1, 128, RAWp, RAWpK
                chunks = [(0, 0), (1, 0)]
                segs = [(0, 0), (1, 0)]
                SL = 64
            else:
                TT, Cc, nseg, L, RAW, RAWK = NS * 32, 32, NS, 32, RAWs, RAWsK
                chunks = [(s, 1 + s) for s in range(NS)]
                segs = [(s, 1 + s) for s in range(NS)]
                SL = 32
            nsg = TT // SL
            cn = lambda n: C[n + "_" + kind]
            m4 = lambda n: cn(n)[0][:TT, :TT].unsqueeze(1).to_broadcast([TT, 4, TT])
            x, xK = xt, xtK
            if part in ("all", "front"):
                x, xK = xt, xtK
                dma(x[:TT, :], xsrc, w=[xK])
                ss, ssK = sc["ss"]; rstd, rstdK = sc["rstd"]
                act(hb[:TT, :], x[:TT, :], AF.Square, [xK], [hbK, ssK], accum_out=ss[:TT, :])
                rsqrt_chain((rstd[:TT, :], rstdK), (ss[:TT, :], ssK), 1.0 / 1024, 1e-6)
                ts("dve", hb[:TT, :], x[:TT, :], rstd[:TT, 0:1], None, ALU.mult, None, [xK, rstdK], [hbK])
                pb, pbK = bbank()
                pbv = pb[:].rearrange("p (k t) -> p k t", k=8)
                for k in range(8):
                    tr(pbv[:, k, :TT], hb[:TT, k * 128:(k + 1) * 128], identb[:TT, :TT], [hbK, identbK], [pbK])
                cp("act", hT[:, :, :TT], pbv[:, :, :TT], [pbK], [hTK])
                for grp in range(4):
                    if grp == 0 and not needq:
                        continue
                    bk, bkK = bank()
                    bv = bk[:].rearrange("p (j t) -> p j t", j=4)
                    c0 = grp * 512 if grp < 3 else 2056
                    for j in range(4):
                        for k in range(8):
                            mm(bv[:, j, :TT], winb[:, k, c0 + j * 128:c0 + (j + 1) * 128], hT[:, k, :TT], k == 0, k == 7, [winbK, hTK], [bkK])
                    if grp < 3:
                        for s in range(nseg):
                            cp("act", RAW[:, grp * 4:(grp + 1) * 4, s, 3:3 + L], bv[:, :, s * L:(s + 1) * L], [bkK], [RAWK])
                    else:
                        cp("act", uT[:, :, :TT], bv[:, :, :TT], [bkK], [uTK])
                if full:
                    bk, bkK = bank()
                    for k in range(8):
                        mm(bk[:TT, :], hT[:, k, :TT], winb[:, k, 1544:2056], k == 0, k == 7, [winbK, hTK], [bkK])
                    act(SZD[:TT, :], bk[:TT, :], AF.Silu, [bkK], [SZDK])
                    bk, bkK = bank()
                    for k in range(8):
                        mm(bk[:TT, :], hT[:, k, :TT], winb[:, k, 2568:3080], k == 0, k == 7, [winbK, hTK], [bkK])
                    act(SZS[:TT, :], bk[:TT, :], AF.Silu, [bkK], [SZSK])
                bk, bkK = bank()
                for k in range(8):
                    mm(bk[:TT, 0:8], hT[:, k, :TT], winb[:, k, 1536:1544], k == 0, k == 7, [winbK, hTK], [bkK])
                GT8, GT8K = GT8s[b]
                cp("dve", GT8[:TT, :], bk[:TT, 0:8], [bkK], [GT8K])
                for t3 in range(3):
                    if t3 == 0 and not full:
                        continue
                    for t4 in range(4):
                        t = t3 * 4 + t4
                        av = acc[:, t4, :TT].rearrange("p (s l) -> p s l", s=nseg)
                        ts("dve", av, RAW[:, t, :, 0:L], CW[:, t, 0:1], None, ALU.mult, None, [RAWK, CWK], [accKs[t4]])
                        for j in range(1, 4):
                            stt("dve", av, RAW[:, t, :, j:j + L], CW[:, t, j:j + 1], av, ALU.mult, ALU.add, [RAWK, CWK, accKs[t4]], [accKs[t4]])
                    act(QKV[:, t3 * 4:(t3 + 1) * 4, :TT], acc[:, :, :TT], AF.Silu, accKs, [QKVK])
                if kind == "p":
                    cp("pool", RAW[:, :, 0, 0:3], RAW[:, :, 0, L:L + 3], [RAWK], [RAWK])
            if part == "front":
                return None
            if full:
                cp("pool", XN[:TT, :], x[:TT, :], [xK], [XNK])
            GT8, GT8K = GT8s[b]
            stG = begin_stream([0, 1], bb=1)
            G_ = lambda n: g4[n][0][:TT, :]
            GK = lambda n: g4[n][1]
            act(G_("BETA"), GT8[:TT, 4:8], AF.Exp, [GT8K], [GK("BETA")], scale=-1.0)
            act(G_("LNB"), G_("BETA"), AF.Ln, [GK("BETA")], [GK("LNB")], bias=1.0)
            act(G_("SQB"), G_("LNB"), AF.Exp, [GK("LNB")], [GK("SQB")], scale=-0.5)
            act(G_("RSB"), G_("LNB"), AF.Exp, [GK("LNB")], [GK("RSB")], scale=0.5)
            tt("dve", G_("XA"), GT8[:TT, 0:4], dtb[:TT, :], ALU.add, [GT8K, dtbK], [GK("XA")])
            act(G_("XA"), G_("XA"), AF.Exp, [GK("XA")], [GK("XA")])
            act(G_("XA"), G_("XA"), AF.Ln, [GK("XA")], [GK("XA")], bias=1.0)
            tt("dve", G_("G"), G_("XA"), negA[:TT, :], ALU.mult, [GK("XA"), negAK], [GK("G")])
            bk, bkK = bank()
            mm(bk[:TT, 0:4], cn("tri")[0][:TT, :TT], G_("G"), True, True, [cn("tri")[1], GK("G")], [bkK])
            mm(bk[:TT, 4:8], cn("blk")[0][:TT, :TT], G_("G"), True, True, [cn("blk")[1], GK("G")], [bkK])
            nch = len(chunks)
            for c in range(nch):
                mm(bk[:, 8 + 4 * c:12 + 4 * c], cn("ind")[0][:TT, c, :], G_("G"), True, True, [cn("ind")[1], GK("G")], [bkK])
            cp("dve", G_("GC"), bk[:TT, 0:4], [bkK], [GK("GC")])
            tt("dve", G_("GD"), bk[:TT, 4:8], G_("GC"), ALU.subtract, [bkK, GK("GC")], [GK("GD")])
            act(G_("EGC"), G_("GC"), AF.Exp, [GK("GC")], [GK("EGC")])
            act(G_("EDEC"), G_("GD"), AF.Exp, [GK("GD")], [GK("EDEC")])
            act(EGL[:, 0:nch, :], bk[:, 8:8 + 4 * nch].rearrange("p (c h) -> p c h", c=nch), AF.Exp, [bkK], [EGLK])
            tt("dve", Rg[:TT, :, :TT], m4("tri"), G_("G").unsqueeze(2).to_broadcast([TT, 4, TT]), ALU.mult, [cn("tri")[1], GK("G")], [RgK])
            bu, buK = bank()
            bl, blK = bank()
            buv = bu[:].rearrange("p (h t) -> p h t", h=4)
            blv = bl[:].rearrange("p (h t) -> p h t", h=4)
            lst, lstK = C["lst"]
            for h in range(4):
                mm(buv[:TT, h, :TT], lst[:TT, :TT], Rg[:TT, h, :TT], True, True, [lstK, RgK], [buK])
                mm(blv[:TT, h, :TT], Rg[:TT, h, :TT], lst[:TT, :TT], True, True, [lstK, RgK], [blK])
            act(EU[:TT, :, :TT], buv[:TT, :, :TT], AF.Exp, [buK], [EUK])
            act(EL[:TT, :, :TT], blv[:TT, :, :TT], AF.Exp, [blK], [ELK])
            if full:
                tt("pool", EUI[:TT, :, :TT], EU[:TT, :, :TT], m4("ui"), ALU.mult, [EUK, cn("ui")[1]], [EUIK])
            tt("pool", EU[:TT, :, :TT], EU[:TT, :, :TT], m4("us"), ALU.mult, [EUK, cn("us")[1]], [EUK])
            tt("pool", EL[:TT, :, :TT], EL[:TT, :, :TT], m4("ls"), ALU.mult, [ELK, cn("ls")[1]], [ELK])
            if full:
                act(QSQ[:, :, :TT], QKV[:, 0:4, :TT], AF.Square, [QKVK], [QSQK])
                bk, bkK = bank()
                bv = bk[:].rearrange("p (h t) -> p h t", h=4)
                for h in range(4):
                    mm(bv[:, h, :TT], onesb[:, :], QSQ[:, h, :TT], True, True, [onesbK, QSQK], [bkK])
                ts("dve", RQ[:, :, :TT], bv[:, :, :TT], 1e-6, None, ALU.add, None, [bkK], [RQK])
                act(RQ[:, :, :TT], RQ[:, :, :TT], AF.Ln, [RQK], [RQK])
                act(RQ[:, :, :TT], RQ[:, :, :TT], AF.Exp, [RQK], [RQK], scale=-0.5)
                stt("dve", QHT[:, :, :TT], QKV[:, 0:4, :TT], 128.0 ** -0.5, RQ[:, :, :TT], ALU.mult, ALU.mult, [QKVK, RQK], [QHTK])
            pk, pkK = bbank()
            pkv = pk[:].rearrange("p (h d) -> p h d", h=8)
            for h in range(4):
                tr(pkv[:TT, h, :], QKV[:, 4 + h, :TT], identb[:, :], [QKVK, identbK], [pkK])
                tr(pkv[:TT, 4 + h, :], QKV[:, 8 + h, :TT], identb[:, :], [QKVK, identbK], [pkK])
            SSK, SSKK = g4["SSK"]
            for h in range(4):
                act(Rr[:TT, 0, :], pkv[:TT, h, :], AF.Square, [pkK], [RrK, SSKK], accum_out=SSK[:TT, h:h + 1])
            rsqrt_chain((G_("RK"), GK("RK")), (G_("SSK"), SSKK), 1.0, 1e-6)
            tt("dve", G_("S1"), G_("RK"), G_("SQB"), ALU.mult, [GK("RK"), GK("SQB")], [GK("S1")])
            tt("dve", G_("S2"), G_("S1"), G_("EGC"), ALU.mult, [GK("S1"), GK("EGC")], [GK("S2")])
            tt("dve", G_("S3"), G_("RK"), G_("EDEC"), ALU.mult, [GK("RK"), GK("EDEC")], [GK("S3")])
            b4 = lambda n: g4[n][0][:TT, :].unsqueeze(2).to_broadcast([TT, 4, 128])
            tt("dve", KTk[:TT], pkv[:TT, 0:4, :], b4("S1"), ALU.mult, [pkK, GK("S1")], [KTkK])
            tt("dve", KG[:TT], pkv[:TT, 0:4, :], b4("S2"), ALU.mult, [pkK, GK("S2")], [KGK])
            tt("dve", KDEC[:TT], pkv[:TT, 0:4, :], b4("S3"), ALU.mult, [pkK, GK("S3")], [KDECK])
            tt("dve", VT[:TT], pkv[:TT, 4:8, :], b4("SQB"), ALU.mult, [pkK, GK("SQB")], [VTK])
            pk2, pk2K = bbank()
            pk2v = pk2[:].rearrange("p (h t) -> p h t", h=8)
            for h in range(4):
                tr(pk2v[:, h, :TT], KTk[:TT, h, :], identb[:TT, :TT], [KTkK, identbK], [pk2K])
                tr(pk2v[:, 4 + h, :TT], KG[:TT, h, :], identb[:TT, :TT], [KGK, identbK], [pk2K])
            cp("act", KTT[:, :, :TT], pk2v[:, 0:4, :TT], [pk2K], [KTTK])
            cp("act", KGT[:, :, :TT], pk2v[:, 4:8, :TT], [pk2K], [KGTK])
            bkk, bkkK = bank()
            bqk, bqkK = bank()
            kkv = bkk[:].rearrange("p (h t) -> p h t", h=4)
            qkv_ = bqk[:].rearrange("p (h t) -> p h t", h=4)
            for h in range(4):
                mm(kkv[:TT, h, :TT], KTT[:, h, :TT], KTT[:, h, :TT], True, True, [KTTK], [bkkK])
                if full:
                    mm(qkv_[:TT, h, :TT], KTT[:, h, :TT], QHT[:, h, :TT], True, True, [KTTK, QHTK], [bqkK])
            tt("dve", Mt[:TT, :, :TT], kkv[:TT, :, :TT], EL[:TT, :, :TT], ALU.mult, [bkkK, ELK], [MtK])
            tt("pool", Mt[:TT, :, :TT], Mt[:TT, :, :TT], m4("id"), ALU.add, [MtK, cn("id")[1]], [MtK])
            cp("pool", Mtb[:TT, :, :TT], Mt[:TT, :, :TT], [MtK], [MtbK])
            tt("dve", EU[:TT, :, :TT], kkv[:TT, :, :TT], EU[:TT, :, :TT], ALU.mult, [bkkK, EUK], [EUK])
            tt("pool", Xb[:TT, :, :TT], m4("id"), EU[:TT, :, :TT], ALU.subtract, [cn("id")[1], EUK], [XbK])
            RSB, RSBK = g4["RSB"]
            for h in range(4 if full else 0):
                stt("dve", ATT[:TT, h, :TT], qkv_[:TT, h, :TT], RSB[:TT, h:h + 1], EUI[:TT, h, :TT], ALU.mult, ALU.mult, [bqkK, RSBK, EUIK], [ATTK])
            for it in range(5):
                last = it == 4
                Xo, XoK, Mo, MoK = (X32, X32K, Mt, MtK) if last else (Xb, XbK, Mtb, MtbK)
                Zo, ZoK = (Z32, Z32K) if last else (Zb, ZbK)
                by, byK = bank()
                byv = by[:].rearrange("p (h t) -> p h t", h=4)
                for h in range(4):
                    mm(byv[:TT, h, :TT], Xo[:TT, h, :TT], Mo[:TT, h, :TT], True, True, [XoK, MoK], [byK])
                stt("dve", Zo[:TT, :, :TT], byv[:TT, :, :TT], -1.0, cn("tw")[0][:TT, :TT].unsqueeze(1).to_broadcast([TT, 4, TT]), ALU.mult, ALU.add, [byK, cn("tw")[1]], [ZoK])
                bx, bxK = bank()
                bxv = bx[:].rearrange("p (h t) -> p h t", h=4)
                for h in range(4):
                    mm(bxv[:TT, h, :TT], Zo[:TT, h, :TT], Xo[:TT, h, :TT], True, True, [ZoK, XoK], [bxK])
                if it < 3:
                    cp("act", Xb[:TT, :, :TT], bxv[:TT, :, :TT], [bxK], [XbK])
                elif it == 3:
                    cp("act", X32[:TT, :, :TT], bxv[:TT, :, :TT], [bxK], [X32K])
                else:
                    cp("act", XF[:TT, :, :TT], bxv[:TT, :, :TT], [bxK], [XFK])
            for (c, st) in chunks:
                r0, r1 = c * Cc, (c + 1) * Cc
                Sm, SmK = Sst[st]
                Sb_, SbK = Sbf[st]
                ba, baK = bank(); bq, bqK = bank()
                bav = ba[:].rearrange("p (h e) -> p h e", h=4)
                bqv = bq[:].rearrange("p (h e) -> p h e", h=4)
                for h in range(4):
                    mm(bav[:TT, h, :], KGT[:, h, :TT], Sb_[:, h, :], True, True, [KGTK, SbK], [baK])
                    if full:
                        mm(bqv[:TT, h, :], QHT[:, h, :TT], Sb_[:, h, :], True, True, [QHTK, SbK], [bqK])
                tt("dve", Rr[r0:r1], VT[r0:r1], bav[r0:r1], ALU.subtract, [VTK, baK], [RrK])
                if full:
                    tt("dve", OO[r0:r1], bqv[r0:r1], g4["EGC"][0][r0:r1, :].unsqueeze(2).to_broadcast([Cc, 4, 128]), ALU.mult, [bqK, GK("EGC")], [OOK])
                bc_, bcK = bank()
                bcv = bc_[:].rearrange("p (h e) -> p h e", h=4)
                for h in range(4):
                    mm(bcv[:TT, h, :], XF[r0:r1, h, :TT], Rr[r0:r1, h, :], True, True, [XFK, RrK], [bcK])
                tt("dve", VN[r0:r1], bcv[r0:r1], g4["SQB"][0][r0:r1, :].unsqueeze(2).to_broadcast([Cc, 4, 128]), ALU.mult, [bcK, GK("SQB")], [VNK])
                bd, bdK = bank(); be, beK = bank()
                bdv = bd[:].rearrange("p (h e) -> p h e", h=4)
                bev = be[:].rearrange("p (h e) -> p h e", h=4)
                for h in range(4):
                    if full:
                        mm(bdv[:TT, h, :], ATT[r0:r1, h, :TT], VN[r0:r1, h, :], True, True, [ATTK, VNK], [bdK])
                    mm(bev[:, h, :], KDEC[r0:r1, h, :], VN[r0:r1, h, :], True, True, [KDECK, VNK], [beK])
                if full:
                    tt("dve", OO[r0:r1], OO[r0:r1], bdv[r0:r1], ALU.add, [OOK, bdK], [OOK])
                tt("dve", Sm[:], Sm[:], EGL[:, c, :].unsqueeze(2).to_broadcast([128, 4, 128]), ALU.mult, [SmK, EGLK], [SmK])
                tt("dve", Sm[:], Sm[:], bev[:, :, :], ALU.add, [SmK, beK], [SmK])
                cp("act", Sb_[:], Sm[:], [SmK], [SbK])
            if full:
                SSO, SSOK = g4["SSO"]
                for h in range(4):
                    act(Rr[:TT, 0, :], OO[:TT, h, :], AF.Square, [OOK], [RrK, SSOK], accum_out=SSO[:TT, h:h + 1])
                rsqrt_chain((G_("RO"), GK("RO")), (G_("SSO"), SSOK), 1.0 / 128, 1e-6)
                tt("dve", OO[:TT], OO[:TT], b4("RO"), ALU.mult, [OOK, GK("RO")], [OOK])
                tt("pool", OO[:TT], OO[:TT], dnwb[:TT, :].unsqueeze(1).to_broadcast([TT, 4, 128]), ALU.mult, [OOK, dnwbK], [OOK])
                tt("dve", MIX[:TT, 0:512].rearrange("p (h e) -> p h e", h=4), OO[:TT], SZD[:TT, :].rearrange("p (h e) -> p h e", h=4), ALU.mult, [OOK, SZDK], [MIXK])
            end_stream()
            stS = begin_stream([2, 3])
            by5, by5K = y5bank
            PAKs = [PAK + "0", PAK + "1"]; PBKs = [PBK + "0", PBK + "1"]
            AbKs = [AbK + "0", AbK + "1"]; BbKs = [BbK + "0", BbK + "1"]
            if kind == "p":
                PAf = PA[:].rearrange("p g t -> p (g t)"); PBf = PB[:].rearrange("p g t -> p (g t)")
                xs_ = XS[0][0]

                def s1(e):
                    i, q, hf, gs = e % 2, e // 2, e % 2, 4 * e
                    hs = slice(64 * hf, 64 * hf + 64)
                    b1, b1K = bank(); b2, b2K = bank()
                    b1v = b1[:].rearrange("p (g t) -> p g t", g=4)
                    b2v = b2[:].rearrange("p (g t) -> p g t", g=4)
                    for j in range(4):
                        mm(b1v[:, j, :], BBn[hs, q, j, :], uT[hs, q, :], True, True, [BBnK, uTK], [b1K])
                        mm(b2v[:, j, :], BBs[hs, q, j, :], uT[hs, q, :], True, True, [BBsK, uTK], [b2K])
                    pa = PAf[:, i * 512:(i + 1) * 512].rearrange("p (h g t) -> p g h t", h=2, g=4)
                    pb = PBf[:, i * 512:(i + 1) * 512].rearrange("p (h g t) -> p g h t", h=2, g=4)
                    cosb = COS[:, gs:gs + 4, :].unsqueeze(2).to_broadcast([128, 4, 2, 64])
                    sinb = SIN[:, gs:gs + 4, :].unsqueeze(2).to_broadcast([128, 4, 2, 64])
                    tt("dve", pa, b1v[:, :, :].rearrange("p g (h t) -> p g h t", h=2), cosb, ALU.mult, [b1K, COSK], [PAKs[i]])
                    tt("dve", pb, b2v[:, :, :].rearrange("p g (h t) -> p g h t", h=2), sinb, ALU.mult, [b2K, SINK], [PBKs[i]])
                    tt("pool", PAf[:, i * 512:(i + 1) * 512], PAf[:, i * 512:(i + 1) * 512], PBf[:, i * 512:(i + 1) * 512], ALU.add, [PAKs[i], PBKs[i]], [PAKs[i]])

                def s2(e, h):
                    i, gs = e % 2, 4 * e
                    k2 = (e % 2) * 2 + h
                    kk = f"s5sm{k2}"
                    pa3 = PAf[:, i * 512:(i + 1) * 512].rearrange("p (h g t) -> p h g t", h=2, g=4)
                    pb3 = PBf[:, i * 512:(i + 1) * 512].rearrange("p (h g t) -> p h g t", h=2, g=4)
                    if h == 0:
                        tt("dve", pa3[:, h, :, 0], pa3[:, h, :, 0], xs_[:, gs:gs + 4], ALU.add, [PAKs[i], XS0K[e]], [PAKs[i]])
                    c0 = i * 512 + h * 256
                    op("dve", lambda en, c0=c0, gs=gs: en.tensor_tensor_scan(
                        out=PBf[:, c0:c0 + 256], data0=RHOT[:, gs:gs + 4, :].rearrange("p g t -> p (g t)"),
                        data1=PAf[:, c0:c0 + 256], initial=0.0, op0=ALU.mult, op1=ALU.add),
                       r=[PAKs[i], RHOTK], w=[PBKs[i]])
                    tt("dve", AL2[:, k2, :], pb3[:, h, :, 63], RC[:, gs:gs + 4], ALU.mult, [PBKs[i], RCK], [kk + "a"])
                    tt("dve", BL2[:, k2, :], pb3[:, h, :, 63], RS[:, gs:gs + 4], ALU.mult, [PBKs[i], RSK], [kk + "b"])
                    bk, bkK = bank()
                    mm(bk[:, 0:4], psw[:, :], BL2[:, k2, :], True, True, [pswK, kk + "b"], [bkK])
                    if h == 0:
                        tt("dve", pa3[:, 1, :, 0], pa3[:, 1, :, 0], AL2[:, k2, :], ALU.add, [PAKs[i], kk + "a"], [PAKs[i]])
                        tt("dve", pa3[:, 1, :, 0], pa3[:, 1, :, 0], bk[:, 0:4], ALU.add, [PAKs[i], bkK], [PAKs[i]])
                    else:
                        tt("dve", xs_[:, gs:gs + 4], bk[:, 0:4], AL2[:, k2, :], ALU.add, [bkK, kk + "a"], [XS0K[e]])

                def s3(e):
                    i, q, gs = e % 2, e // 2, 4 * e
                    pb = PBf[:, i * 512:(i + 1) * 512].rearrange("p (h g t) -> p g h t", h=2, g=4)
                    cosb = COS[:, gs:gs + 4, :].unsqueeze(2).to_broadcast([128, 4, 2, 64])
                    sinb = SIN[:, gs:gs + 4, :].unsqueeze(2).to_broadcast([128, 4, 2, 64])
                    ab = Ab[:, 4 * i:4 * i + 4, :].rearrange("p g (h t) -> p g h t", h=2)
                    bbv = Bb[:, 4 * i:4 * i + 4, :].rearrange("p g (h t) -> p g h t", h=2)
                    tt("pool", ab, pb, cosb, ALU.mult, [PBKs[i], COSK], [AbKs[i]])
                    tt("dve", bbv, pb, sinb, ALU.mult, [PBKs[i], SINK], [BbKs[i]])
                    for g4_ in range(4):
                        g = gs + g4_
                        g8 = g % 8
                        mm(by5[:TT, g * 16:(g + 1) * 16], uT[:, q, :TT], diagD[:, q, g8 * 16:(g8 + 1) * 16], True, False, [uTK, diagDK], [by5K])
                        mm(by5[:TT, g * 16:(g + 1) * 16], Ab[:, 4 * i + g4_, :TT], Md1[:, g * 16:(g + 1) * 16], False, False, [AbKs[i], Md1K], [by5K])
                        mm(by5[:TT, g * 16:(g + 1) * 16], Bb[:, 4 * i + g4_, :TT], Md2[:, g * 16:(g + 1) * 16], False, True, [BbKs[i], Md2K], [by5K])

                s1(0)
                for e in range(8):
                    s2(e, 0)
                    if e + 1 < 8:
                        s1(e + 1)
                    s2(e, 1)
                    if full:
                        s3(e)
            else:
                for q in range(4):
                    PAv = PA[:, :, :TT].rearrange("p g (s l) -> p g s l", s=nsg)
                    PBv = PB[:, :, :TT].rearrange("p g (s l) -> p g s l", s=nsg)
                    for half in range(2):
                        b1, b1K = bank(); b2, b2K = bank()
                        b1v = b1[:].rearrange("p (g t) -> p g t", g=4)
                        b2v = b2[:].rearrange("p (g t) -> p g t", g=4)
                        hs = slice(64 * half, 64 * half + 64)
                        for j in range(4):
                            mm(b1v[:, j, :TT], BBn[hs, q, j, :], uT[hs, q, :TT], True, True, [BBnK, uTK], [b1K])
                            mm(b2v[:, j, :TT], BBs[hs, q, j, :], uT[hs, q, :TT], True, True, [BBsK, uTK], [b2K])
                        gs = 8 * q + 4 * half
                        cosb = COS[:, gs:gs + 4, 0:SL].unsqueeze(2).to_broadcast([128, 4, nsg, SL])
                        sinb = SIN[:, gs:gs + 4, 0:SL].unsqueeze(2).to_broadcast([128, 4, nsg, SL])
                        tt("dve", PAv[:, 4 * half:4 * half + 4], b1v[:, :, :TT].rearrange("p g (s l) -> p g s l", s=nsg), cosb, ALU.mult, [b1K, COSK], PAKs)
                        tt("dve", PBv[:, 4 * half:4 * half + 4], b2v[:, :, :TT].rearrange("p g (s l) -> p g s l", s=nsg), sinb, ALU.mult, [b2K, SINK], PBKs)
                    tt("pool", PA[:, :, :TT], PA[:, :, :TT], PB[:, :, :TT], ALU.add, PAKs + PBKs, PAKs)
                    for (s, st) in segs:
                        xs_, xsK = XS[st]
                        for g in range(8):
                            op("dve", lambda e, g=g, s=s, xs_=xs_, q=q: e.tensor_tensor_scan(
                                out=PB[:, g, s * SL:(s + 1) * SL], data0=rho[:, 8 * q + g:8 * q + g + 1].to_broadcast([128, SL]),
                                data1=PA[:, g, s * SL:(s + 1) * SL], initial=xs_[:, 8 * q + g:8 * q + g + 1], op0=ALU.mult, op1=ALU.add),
                               r=PAKs + [rhoK, xsK], w=PBKs)
                        e_ = (s + 1) * SL - 1
                        tt("dve", AL[:], PB[:, :, e_], COS[:, 8 * q:8 * q + 8, SL - 1], ALU.mult, PBKs + [COSK], [ALK])
                        tt("dve", BL[:], PB[:, :, e_], SIN[:, 8 * q:8 * q + 8, SL - 1], ALU.mult, PBKs + [SINK], [BLK])
                        bk, bkK = bank()
                        mm(bk[:, 0:8], ident[:, :], AL[:], True, False, [identK, ALK], [bkK])
                        mm(bk[:, 0:8], psw[:, :], BL[:], False, True, [pswK, BLK], [bkK])
                        cp("act", xs_[:, 8 * q:8 * q + 8], bk[:, 0:8], [bkK], [xsK])
                    if not full:
                        continue
                    cosf = COS[:, 8 * q:8 * q + 8, 0:SL].unsqueeze(2).to_broadcast([128, 8, nsg, SL])
                    sinf = SIN[:, 8 * q:8 * q + 8, 0:SL].unsqueeze(2).to_broadcast([128, 8, nsg, SL])
                    tt("pool", Ab[:, :, :TT].rearrange("p g (s l) -> p g s l", s=nsg), PBv, cosf, ALU.mult, PBKs + [COSK], AbKs)
                    tt("dve", Bb[:, :, :TT].rearrange("p g (s l) -> p g s l", s=nsg), PBv, sinf, ALU.mult, PBKs + [SINK], BbKs)
                    for g8 in range(8):
                        g = 8 * q + g8
                        mm(by5[:TT, g * 16:(g + 1) * 16], uT[:, q, :TT], diagD[:, q, g8 * 16:(g8 + 1) * 16], True, False, [uTK, diagDK], [by5K])
                        mm(by5[:TT, g * 16:(g + 1) * 16], Ab[:, g8, :TT], Md1[:, g * 16:(g + 1) * 16], False, False, AbKs + [Md1K], [by5K])
                        mm(by5[:TT, g * 16:(g + 1) * 16], Bb[:, g8, :TT], Md2[:, g * 16:(g + 1) * 16], False, True, BbKs + [Md2K], [by5K])
            end_stream()
            if part == "rest_nomerge":
                return [stG, stS]
            merge_streams([stG, stS])
            if not full:
                return None
            tile_tail(kind, ti, ydst, b, TT)
            return None

        def tile_tail(kind, ti, ydst, b, TT):
            x, xK = xt, xtK
            SZS, SZSK = SZSs[b]
            by5, by5K = y5bank
            cp("act", Y5[:TT, :], by5[:TT, :], [by5K], [Y5K])
            tt("dve", G1[:TT, :], Y5[:TT, :], Y5[:TT, :], ALU.mult, [Y5K], [G1K])
            ts("dve", G1[:TT, :], G1[:TT, :], 0.044715, 1.0, ALU.mult, ALU.add, [G1K], [G1K])
            tt("dve", G1[:TT, :], G1[:TT, :], Y5[:TT, :], ALU.mult, [G1K, Y5K], [G1K])
            act(G2[:TT, :], G1[:TT, :], AF.Sigmoid, [G1K], [G2K], scale=2.0 * math.sqrt(2.0 / PI))
            tt("dve", GY[:TT, :], Y5[:TT, :], G2[:TT, :], ALU.mult, [Y5K, G2K], [GYK])
            cp("pool", GYb[:TT, :], GY[:TT, :], [GYK], [GYbK])
            pg, pgK = bbank()
            pgv = pg[:].rearrange("p (k t) -> p k t", k=8)
            for q in range(4):
                tr(pgv[:, q, :TT], GYb[:TT, q * 128:(q + 1) * 128], identb[:TT, :TT], [GYbK, identbK], [pgK])
            cp("act", GYT[:, :, :TT], pgv[:, 0:4, :TT], [pgK], [GYTK])
            bg, bgK = bank()
            for q in range(4):
                mm(bg[:TT, :], GYT[:, q, :TT], glub[:, q, :], q == 0, q == 3, [GYTK, glubK], [bgK])
            tt("dve", G1[:TT, :], bg[:TT, :], glbb[:TT, :], ALU.add, [bgK, glbbK], [G1K])
            act(G2[:TT, :], G1[:TT, :], AF.Sigmoid, [G1K], [G2K])
            tt("dve", G1[:TT, :], GY[:TT, :], G2[:TT, :], ALU.mult, [GYK, G2K], [G1K])
            tt("dve", MIX[:TT, 512:1024], G1[:TT, :], SZS[:TT, :], ALU.mult, [G1K, SZSK], [MIXK])
            if dbg and kind == "p" and ti == TP // 128 - 1:
                dma(dbg_mix[:, :], MIX[:TT, :], r=[MIXK])
            pm, pmK = bbank()
            pmv = pm[:].rearrange("p (k t) -> p k t", k=8)
            for k in range(8):
                tr(pmv[:, k, :TT], MIX[:TT, k * 128:(k + 1) * 128], identb[:TT, :TT], [MIXK, identbK], [pmK])
            cp("act", MIXT[:, :, :TT], pmv[:, :, :TT], [pmK], [MIXTK])
            for half in range(2):
                bo, boK = bank()
                for k in range(8):
                    mm(bo[:TT, :], MIXT[:, k, :TT], woutb[:, k, half * 512:(half + 1) * 512], k == 0, k == 7, [MIXTK, woutbK], [boK])
                tt("dve", XN[:TT, half * 512:(half + 1) * 512], XN[:TT, half * 512:(half + 1) * 512], bo[:TT, :], ALU.add, [XNK, boK], [XNK])
            ss2, ss2K = sc["ss2"]; r2, r2K = sc["r2"]
            act(MIXT[:TT].rearrange("p k t -> p (k t)"), XN[:TT, :], AF.Square, [XNK], [MIXTK, ss2K], accum_out=ss2[:TT, :])
            rsqrt_chain((r2[:TT, :], r2K), (ss2[:TT, :], ss2K), 1.0 / 1024, 1e-6)
            ts("dve", XN[:TT, :], XN[:TT, :], r2[:TT, 0:1], None, ALU.mult, None, [XNK, r2K], [XNK])
            tt("pool", XN[:TT, :], XN[:TT, :], fnwb[:TT, :], ALU.mult, [XNK, fnwbK], [XNK])
            dma(ydst, XN[:TT, :], r=[XNK])

        def write_states(st, conv_dst, dn_dst, re_dst, im_dst, RAW, RAWK, s, L):
            for t in range(12):
                dma(conv_dst[:, t * 128:(t + 1) * 128].rearrange("r c -> c r"), RAW[:, t, s, L:L + 3], r=[RAWK], slow=True)
            dma(dn_dst.rearrange("h d e -> d h e"), Sst[st][0][:], r=[Sst[st][1]])
            if st == 0:
                op("dve", lambda e: e.reciprocal(out=RC[:], in_=rho[:]), r=[rhoK, RCK], w=[RCK])
                tt("dve", XS[0][0][:], XS[0][0][:], RC[:], ALU.mult, XS0K + [RCK], XS0K)
            bk, bkK = bank()
            tr(bk[0:32, 0:128], XS[st][0][:, :], ident[:, :], (XS0K if st == 0 else [XS[st][1]]) + [identK], [bkK])
            cp("dve", xso, bk[0:32, 0:128], [bkK], [xsoK])
            dma(re_dst, xso[:, 0:64], r=[xsoK])
            dma(im_dst, xso[:, 64:128], r=[xsoK])

        do_tile("s", 0, xs[:, :], ys[:, :])
        for s in range(NS):
            write_states(1 + s, o_convs[s], o_dns[s], o_res[s], o_ims[s], RAWs, RAWsK, s, 32)
        op("pool", lambda e: e.memset(RAWp[:], 0.0), w=[RAWpK])
        npre = NPRE // 128
        tiles = []
        for ti in range(npre):
            tiles.append((xpre[ti * 128:(ti + 1) * 128, :], None, ("stateq" if ti == npre - 1 else "state")))
        for ti in range(TP // 128):
            tiles.append((xp[ti * 128:(ti + 1) * 128, :], yp[ti * 128:(ti + 1) * 128, :], "full"))
        front_done = False
        for t, (xsrc_, ydst_, mode_) in enumerate(tiles):
            b_ = t % 2
            if not front_done:
                do_tile("p", t, xsrc_, ydst_, mode=mode_, b=b_, part="front")
            front_done = False
            if t + 1 < len(tiles):
                sts = do_tile("p", t, xsrc_, ydst_, mode=mode_, b=b_, part="rest_nomerge")
                nx = tiles[t + 1]
                stF = begin_stream([4], bb=0)
                do_tile("p", t + 1, nx[0], nx[1], mode=nx[2], b=(t + 1) % 2, part="front")
                end_stream()
                merge_streams(sts + [stF])
                front_done = True
                if mode_ == "full":
                    do_tile("p", t, xsrc_, ydst_, mode=mode_, b=b_, part="tail")
            else:
                do_tile("p", t, xsrc_, ydst_, mode=mode_, b=b_, part="rest")
        write_states(0, o_convp, o_dnp, o_rep, o_imp, RAWp, RAWpK, 0, 128)
        P.emit()
    return nc


_CACHE = {}


DBG = ()
LAST = {}
_CACHE = {}


def run(inputs, T, n_cores=8):
    NS = 2
    f = lambda a: np.ascontiguousarray(np.asarray(a, dtype=np.float32))
    xpf = f(inputs["x_prompt"])[:, :T]
    xsf = f(inputs["x_sample"])
    nb = xpf.shape[0]
    nseg = max(1, n_cores // nb)
    TP = T // nseg
    NPRE = (nseg - 1) * TP
    key = (TP, NPRE)
    if key not in _CACHE:
        _CACHE[key] = build(TP, NS, dbg=DBG, NPRE=NPRE)
    nc = _CACHE[key]
    in_maps = []
    wmap = {k: f(inputs[k]).reshape(WSHAPES[k]) for k in WSHAPES}
    cmap = {"c_" + k: v for k, v in CONSTS.items()}
    for c in range(n_cores):
        b, k = c // nseg, c % nseg
        m = {}
        if b < nb:
            m["xp"] = f(xpf[b, k * TP:(k + 1) * TP])
            if NPRE:
                pre = np.zeros((NPRE, 1024), np.float32)
                if k > 0:
                    pre[NPRE - k * TP:] = xpf[b, :k * TP]
                m["xpre"] = pre
        else:
            m["xp"] = np.zeros((TP, 1024), np.float32)
            if NPRE:
                m["xpre"] = np.zeros((NPRE, 1024), np.float32)
        sl = slice(NS * c, NS * c + NS)
        m["xs"] = f(xsf[sl]).reshape(NS * 32, 1024)
        m["cconv"] = f(inputs["cache_conv"][0, sl])
        m["sdn"] = f(inputs["state_dn"][0, sl])
        m["sre"] = f(inputs["state_s5_re"][0, sl])
        m["sim"] = f(inputs["state_s5_im"][0, sl])
        m.update(wmap); m.update(cmap)
        in_maps.append(m)
    res = run_bass_kernel_spmd(nc, in_maps, core_ids=list(range(n_cores)))
    R = res.results
    LAST['R'] = R
    y_prompt = np.stack([np.concatenate([R[b * nseg + k]["yp"] for k in range(nseg)]) for b in range(nb)])
    y_sample = np.concatenate([R[c]["ys"].reshape(NS, 32, 1024) for c in range(n_cores)])
    last = [b * nseg + nseg - 1 for b in range(nb)]
    convp = np.stack([R[c]["convp"] for c in last])[None]
    dnp = np.stack([R[c]["dnp"] for c in last])[None]
    rep = np.stack([R[c]["rep"] for c in last])[None]
    imp = np.stack([R[c]["imp"] for c in last])[None]
    convs = np.concatenate([R[c]["convs"] for c in range(n_cores)])[None]
    dns = np.concatenate([R[c]["dns"] for c in range(n_cores)])[None]
    res_ = np.concatenate([R[c]["res"] for c in range(n_cores)])[None]
    ims = np.concatenate([R[c]["ims"] for c in range(n_cores)])[None]
    outs = (y_prompt, y_sample, convp, dnp, rep, imp, convs, dns, res_, ims)
    return tuple(np.ascontiguousarray(o, dtype=np.float32) for o in outs)


def kernel(**inputs):
    return run(inputs, T=16384, n_cores=8)
```

```python
import math
import numpy as np
from contextlib import ExitStack
import concourse.bass as bass
import concourse.mybir as mybir
from concourse.bass_utils import run_bass_kernel_spmd

F32 = mybir.dt.float32
BF16 = mybir.dt.bfloat16
I32 = mybir.dt.int32
ALU = mybir.AluOpType
AF = mybir.ActivationFunctionType

EPOCH = 30000
NDMASEM = 24
PI = math.pi


class Prog:
    ENGS = ["pe", "act", "dve", "pool", "sp"]

    def __init__(self, nc, es):
        self.nc = nc
        self.es = es
        self.ops = {e: [] for e in self.ENGS}
        self.cnt = {e: 0 for e in self.ENGS}
        self.sems = {e: [] for e in self.ENGS}
        self.dma_sems = [es.enter_context(nc.semaphore(f"dq{i}")) for i in range(NDMASEM)]
        self.dma_n = 0
        self.dma_tokens = []
        self.lastw = {}
        self.readers = {}
        self.waited = {e: {} for e in self.ENGS}

    def _sem(self, e, idx):
        ep = idx // EPOCH
        while len(self.sems[e]) <= ep:
            self.sems[e].append(self.es.enter_context(self.nc.semaphore(f"s_{e}_{len(self.sems[e])}")))
        return self.sems[e][ep], idx % EPOCH + 1

    def op(self, eng, fn, r=(), w=(), dma=False):
        deps = []
        for k in r:
            if k in self.lastw:
                deps.append(self.lastw[k])
        for k in w:
            if k in self.lastw:
                deps.append(self.lastw[k])
            deps.extend(self.readers.get(k, []))
        if dma:
            i = self.dma_n
            self.dma_n += 1
            sem = self.dma_sems[i % NDMASEM]
            val = 16 * (i // NDMASEM + 1)
            if i >= NDMASEM:
                deps.append(self.dma_tokens[i - NDMASEM])
            tok = [sem, val, "dma", None]
            self.dma_tokens.append(tok)
            inc = (sem, 16)
        else:
            idx = self.cnt[eng]
            self.cnt[eng] += 1
            sem, val = self._sem(eng, idx)
            tok = [sem, val, eng, None]
            inc = (sem, 1)
        waits = []
        wd = self.waited[eng]
        for (s, v, te, vc) in deps:
            if te == "pe" and eng == "pe" and not dma:
                continue
            key = id(s)
            if wd.get(key, 0) >= v:
                continue
            wd[key] = v
            waits.append((s, v))
            if vc:
                for k2, v2 in vc.items():
                    if wd.get(k2, 0) < v2:
                        wd[k2] = v2
        tok[3] = dict(wd)
        tok = tuple(tok)
        if dma:
            self.dma_tokens[-1] = tok
        self.ops[eng].append((fn, waits, inc))
        for k in r:
            self.readers.setdefault(k, []).append(tok)
        for k in w:
            self.lastw[k] = tok
            self.readers[k] = []
        return tok

    def emit(self):
        nc = self.nc
        final = []
        for t in self.dma_tokens[-NDMASEM:]:
            final.append((t[0], t[1]))
        for e in self.ENGS:
            if self.cnt[e] > 0:
                final.append(self._sem(e, self.cnt[e] - 1))
        ops = self.ops

        def run(engh, lst, extra=None):
            for fn, waits, inc in lst:
                for (s, v) in waits:
                    engh.wait_ge(s, v)
                ins = fn(engh)
                ins.then_inc(inc[0], inc[1])
            if extra:
                for (s, v) in extra:
                    engh.wait_ge(s, v)

        with nc.Block() as block:
            @block.tensor
            def _(e):
                run(e, ops["pe"])

            @block.scalar
            def _(e):
                run(e, ops["act"])

            @block.vector
            def _(e):
                run(e, ops["dve"])

            @block.gpsimd
            def _(e):
                run(e, ops["pool"])

            @block.sync
            def _(e):
                run(e, ops["sp"], final)


def make_consts():
    c = {}
    c["ident"] = np.eye(128, dtype=np.float32)
    c["lst"] = np.tril(np.ones((128, 128), np.float32), -1)
    c["ones"] = np.ones((128, 128), np.float32)
    for name, TT, C in (("p", 128, 64), ("s", 64, 32)):
        idx = np.arange(TT)
        same = (idx[:, None] // C) == (idx[None, :] // C)
        tri = same & (idx[:, None] <= idx[None, :])
        us = same & (idx[None, :] > idx[:, None])
        ui = same & (idx[None, :] >= idx[:, None])
        ls = same & (idx[:, None] > idx[None, :])
        f = lambda a: a.astype(np.float32).copy()
        c["tri_" + name] = f(tri)
        c["blk_" + name] = f(same)
        c["us_" + name] = f(us)
        c["ui_" + name] = f(ui)
        c["ls_" + name] = f(ls)
        c["id_" + name] = f(np.eye(TT))
        c["tw_" + name] = f(2 * np.eye(TT))
        nch = TT // C
        ind = np.zeros((TT, nch, 128), np.float32)
        for ch in range(nch):
            ind[ch * C:(ch + 1) * C, ch, :] = 1.0
        c["ind_" + name] = ind
    c["tv"] = np.tile(np.arange(1, 65, dtype=np.float32)[None, :], (128, 1))
    m64 = np.ones((128, 64), np.float32)
    m64[:, 0] = 0.0
    c["m64"] = m64
    sg = np.ones((128, 4), np.float32)
    sg[:64, 0] = -1.0
    sg[64:, 1] = -1.0
    sg[:, 2] = -1.0
    c["sgn"] = sg
    psw = np.zeros((128, 128), np.float32)
    for n in range(64):
        psw[64 + n, n] = -1.0
        psw[n, 64 + n] = 1.0
    c["psw"] = psw
    rm = np.zeros((128, 8), np.float32)
    for g in range(8):
        rm[g * 16:(g + 1) * 16, g] = 1.0
    c["rowmask"] = rm
    return c


CONSTS = make_consts()

WSHAPES = {
    "norm_w": [1024], "w_in": [1024, 3080], "conv_w": [4, 1536], "dn_A_log": [4], "dn_dt_bias": [4],
    "dn_norm_w": [128], "s5_A_re": [32, 64], "s5_A_im": [32, 64], "s5_log_dt": [32],
    "s5_B_re": [32, 64, 16], "s5_B_im": [32, 64, 16], "s5_C_re": [32, 16, 64], "s5_C_im": [32, 16, 64],
    "s5_D": [512], "glu_w": [512, 512], "glu_b": [512], "w_out": [1024, 1024], "final_norm_w": [1024],
}


def build(TP, NS=2, dbg=(), NPRE=0):
    nc = bass.Bass("TRN2", target_bir_lowering=False)
    din = lambda n, s: nc.dram_tensor(n, list(s), F32, kind="ExternalInput").ap()
    dout = lambda n, s: nc.dram_tensor(n, list(s), F32, kind="ExternalOutput").ap()
    xp = din("xp", [TP, 1024]); xs = din("xs", [NS * 32, 1024])
    xpre = din("xpre", [NPRE, 1024]) if NPRE else None
    cconv = din("cconv", [NS, 3, 1536]); sdn = din("sdn", [NS, 4, 128, 128])
    sre = din("sre", [NS, 32, 64]); sim = din("sim", [NS, 32, 64])
    W = {k: din(k, v) for k, v in WSHAPES.items()}
    CD = {k: din("c_" + k, v.shape) for k, v in CONSTS.items()}
    yp = dout("yp", [TP, 1024]); ys = dout("ys", [NS * 32, 1024])
    o_convp = dout("convp", [3, 1536]); o_dnp = dout("dnp", [4, 128, 128])
    o_rep = dout("rep", [32, 64]); o_imp = dout("imp", [32, 64])
    o_convs = dout("convs", [NS, 3, 1536]); o_dns = dout("dns", [NS, 4, 128, 128])
    o_res = dout("res", [NS, 32, 64]); o_ims = dout("ims", [NS, 32, 64])
    dbg_mix = nc.dram_tensor("dbg_mix", [128, 1024], BF16, kind="ExternalOutput").ap() if dbg else None

    with ExitStack() as es:
        P = Prog(nc, es)
        _sbn = [0]

        def sb(shape, dt=F32, name=None):
            _sbn[0] += 1
            nm = name or f"t{_sbn[0]}"
            t = es.enter_context(nc.sbuf_tensor(nm, list(shape), dt))
            return t, nm

        banks = []
        for i in range(6):
            banks.append((es.enter_context(nc.psum_tensor(f"pb{i}", [128, 512], F32)), f"pb{i}"))
        bbanks = []
        for i in range(2):
            bbanks.append((es.enter_context(nc.psum_tensor(f"pbb{i}", [128, 1024], BF16)), f"pbb{i}"))
        _bi = [0, 0]
        _stream = [None]

        y5bank = banks.pop()

        def bank():
            st_ = _stream[0]
            if st_ is not None:
                st_["bi"] = (st_["bi"] + 1) % len(st_["banks"])
                return st_["banks"][st_["bi"]]
            _bi[0] = (_bi[0] + 1) % len(banks)
            return banks[_bi[0]]

        def begin_stream(bank_ids, bb=None):
            bl_ = [y5bank if i == "y5" else banks[i] for i in bank_ids]
            _stream[0] = dict(ops=[], banks=bl_, bi=0, bb=bb)
            return _stream[0]

        def end_stream():
            _stream[0] = None

        def merge_streams(sts):
            lists = [st_["ops"] for st_ in sts]
            idx = [0] * len(lists)
            total = sum(len(l) for l in lists)
            for _ in range(total):
                best, bv = None, None
                for i, l in enumerate(lists):
                    if idx[i] < len(l):
                        frac = idx[i] / len(l)
                        if bv is None or frac < bv:
                            best, bv = i, frac
                a = lists[best][idx[best]]
                idx[best] += 1
                P.op(a[0], a[1], r=a[2], w=a[3], dma=a[4])

        def bbank():
            st_ = _stream[0]
            if st_ is not None and st_.get("bb") is not None:
                return bbanks[st_["bb"]]
            _bi[1] = (_bi[1] + 1) % len(bbanks)
            return bbanks[_bi[1]]

        def op(eng, fn, r=(), w=(), dma=False):
            if _stream[0] is not None:
                _stream[0]["ops"].append((eng, fn, tuple(r), tuple(w), dma))
                return None
            return P.op(eng, fn, r=r, w=w, dma=dma)

        def dma(out, in_, r=(), w=(), slow=False):
            if slow:
                return op("sp", lambda e: e.dma_start(out=out, in_=in_, allow_slow_non_contiguous=True), r=r, w=w, dma=True)
            return op("sp", lambda e: e.dma_start(out=out, in_=in_), r=r, w=w, dma=True)

        def mm(out, lhsT, rhs, start, stop, r, w):
            return op("pe", lambda e: e.matmul(out, lhsT=lhsT, rhs=rhs, start=start, stop=stop), r=r, w=w)

        def tr(out, in_, ident, r, w):
            return op("pe", lambda e: e.transpose(out=out, in_=in_, identity=ident), r=r, w=w)

        def act(out, in_, func, r, w, **kw):
            return op("act", lambda e: e.activation(out=out, in_=in_, func=func, **kw), r=r, w=w)

        def tt(eng, out, in0, in1, o, r, w):
            return op(eng, lambda e: e.tensor_tensor(out=out, in0=in0, in1=in1, op=o), r=r, w=w)

        def ts(eng, out, in0, s1, s2, o0, o1, r, w):
            if o1 is None:
                return op(eng, lambda e: e.tensor_scalar(out=out, in0=in0, scalar1=s1, scalar2=None, op0=o0), r=r, w=w)
            return op(eng, lambda e: e.tensor_scalar(out=out, in0=in0, scalar1=s1, scalar2=s2, op0=o0, op1=o1), r=r, w=w)

        def stt(eng, out, in0, sc, in1, o0, o1, r, w):
            return op(eng, lambda e: e.scalar_tensor_tensor(out=out, in0=in0, scalar=sc, in1=in1, op0=o0, op1=o1), r=r, w=w)

        def cp(eng, out, in_, r, w):
            if eng == "act":
                return op("act", lambda e: e.copy(out=out, in_=in_), r=r, w=w)
            return op(eng, lambda e: e.tensor_copy(out=out, in_=in_), r=r, w=w)

        TA, TAK = sb([128, 512], F32, "TA")
        TB, TBK = sb([128, 512], F32, "TB")
        TC, TCK = sb([128, 512], F32, "TC")
        TD, TDK = sb([128, 512], F32, "TD")
        TE, TEK = sb([128, 512], F32, "TE")
        TF, TFK = sb([128, 512], F32, "TF")
        xt, xtK = sb([128, 1024], F32, "xt")
        XN, XNK = sb([128, 1024], F32, "XN")
        v4 = lambda t: t[:].rearrange("p (h e) -> p h e", h=4)

        C = {}
        for k, v in CONSTS.items():
            if k in ("ones",):
                dma(XN[:, 0:128], CD[k], w=[XNK])
                C[k] = (XN[:, 0:128], XNK)
                continue
            t, nm = sb(v.shape, F32, "k_" + k)
            dma(t[:], CD[k], w=[nm])
            C[k] = (t, nm)
        ident, identK = C["ident"]
        identb, identbK = sb([128, 128], BF16, "identb")
        cp("dve", identb[:], ident[:], [identK], [identbK])
        onesb, onesbK = sb([128, 128], BF16, "onesb")
        cp("dve", onesb[:], C["ones"][0], [C["ones"][1]], [onesbK])
        sgn, sgnK = C["sgn"]

        nwc, nwcK = sb([128, 8], F32, "nwc")
        for k in range(8):
            dma(nwc[:, k:k + 1], W["norm_w"][k * 128:(k + 1) * 128].rearrange("(p o) -> p o", o=1), w=[nwcK])
        winb, winbK = sb([128, 8, 3080], BF16, "winb")
        n_ = 0
        for k in range(8):
            for pc in range(4):
                c0, c1 = pc * 770, (pc + 1) * 770
                dma(XN[:, 0:770], W["w_in"][k * 128:(k + 1) * 128, c0:c1], w=[XNK])
                ts("dve" if n_ % 2 == 0 else "pool", winb[:, k, c0:c1], XN[:, 0:770], nwc[:, k:k + 1], None, ALU.mult, None, [XNK, nwcK], [winbK])
                n_ += 1
        woutb, woutbK = sb([128, 8, 1024], BF16, "woutb")
        for k in range(8):
            dma(XN[:, :], W["w_out"][k * 128:(k + 1) * 128, :], w=[XNK])
            cp("dve" if k % 2 == 0 else "pool", woutb[:, k, :], XN[:, :], [XNK], [woutbK])
        glub, glubK = sb([128, 4, 512], BF16, "glub")
        for k in range(4):
            dma(XN[:, 0:512], W["glu_w"][k * 128:(k + 1) * 128, :], w=[XNK])
            cp("dve", glub[:, k, :], XN[:, 0:512], [XNK], [glubK])
        fnwb, fnwbK = sb([128, 1024], F32, "fnwb")
        dma(fnwb[:], W["final_norm_w"].partition_broadcast(128), w=[fnwbK])
        glbb, glbbK = sb([128, 512], BF16, "glbb")
        dma(XN[:, 0:512], W["glu_b"].partition_broadcast(128), w=[XNK])
        cp("dve", glbb[:], XN[:, 0:512], [XNK], [glbbK])
        dnwb, dnwbK = sb([128, 128], F32, "dnwb")
        dma(dnwb[:], W["dn_norm_w"].partition_broadcast(128), w=[dnwbK])
        dtb, dtbK = sb([128, 4], F32, "dtb")
        dma(dtb[:], W["dn_dt_bias"].partition_broadcast(128), w=[dtbK])
        negA, negAK = sb([128, 4], F32, "negA")
        dma(negA[:], W["dn_A_log"].partition_broadcast(128), w=[negAK])
        act(negA[:], negA[:], AF.Exp, [negAK], [negAK])
        ts("dve", negA[:], negA[:], -1.0, None, ALU.mult, None, [negAK], [negAK])
        acc, accK = sb([128, 4, 128], F32, "acc")
        accKs = [accK + str(t) for t in range(4)]
        cw4, cw4K = None, None
        CW, CWK = sb([128, 12, 4], F32, "CW")
        for t3 in range(3):
            dma(XN[0:4, 0:512], W["conv_w"][:, t3 * 512:(t3 + 1) * 512], w=[XNK])
            for t4 in range(4):
                t = t3 * 4 + t4
                bk, bkK = bank()
                tr(bk[:, 0:4], XN[0:4, t4 * 128:(t4 + 1) * 128], ident[0:4, 0:4], [XNK, identK], [bkK])
                cp("dve", CW[:, t, :], bk[:, 0:4], [bkK], [CWK])

        PA, PAK = sb([128, 8, 128], F32, "PA")
        PB, PBK = sb([128, 8, 128], F32, "PB")
        KI, KIK = sb([128, 512], I32, "KI")
        COS, COSK = sb([128, 32, 64], F32, "COS")
        SIN, SINK = sb([128, 32, 64], F32, "SIN")

        def reduce_sin(out, arg, tmp, ki, keys, shift):
            (oK, aK, tK, kK) = keys
            ts("dve", tmp, arg, shift + 64 * PI, 1.0 / (2 * PI), ALU.add, ALU.mult, [aK], [tK])
            cp("dve", ki, tmp, [tK], [kK])
            cp("dve", tmp, ki, [kK], [tK])
            ts("dve", tmp, tmp, -2 * PI, shift + 64 * PI, ALU.mult, ALU.add, [tK], [tK])
            tt("dve", tmp, tmp, arg, ALU.add, [tK, aK], [tK])
            ts("dve", tmp, tmp, -3.1415925, 3.1415925, ALU.max, ALU.min, [tK], [tK])
            act(out, tmp, AF.Sin, [tK], [oK])

        ar2, ar2K = sb([32, 128], F32, "ar2")
        ai2, ai2K = sb([32, 128], F32, "ai2")
        for h in range(2):
            dma(ar2[:, h * 64:(h + 1) * 64], W["s5_A_re"], w=[ar2K])
            dma(ai2[:, h * 64:(h + 1) * 64], W["s5_A_im"], w=[ai2K])
        sm = {}
        for nm in ["lre", "lim", "dt", "ldr", "th", "rho", "c0", "s0", "t0", "t1", "lbr", "lbi", "den", "fre", "fim",
                   "FIMS", "FRES", "t2"]:
            sm[nm] = sb([128, 32], F32, "s5_" + nm)
        bk, bkK = bank()
        tr(bk[:, 0:32], ar2[0:32, :], ident[0:32, 0:32], [ar2K, identK], [bkK])
        ts("dve", sm["lre"][0][:], bk[:, 0:32], -1e-4, None, ALU.min, None, [bkK], [sm["lre"][1]])
        bk, bkK = bank()
        tr(bk[:, 0:32], ai2[0:32, :], ident[0:32, 0:32], [ai2K, identK], [bkK])
        cp("dve", sm["lim"][0][:], bk[:, 0:32], [bkK], [sm["lim"][1]])
        dma(sm["dt"][0][:], W["s5_log_dt"].partition_broadcast(128), w=[sm["dt"][1]])
        act(sm["dt"][0][:], sm["dt"][0][:], AF.Exp, [sm["dt"][1]], [sm["dt"][1]])
        S = lambda n: sm[n][0][:]
        K_ = lambda n: sm[n][1]
        tt("dve", S("ldr"), S("lre"), S("dt"), ALU.mult, [K_("lre"), K_("dt")], [K_("ldr")])
        tt("dve", S("th"), S("lim"), S("dt"), ALU.mult, [K_("lim"), K_("dt")], [K_("th")])
        act(S("rho"), S("ldr"), AF.Exp, [K_("ldr")], [K_("rho")])
        reduce_sin(S("s0"), S("th"), S("t0"), KI[:, 0:32], (K_("s0"), K_("th"), K_("t0"), KIK), 0.0)
        reduce_sin(S("c0"), S("th"), S("t0"), KI[:, 0:32], (K_("c0"), K_("th"), K_("t0"), KIK), PI / 2)
        tt("dve", S("lbr"), S("rho"), S("c0"), ALU.mult, [K_("rho"), K_("c0")], [K_("lbr")])
        tt("dve", S("lbi"), S("rho"), S("s0"), ALU.mult, [K_("rho"), K_("s0")], [K_("lbi")])
        tt("dve", S("den"), S("lre"), S("lre"), ALU.mult, [K_("lre")], [K_("den")])
        tt("dve", S("t0"), S("lim"), S("lim"), ALU.mult, [K_("lim")], [K_("t0")])
        tt("dve", S("den"), S("den"), S("t0"), ALU.add, [K_("den"), K_("t0")], [K_("den")])
        op("dve", lambda e: e.reciprocal(out=S("den"), in_=S("den")), r=[K_("den")], w=[K_("den")])
        ts("dve", S("t1"), S("lbr"), -1.0, None, ALU.add, None, [K_("lbr")], [K_("t1")])
        tt("dve", S("t0"), S("t1"), S("lre"), ALU.mult, [K_("t1"), K_("lre")], [K_("t0")])
        tt("dve", S("t2"), S("lbi"), S("lim"), ALU.mult, [K_("lbi"), K_("lim")], [K_("t2")])
        tt("dve", S("t0"), S("t0"), S("t2"), ALU.add, [K_("t0"), K_("t2")], [K_("t0")])
        tt("dve", S("fre"), S("t0"), S("den"), ALU.mult, [K_("t0"), K_("den")], [K_("fre")])
        tt("dve", S("t0"), S("lbi"), S("lre"), ALU.mult, [K_("lbi"), K_("lre")], [K_("t0")])
        tt("dve", S("t2"), S("t1"), S("lim"), ALU.mult, [K_("t1"), K_("lim")], [K_("t2")])
        tt("dve", S("t0"), S("t0"), S("t2"), ALU.subtract, [K_("t0"), K_("t2")], [K_("t0")])
        tt("dve", S("fim"), S("t0"), S("den"), ALU.mult, [K_("t0"), K_("den")], [K_("fim")])
        ts("dve", S("FIMS"), S("fim"), sgn[:, 0:1], None, ALU.mult, None, [K_("fim"), sgnK], [K_("FIMS")])
        ts("dve", S("FRES"), S("fre"), sgn[:, 1:2], None, ALU.mult, None, [K_("fre"), sgnK], [K_("FRES")])
        g16 = lambda t: t[:].rearrange("p (g c) -> p g c", g=32)
        Bst, BstK, Bsw, BswK, bb, bbK, bbs, bbsK, btmp, btmpK = g16(TA), TAK, g16(TB), TBK, g16(TC), TCK, g16(TD), TDK, g16(TE), TEK
        bre = W["s5_B_re"].rearrange("g n c -> n g c")
        bim = W["s5_B_im"].rearrange("g n c -> n g c")
        dma(Bst[0:64], bre, w=[BstK]); dma(Bst[64:128], bim, w=[BstK])
        dma(Bsw[0:64], bim, w=[BswK]); dma(Bsw[64:128], bre, w=[BswK])
        bc = lambda nm: sm[nm][0][:].unsqueeze(2).to_broadcast([128, 32, 16])
        tt("dve", bb, Bst, bc("fre"), ALU.mult, [BstK, K_("fre")], [bbK])
        tt("dve", btmp, Bsw, bc("FIMS"), ALU.mult, [BswK, K_("FIMS")], [btmpK])
        tt("dve", bb, bb, btmp, ALU.add, [bbK, btmpK], [bbK])
        tt("dve", bbs, Bsw, bc("FRES"), ALU.mult, [BswK, K_("FRES")], [bbsK])
        tt("dve", btmp, Bst, bc("fim"), ALU.mult, [BstK, K_("fim")], [btmpK])
        tt("dve", bbs, bbs, btmp, ALU.add, [bbsK, btmpK], [bbsK])
        BBn, BBnK = sb([128, 4, 4, 128], BF16, "BBn")
        BBs, BBsK = sb([128, 4, 4, 128], BF16, "BBs")
        rmask, rmaskK = C["rowmask"]
        for (src, srcK, dst, dstK) in ((bb, bbK, BBn, BBnK), (bbs, bbsK, BBs, BBsK)):
            for q in range(4):
                bk, bkK = bank()
                tr(bk[:, 0:128], src[:, 8 * q:8 * q + 8, :].rearrange("p g c -> p (g c)"), ident[:, :], [srcK, identK], [bkK])
                for g in range(8):
                    hf, j = g // 4, g % 4
                    ts("dve", dst[64 * hf:64 * hf + 64, q, j, :], bk[64 * hf:64 * hf + 64, 0:128], rmask[64 * hf:64 * hf + 64, g:g + 1], None, ALU.mult, None, [bkK, rmaskK], [dstK])
        tv, tvK = C["tv"]
        PAf = PA[:].rearrange("p g t -> p (g t)")
        PBf = PB[:].rearrange("p g t -> p (g t)")
        for q4 in range(4):
            argv = PAf[:, 0:512].rearrange("p (g t) -> p g t", g=8)
            tt("dve", argv, sm["th"][0][:, 8 * q4:8 * q4 + 8].unsqueeze(2).to_broadcast([128, 8, 64]),
               tv[:].unsqueeze(1).to_broadcast([128, 8, 64]), ALU.mult, [K_("th"), tvK], [PAK])
            reduce_sin(SIN[:, 8 * q4:8 * q4 + 8, :].rearrange("p g t -> p (g t)"), PAf[:, 0:512], PBf[:, 0:512], KI[:, :], (SINK, PAK, PBK, KIK), 0.0)
            reduce_sin(COS[:, 8 * q4:8 * q4 + 8, :].rearrange("p g t -> p (g t)"), PAf[:, 0:512], PBf[:, 0:512], KI[:, :], (COSK, PAK, PBK, KIK), PI / 2)
        Md1, Md1K = sb([128, 512], BF16, "Md1")
        Md2, Md2K = sb([128, 512], BF16, "Md2")
        cri, criK = xt[:].rearrange("p (q n) -> p q n", q=8), xtK
        cre = W["s5_C_re"].rearrange("g c n -> (g c) n")
        cim = W["s5_C_im"].rearrange("g c n -> (g c) n")
        for q in range(4):
            dma(cri[:, q, 0:64], cre[q * 128:(q + 1) * 128, :], w=[criK])
            dma(cri[:, q, 64:128], cim[q * 128:(q + 1) * 128, :], w=[criK])
            dma(cri[:, 4 + q, 0:64], cim[q * 128:(q + 1) * 128, :], w=[criK])
            dma(cri[:, 4 + q, 64:128], cre[q * 128:(q + 1) * 128, :], w=[criK])
        for q in range(4):
            bk, bkK = bank()
            tr(bk[:, 0:128], cri[:, q, :], ident[:, :], [criK, identK], [bkK])
            ts("dve", Md1[:, q * 128:(q + 1) * 128], bk[:, 0:128], sgn[:, 1:2], None, ALU.mult, None, [bkK, sgnK], [Md1K])
            bk, bkK = bank()
            tr(bk[:, 0:128], cri[:, 4 + q, :], ident[:, :], [criK, identK], [bkK])
            ts("dve", Md2[:, q * 128:(q + 1) * 128], bk[:, 0:128], -1.0, None, ALU.mult, None, [bkK], [Md2K])
        dcol, dcolK = sb([128, 4], F32, "dcol")
        for q in range(4):
            dma(dcol[:, q:q + 1], W["s5_D"][q * 128:(q + 1) * 128].rearrange("(p o) -> p o", o=1), w=[dcolK])
        diagD, diagDK = sb([128, 4, 128], BF16, "diagD")
        for q in range(4):
            ts("dve", diagD[:, q, :], ident[:, :], dcol[:, q:q + 1], None, ALU.mult, None, [identK, dcolK], [diagDK])
        psw, pswK = C["psw"]
        rho, rhoK = sm["rho"]
        RC, RCK = sb([128, 32], F32, "RC"); RS, RSK = sb([128, 32], F32, "RS")
        tt("dve", RC[:], COS[:, :, 63], rho[:], ALU.mult, [COSK, rhoK], [RCK])
        tt("dve", RS[:], SIN[:, :, 63], rho[:], ALU.mult, [SINK, rhoK], [RSK])
        RHOT, RHOTK = sb([128, 32, 64], F32, "RHOT")
        tt("dve", RHOT[:], rho[:].unsqueeze(2).to_broadcast([128, 32, 64]), C["m64"][0][:].unsqueeze(1).to_broadcast([128, 32, 64]), ALU.mult, [rhoK, C["m64"][1]], [RHOTK])

        NST = 1 + NS
        Sst = [sb([128, 4, 128], F32, f"Sst{i}") for i in range(NST)]
        Sbf = [sb([128, 4, 128], BF16, f"Sbf{i}") for i in range(NST)]
        XS = [sb([128, 32], F32, f"XS{i}") for i in range(NST)]
        op("pool", lambda e: e.memset(Sst[0][0][:], 0.0), w=[Sst[0][1]])
        op("pool", lambda e: e.memset(Sbf[0][0][:], 0.0), w=[Sbf[0][1]])
        XS0K = [f"XS0_{e}" for e in range(8)]
        op("pool", lambda e: e.memset(XS[0][0][:], 0.0), w=XS0K)
        xs2, xs2K = sb([32, 128], F32, "xs2")
        for s in range(NS):
            dma(Sst[1 + s][0][:], sdn[s].rearrange("h d e -> d h e"), w=[Sst[1 + s][1]])
            cp("pool", Sbf[1 + s][0][:], Sst[1 + s][0][:], [Sst[1 + s][1]], [Sbf[1 + s][1]])
            dma(xs2[:, 0:64], sre[s], w=[xs2K]); dma(xs2[:, 64:128], sim[s], w=[xs2K])
            bk, bkK = bank()
            tr(bk[:, 0:32], xs2[0:32, :], ident[0:32, 0:32], [xs2K, identK], [bkK])
            cp("dve", XS[1 + s][0][:], bk[:, 0:32], [bkK], [XS[1 + s][1]])

        sc = {}
        for nm in ["ss", "rstd", "ss2", "r2"]:
            sc[nm] = sb([128, 1], F32, "sc_" + nm)
        hb, hbK = sb([128, 1024], BF16, "hb")
        MIX, MIXK = KI[:].bitcast(BF16), KIK
        hT, hTK = sb([128, 8, 128], BF16, "hT")
        MIXT, MIXTK = sb([128, 8, 128], BF16, "MIXT")
        RAWp, RAWpK = sb([128, 12, 1, 131], F32, "RAWp")
        RAWs = RAWp[:].rearrange("p a b c -> p (a b c)")[:, 0:12 * NS * 35].rearrange("p (a b c) -> p a b c", a=12, b=NS)
        RAWsK = RAWpK
        for s in range(NS):
            for t in range(12):
                dma(RAWs[:, t, s, 0:3], cconv[s, :, t * 128:(t + 1) * 128].rearrange("r c -> c r"), w=[RAWsK], slow=True)
        QKVs = [sb([128, 12, 128], BF16, f"QKV{i}") for i in range(2)]
        uTs = [sb([128, 4, 128], BF16, f"uT{i}") for i in range(2)]
        GT8s = [sb([128, 8], F32, f"GT8_{i}") for i in range(2)]
        SZDs = [sb([128, 512], BF16, f"SZD{i}") for i in range(2)]
        SZSs = [sb([128, 512], BF16, f"SZS{i}") for i in range(2)]
        g4 = {}
        for nm in ["GT8", "BETA", "LNB", "SQB", "RSB", "XA", "G", "GC", "GD", "EGC", "EDEC", "SSK", "RK", "S1", "S2", "S3", "SSO", "RO"]:
            g4[nm] = sb([128, 8 if nm == "GT8" else 4], F32, "g_" + nm)
        EGL, EGLK = sb([128, 2, 4], F32, "EGL")
        Rg, RgK = v4(TA), TAK
        RQ, RQK = v4(TA), TAK
        OT, OTK = v4(TA), TAK
        EU, EUK = v4(TB), TBK
        X32, X32K = v4(TB), TBK
        Y5, Y5K = TB, TBK
        EL, ELK = v4(TC), TCK
        Z32, Z32K = v4(TC), TCK
        G1, G1K = TC, TCK
        EUI, EUIK = v4(TD), TDK
        OO, OOK = v4(TD), TDK
        Mt, MtK = v4(TE), TEK
        G2, G2K = TE, TEK
        VT, VTK = v4(TF), TFK
        GY, GYK = TF, TFK
        QHT, QHTK = sb([128, 4, 128], BF16, "QHT")
        KTk, KTkK = sb([128, 4, 128], BF16, "KTk")
        QSQ, QSQK = KTk, KTkK
        KG, KGK = sb([128, 4, 128], BF16, "KG"); KDEC, KDECK = sb([128, 4, 128], BF16, "KDEC")
        KTT, KTTK = sb([128, 4, 128], BF16, "KTT"); KGT, KGTK = sb([128, 4, 128], BF16, "KGT")
        Xb, XbK = sb([128, 4, 128], BF16, "Xb"); Mtb, MtbK = sb([128, 4, 128], BF16, "Mtb")
        Zb, ZbK = sb([128, 4, 128], BF16, "Zb")
        XF, XFK = sb([128, 4, 128], BF16, "XF")
        ATT, ATTK = sb([128, 4, 128], BF16, "ATT")
        Rr, RrK = sb([128, 4, 128], BF16, "Rr"); VN, VNK = sb([128, 4, 128], BF16, "VN")
        GYb, GYbK = Rr[:].rearrange("p h e -> p (h e)"), RrK
        GYT, GYTK = VN, VNK
        Ab, AbK = sb([128, 8, 128], BF16, "Ab"); Bb, BbK = sb([128, 8, 128], BF16, "Bb")
        AL, ALK = sb([128, 8], F32, "AL"); BL, BLK = sb([128, 8], F32, "BL")
        AL2, _ = sb([128, 4, 4], F32, "AL2"); BL2, _ = sb([128, 4, 4], F32, "BL2"); FX2, _ = sb([128, 4, 4], F32, "FX2")
        xso, xsoK = XN[0:32, 0:128], XNK
        def rsqrt_chain(outk, ink, scale, eps):
            o, oK = outk
            i_, iK = ink
            ts("dve", o, i_, scale, eps, ALU.mult, ALU.add, [iK], [oK])
            act(o, o, AF.Ln, [oK], [oK])
            act(o, o, AF.Exp, [oK], [oK], scale=-0.5)

        def do_tile(kind, ti, xsrc, ydst, mode="full", b=0, part="all"):
            if part == "tail":
                tile_tail(kind, ti, ydst, b, 128 if kind == "p" else NS * 32)
                return None
            full = mode == "full"
            needq = mode != "state"
            QKV, QKVK = QKVs[b]
            uT, uTK = uTs[b]
            SZD, SZDK = SZDs[b]
            SZS, SZSK = SZSs[b]
            if kind == "p":
                TT, Cc, nseg, L, RAW, RAWK = 128, 64, 1, 128, RAWp, RAWpK
                chunks = [(0, 0), (1, 0)]
                segs = [(0, 0), (1, 0)]
                SL = 64
            else:
                TT, Cc, nseg, L, RAW, RAWK = NS * 32, 32, NS, 32, RAWs, RAWsK
                chunks = [(s, 1 + s) for s in range(NS)]
                segs = [(s, 1 + s) for s in range(NS)]
                SL = 32
            nsg = TT // SL
            cn = lambda n: C[n + "_" + kind]
            m4 = lambda n: cn(n)[0][:TT, :TT].unsqueeze(1).to_broadcast([TT, 4, TT])
            x, xK = xt, xtK
            if part in ("all", "front"):
                x, xK = xt, xtK
                dma(x[:TT, :], xsrc, w=[xK])
                ss, ssK = sc["ss"]; rstd, rstdK = sc["rstd"]
                act(hb[:TT, :], x[:TT, :], AF.Square, [xK], [hbK, ssK], accum_out=ss[:TT, :])
                rsqrt_chain((rstd[:TT, :], rstdK), (ss[:TT, :], ssK), 1.0 / 1024, 1e-6)
                ts("dve", hb[:TT, :], x[:TT, :], rstd[:TT, 0:1], None, ALU.mult, None, [xK, rstdK], [hbK])
                pb, pbK = bbank()
                pbv = pb[:].rearrange("p (k t) -> p k t", k=8)
                for k in range(8):
                    tr(pbv[:, k, :TT], hb[:TT, k * 128:(k + 1) * 128], identb[:TT, :TT], [hbK, identbK], [pbK])
                cp("act", hT[:, :, :TT], pbv[:, :, :TT], [pbK], [hTK])
                for grp in range(4):
                    if grp == 0 and not needq:
                        continue
                    bk, bkK = bank()
                    bv = bk[:].rearrange("p (j t) -> p j t", j=4)
                    c0 = grp * 512 if grp < 3 else 2056
                    for j in range(4):
                        for k in range(8):
                            mm(bv[:, j, :TT], winb[:, k, c0 + j * 128:c0 + (j + 1) * 128], hT[:, k, :TT], k == 0, k == 7, [winbK, hTK], [bkK])
                    if grp < 3:
                        for s in range(nseg):
                            cp("act", RAW[:, grp * 4:(grp + 1) * 4, s, 3:3 + L], bv[:, :, s * L:(s + 1) * L], [bkK], [RAWK])
                    else:
                        cp("act", uT[:, :, :TT], bv[:, :, :TT], [bkK], [uTK])
                if full:
                    bk, bkK = bank()
                    for k in range(8):
                        mm(bk[:TT, :], hT[:, k, :TT], winb[:, k, 1544:2056], k == 0, k == 7, [winbK, hTK], [bkK])
                    act(SZD[:TT, :], bk[:TT, :], AF.Silu, [bkK], [SZDK])
                    bk, bkK = bank()
                    for k in range(8):
                        mm(bk[:TT, :], hT[:, k, :TT], winb[:, k, 2568:3080], k == 0, k == 7, [winbK, hTK], [bkK])
                    act(SZS[:TT, :], bk[:TT, :], AF.Silu, [bkK], [SZSK])
                bk, bkK = bank()
                for k in range(8):
                    mm(bk[:TT, 0:8], hT[:, k, :TT], winb[:, k, 1536:1544], k == 0, k == 7, [winbK, hTK], [bkK])
                GT8, GT8K = GT8s[b]
                cp("dve", GT8[:TT, :], bk[:TT, 0:8], [bkK], [GT8K])
                for t3 in range(3):
                    if t3 == 0 and not full:
                        continue
                    for t4 in range(4):
                        t = t3 * 4 + t4
                        av = acc[:, t4, :TT].rearrange("p (s l) -> p s l", s=nseg)
                        ts("dve", av, RAW[:, t, :, 0:L], CW[:, t, 0:1], None, ALU.mult, None, [RAWK, CWK], [accKs[t4]])
                        for j in range(1, 4):
                            stt("dve", av, RAW[:, t, :, j:j + L], CW[:, t, j:j + 1], av, ALU.mult, ALU.add, [RAWK, CWK, accKs[t4]], [accKs[t4]])
                    act(QKV[:, t3 * 4:(t3 + 1) * 4, :TT], acc[:, :, :TT], AF.Silu, accKs, [QKVK])
                if kind == "p":
                    cp("pool", RAW[:, :, 0, 0:3], RAW[:, :, 0, L:L + 3], [RAWK], [RAWK])
            if part == "front":
                return None
            if full:
                cp("pool", XN[:TT, :], x[:TT, :], [xK], [XNK])
            GT8, GT8K = GT8s[b]
            stG = begin_stream([0, 1], bb=1)
            G_ = lambda n: g4[n][0][:TT, :]
            GK = lambda n: g4[n][1]
            act(G_("BETA"), GT8[:TT, 4:8], AF.Exp, [GT8K], [GK("BETA")], scale=-1.0)
            act(G_("LNB"), G_("BETA"), AF.Ln, [GK("BETA")], [GK("LNB")], bias=1.0)
            act(G_("SQB"), G_("LNB"), AF.Exp, [GK("LNB")], [GK("SQB")], scale=-0.5)
            act(G_("RSB"), G_("LNB"), AF.Exp, [GK("LNB")], [GK("RSB")], scale=0.5)
            tt("dve", G_("XA"), GT8[:TT, 0:4], dtb[:TT, :], ALU.add, [GT8K, dtbK], [GK("XA")])
            act(G_("XA"), G_("XA"), AF.Exp, [GK("XA")], [GK("XA")])
            act(G_("XA"), G_("XA"), AF.Ln, [GK("XA")], [GK("XA")], bias=1.0)
            tt("dve", G_("G"), G_("XA"), negA[:TT, :], ALU.mult, [GK("XA"), negAK], [GK("G")])
            bk, bkK = bank()
            mm(bk[:TT, 0:4], cn("tri")[0][:TT, :TT], G_("G"), True, True, [cn("tri")[1], GK("G")], [bkK])
            mm(bk[:TT, 4:8], cn("blk")[0][:TT, :TT], G_("G"), True, True, [cn("blk")[1], GK("G")], [bkK])
            nch = len(chunks)
            for c in range(nch):
                mm(bk[:, 8 + 4 * c:12 + 4 * c], cn("ind")[0][:TT, c, :], G_("G"), True, True, [cn("ind")[1], GK("G")], [bkK])
            cp("dve", G_("GC"), bk[:TT, 0:4], [bkK], [GK("GC")])
            tt("dve", G_("GD"), bk[:TT, 4:8], G_("GC"), ALU.subtract, [bkK, GK("GC")], [GK("GD")])
            act(G_("EGC"), G_("GC"), AF.Exp, [GK("GC")], [GK("EGC")])
            act(G_("EDEC"), G_("GD"), AF.Exp, [GK("GD")], [GK("EDEC")])
            act(EGL[:, 0:nch, :], bk[:, 8:8 + 4 * nch].rearrange("p (c h) -> p c h", c=nch), AF.Exp, [bkK], [EGLK])
            tt("dve", Rg[:TT, :, :TT], m4("tri"), G_("G").unsqueeze(2).to_broadcast([TT, 4, TT]), ALU.mult, [cn("tri")[1], GK("G")], [RgK])
            bu, buK = bank()
            bl, blK = bank()
            buv = bu[:].rearrange("p (h t) -> p h t", h=4)
            blv = bl[:].rearrange("p (h t) -> p h t", h=4)
            lst, lstK = C["lst"]
            for h in range(4):
                mm(buv[:TT, h, :TT], lst[:TT, :TT], Rg[:TT, h, :TT], True, True, [lstK, RgK], [buK])
                mm(blv[:TT, h, :TT], Rg[:TT, h, :TT], lst[:TT, :TT], True, True, [lstK, RgK], [blK])
            act(EU[:TT, :, :TT], buv[:TT, :, :TT], AF.Exp, [buK], [EUK])
            act(EL[:TT, :, :TT], blv[:TT, :, :TT], AF.Exp, [blK], [ELK])
            if full:
                tt("pool", EUI[:TT, :, :TT], EU[:TT, :, :TT], m4("ui"), ALU.mult, [EUK, cn("ui")[1]], [EUIK])
            tt("pool", EU[:TT, :, :TT], EU[:TT, :, :TT], m4("us"), ALU.mult, [EUK, cn("us")[1]], [EUK])
            tt("pool", EL[:TT, :, :TT], EL[:TT, :, :TT], m4("ls"), ALU.mult, [ELK, cn("ls")[1]], [ELK])
            if full:
                act(QSQ[:, :, :TT], QKV[:, 0:4, :TT], AF.Square, [QKVK], [QSQK])
                bk, bkK = bank()
                bv = bk[:].rearrange("p (h t) -> p h t", h=4)
                for h in range(4):
                    mm(bv[:, h, :TT], onesb[:, :], QSQ[:, h, :TT], True, True, [onesbK, QSQK], [bkK])
                ts("dve", RQ[:, :, :TT], bv[:, :, :TT], 1e-6, None, ALU.add, None, [bkK], [RQK])
                act(RQ[:, :, :TT], RQ[:, :, :TT], AF.Ln, [RQK], [RQK])
                act(RQ[:, :, :TT], RQ[:, :, :TT], AF.Exp, [RQK], [RQK], scale=-0.5)
                stt("dve", QHT[:, :, :TT], QKV[:, 0:4, :TT], 128.0 ** -0.5, RQ[:, :, :TT], ALU.mult, ALU.mult, [QKVK, RQK], [QHTK])
            pk, pkK = bbank()
            pkv = pk[:].rearrange("p (h d) -> p h d", h=8)
            for h in range(4):
                tr(pkv[:TT, h, :], QKV[:, 4 + h, :TT], identb[:, :], [QKVK, identbK], [pkK])
                tr(pkv[:TT, 4 + h, :], QKV[:, 8 + h, :TT], identb[:, :], [QKVK, identbK], [pkK])
            SSK, SSKK = g4["SSK"]
            for h in range(4):
                act(Rr[:TT, 0, :], pkv[:TT, h, :], AF.Square, [pkK], [RrK, SSKK], accum_out=SSK[:TT, h:h + 1])
            rsqrt_chain((G_("RK"), GK("RK")), (G_("SSK"), SSKK), 1.0, 1e-6)
            tt("dve", G_("S1"), G_("RK"), G_("SQB"), ALU.mult, [GK("RK"), GK("SQB")], [GK("S1")])
            tt("dve", G_("S2"), G_("S1"), G_("EGC"), ALU.mult, [GK("S1"), GK("EGC")], [GK("S2")])
            tt("dve", G_("S3"), G_("RK"), G_("EDEC"), ALU.mult, [GK("RK"), GK("EDEC")], [GK("S3")])
            b4 = lambda n: g4[n][0][:TT, :].unsqueeze(2).to_broadcast([TT, 4, 128])
            tt("dve", KTk[:TT], pkv[:TT, 0:4, :], b4("S1"), ALU.mult, [pkK, GK("S1")], [KTkK])
            tt("dve", KG[:TT], pkv[:TT, 0:4, :], b4("S2"), ALU.mult, [pkK, GK("S2")], [KGK])
            tt("dve", KDEC[:TT], pkv[:TT, 0:4, :], b4("S3"), ALU.mult, [pkK, GK("S3")], [KDECK])
            tt("dve", VT[:TT], pkv[:TT, 4:8, :], b4("SQB"), ALU.mult, [pkK, GK("SQB")], [VTK])
            pk2, pk2K = bbank()
            pk2v = pk2[:].rearrange("p (h t) -> p h t", h=8)
            for h in range(4):
                tr(pk2v[:, h, :TT], KTk[:TT, h, :], identb[:TT, :TT], [KTkK, identbK], [pk2K])
                tr(pk2v[:, 4 + h, :TT], KG[:TT, h, :], identb[:TT, :TT], [KGK, identbK], [pk2K])
            cp("act", KTT[:, :, :TT], pk2v[:, 0:4, :TT], [pk2K], [KTTK])
            cp("act", KGT[:, :, :TT], pk2v[:, 4:8, :TT], [pk2K], [KGTK])
            bkk, bkkK = bank()
            bqk, bqkK = bank()
            kkv = bkk[:].rearrange("p (h t) -> p h t", h=4)
            qkv_ = bqk[:].rearrange("p (h t) -> p h t", h=4)
            for h in range(4):
                mm(kkv[:TT, h, :TT], KTT[:, h, :TT], KTT[:, h, :TT], True, True, [KTTK], [bkkK])
                if full:
                    mm(qkv_[:TT, h, :TT], KTT[:, h, :TT], QHT[:, h, :TT], True, True, [KTTK, QHTK], [bqkK])
            tt("dve", Mt[:TT, :, :TT], kkv[:TT, :, :TT], EL[:TT, :, :TT], ALU.mult, [bkkK, ELK], [MtK])
            tt("pool", Mt[:TT, :, :TT], Mt[:TT, :, :TT], m4("id"), ALU.add, [MtK, cn("id")[1]], [MtK])
            cp("pool", Mtb[:TT, :, :TT], Mt[:TT, :, :TT], [MtK], [MtbK])
            tt("dve", EU[:TT, :, :TT], kkv[:TT, :, :TT], EU[:TT, :, :TT], ALU.mult, [bkkK, EUK], [EUK])
            tt("pool", Xb[:TT, :, :TT], m4("id"), EU[:TT, :, :TT], ALU.subtract, [cn("id")[1], EUK], [XbK])
            RSB, RSBK = g4["RSB"]
            for h in range(4 if full else 0):
                stt("dve", ATT[:TT, h, :TT], qkv_[:TT, h, :TT], RSB[:TT, h:h + 1], EUI[:TT, h, :TT], ALU.mult, ALU.mult, [bqkK, RSBK, EUIK], [ATTK])
            for it in range(5):
                last = it == 4
                Xo, XoK, Mo, MoK = (X32, X32K, Mt, MtK) if last else (Xb, XbK, Mtb, MtbK)
                Zo, ZoK = (Z32, Z32K) if last else (Zb, ZbK)
                by, byK = bank()
                byv = by[:].rearrange("p (h t) -> p h t", h=4)
                for h in range(4):
                    mm(byv[:TT, h, :TT], Xo[:TT, h, :TT], Mo[:TT, h, :TT], True, True, [XoK, MoK], [byK])
                stt("dve", Zo[:TT, :, :TT], byv[:TT, :, :TT], -1.0, cn("tw")[0][:TT, :TT].unsqueeze(1).to_broadcast([TT, 4, TT]), ALU.mult, ALU.add, [byK, cn("tw")[1]], [ZoK])
                bx, bxK = bank()
                bxv = bx[:].rearrange("p (h t) -> p h t", h=4)
                for h in range(4):
                    mm(bxv[:TT, h, :TT], Zo[:TT, h, :TT], Xo[:TT, h, :TT], True, True, [ZoK, XoK], [bxK])
                if it < 3:
                    cp("act", Xb[:TT, :, :TT], bxv[:TT, :, :TT], [bxK], [XbK])
                elif it == 3:
                    cp("act", X32[:TT, :, :TT], bxv[:TT, :, :TT], [bxK], [X32K])
                else:
                    cp("act", XF[:TT, :, :TT], bxv[:TT, :, :TT], [bxK], [XFK])
            for (c, st) in chunks:
                r0, r1 = c * Cc, (c + 1) * Cc
                Sm, SmK = Sst[st]
                Sb_, SbK = Sbf[st]
                ba, baK = bank(); bq, bqK = bank()
                bav = ba[:].rearrange("p (h e) -> p h e", h=4)
                bqv = bq[:].rearrange("p (h e) -> p h e", h=4)
                for h in range(4):
                    mm(bav[:TT, h, :], KGT[:, h, :TT], Sb_[:, h, :], True, True, [KGTK, SbK], [baK])
                    if full:
                        mm(bqv[:TT, h, :], QHT[:, h, :TT], Sb_[:, h, :], True, True, [QHTK, SbK], [bqK])
                tt("dve", Rr[r0:r1], VT[r0:r1], bav[r0:r1], ALU.subtract, [VTK, baK], [RrK])
                if full:
                    tt("dve", OO[r0:r1], bqv[r0:r1], g4["EGC"][0][r0:r1, :].unsqueeze(2).to_broadcast([Cc, 4, 128]), ALU.mult, [bqK, GK("EGC")], [OOK])
                bc_, bcK = bank()
                bcv = bc_[:].rearrange("p (h e) -> p h e", h=4)
                for h in range(4):
                    mm(bcv[:TT, h, :], XF[r0:r1, h, :TT], Rr[r0:r1, h, :], True, True, [XFK, RrK], [bcK])
                tt("dve", VN[r0:r1], bcv[r0:r1], g4["SQB"][0][r0:r1, :].unsqueeze(2).to_broadcast([Cc, 4, 128]), ALU.mult, [bcK, GK("SQB")], [VNK])
                bd, bdK = bank(); be, beK = bank()
                bdv = bd[:].rearrange("p (h e) -> p h e", h=4)
                bev = be[:].rearrange("p (h e) -> p h e", h=4)
                for h in range(4):
                    if full:
                        mm(bdv[:TT, h, :], ATT[r0:r1, h, :TT], VN[r0:r1, h, :], True, True, [ATTK, VNK], [bdK])
                    mm(bev[:, h, :], KDEC[r0:r1, h, :], VN[r0:r1, h, :], True, True, [KDECK, VNK], [beK])
                if full:
                    tt("dve", OO[r0:r1], OO[r0:r1], bdv[r0:r1], ALU.add, [OOK, bdK], [OOK])
                tt("dve", Sm[:], Sm[:], EGL[:, c, :].unsqueeze(2).to_broadcast([128, 4, 128]), ALU.mult, [SmK, EGLK], [SmK])
                tt("dve", Sm[:], Sm[:], bev[:, :, :], ALU.add, [SmK, beK], [SmK])
                cp("act", Sb_[:], Sm[:], [SmK], [SbK])
            if full:
                SSO, SSOK = g4["SSO"]
                for h in range(4):
                    act(Rr[:TT, 0, :], OO[:TT, h, :], AF.Square, [OOK], [RrK, SSOK], accum_out=SSO[:TT, h:h + 1])
                rsqrt_chain((G_("RO"), GK("RO")), (G_("SSO"), SSOK), 1.0 / 128, 1e-6)
                tt("dve", OO[:TT], OO[:TT], b4("RO"), ALU.mult, [OOK, GK("RO")], [OOK])
                tt("pool", OO[:TT], OO[:TT], dnwb[:TT, :].unsqueeze(1).to_broadcast([TT, 4, 128]), ALU.mult, [OOK, dnwbK], [OOK])
                tt("dve", MIX[:TT, 0:512].rearrange("p (h e) -> p h e", h=4), OO[:TT], SZD[:TT, :].rearrange("p (h e) -> p h e", h=4), ALU.mult, [OOK, SZDK], [MIXK])
            end_stream()
            stS = begin_stream([2, 3])
            by5, by5K = y5bank
            PAKs = [PAK + "0", PAK + "1"]; PBKs = [PBK + "0", PBK + "1"]
            AbKs = [AbK + "0", AbK + "1"]; BbKs = [BbK + "0", BbK + "1"]
            if kind == "p":
                PAf = PA[:].rearrange("p g t -> p (g t)"); PBf = PB[:].rearrange("p g t -> p (g t)")
                xs_ = XS[0][0]

                def s1(e):
                    i, q, hf, gs = e % 2, e // 2, e % 2, 4 * e
                    hs = slice(64 * hf, 64 * hf + 64)
                    b1, b1K = bank(); b2, b2K = bank()
                    b1v = b1[:].rearrange("p (g t) -> p g t", g=4)
                    b2v = b2[:].rearrange("p (g t) -> p g t", g=4)
                    for j in range(4):
                        mm(b1v[:, j, :], BBn[hs, q, j, :], uT[hs, q, :], True, True, [BBnK, uTK], [b1K])
                        mm(b2v[:, j, :], BBs[hs, q, j, :], uT[hs, q, :], True, True, [BBsK, uTK], [b2K])
                    pa = PAf[:, i * 512:(i + 1) * 512].rearrange("p (h g t) -> p g h t", h=2, g=4)
                    pb = PBf[:, i * 512:(i + 1) * 512].rearrange("p (h g t) -> p g h t", h=2, g=4)
                    cosb = COS[:, gs:gs + 4, :].unsqueeze(2).to_broadcast([128, 4, 2, 64])
                    sinb = SIN[:, gs:gs + 4, :].unsqueeze(2).to_broadcast([128, 4, 2, 64])
                    tt("dve", pa, b1v[:, :, :].rearrange("p g (h t) -> p g h t", h=2), cosb, ALU.mult, [b1K, COSK], [PAKs[i]])
                    tt("dve", pb, b2v[:, :, :].rearrange("p g (h t) -> p g h t", h=2), sinb, ALU.mult, [b2K, SINK], [PBKs[i]])
                    tt("pool", PAf[:, i * 512:(i + 1) * 512], PAf[:, i * 512:(i + 1) * 512], PBf[:, i * 512:(i + 1) * 512], ALU.add, [PAKs[i], PBKs[i]], [PAKs[i]])

                def s2(e, h):
                    i, gs = e % 2, 4 * e
                    k2 = (e % 2) * 2 + h
                    kk = f"s5sm{k2}"
                    pa3 = PAf[:, i * 512:(i + 1) * 512].rearrange("p (h g t) -> p h g t", h=2, g=4)
                    pb3 = PBf[:, i * 512:(i + 1) * 512].rearrange("p (h g t) -> p h g t", h=2, g=4)
                    if h == 0:
                        tt("dve", pa3[:, h, :, 0], pa3[:, h, :, 0], xs_[:, gs:gs + 4], ALU.add, [PAKs[i], XS0K[e]], [PAKs[i]])
                    c0 = i * 512 + h * 256
                    op("dve", lambda en, c0=c0, gs=gs: en.tensor_tensor_scan(
                        out=PBf[:, c0:c0 + 256], data0=RHOT[:, gs:gs + 4, :].rearrange("p g t -> p (g t)"),
                        data1=PAf[:, c0:c0 + 256], initial=0.0, op0=ALU.mult, op1=ALU.add),
                       r=[PAKs[i], RHOTK], w=[PBKs[i]])
                    tt("dve", AL2[:, k2, :], pb3[:, h, :, 63], RC[:, gs:gs + 4], ALU.mult, [PBKs[i], RCK], [kk + "a"])
                    tt("dve", BL2[:, k2, :], pb3[:, h, :, 63], RS[:, gs:gs + 4], ALU.mult, [PBKs[i], RSK], [kk + "b"])
                    bk, bkK = bank()
                    mm(bk[:, 0:4], psw[:, :], BL2[:, k2, :], True, True, [pswK, kk + "b"], [bkK])
                    if h == 0:
                        tt("dve", pa3[:, 1, :, 0], pa3[:, 1, :, 0], AL2[:, k2, :], ALU.add, [PAKs[i], kk + "a"], [PAKs[i]])
                        tt("dve", pa3[:, 1, :, 0], pa3[:, 1, :, 0], bk[:, 0:4], ALU.add, [PAKs[i], bkK], [PAKs[i]])
                    else:
                        tt("dve", xs_[:, gs:gs + 4], bk[:, 0:4], AL2[:, k2, :], ALU.add, [bkK, kk + "a"], [XS0K[e]])

                def s3(e):
                    i, q, gs = e % 2, e // 2, 4 * e
                    pb = PBf[:, i * 512:(i + 1) * 512].rearrange("p (h g t) -> p g h t", h=2, g=4)
                    cosb = COS[:, gs:gs + 4, :].unsqueeze(2).to_broadcast([128, 4, 2, 64])
                    sinb = SIN[:, gs:gs + 4, :].unsqueeze(2).to_broadcast([128, 4, 2, 64])
                    ab = Ab[:, 4 * i:4 * i + 4, :].rearrange("p g (h t) -> p g h t", h=2)
                    bbv = Bb[:, 4 * i:4 * i + 4, :].rearrange("p g (h t) -> p g h t", h=2)
                    tt("pool", ab, pb, cosb, ALU.mult, [PBKs[i], COSK], [AbKs[i]])
                    tt("dve", bbv, pb, sinb, ALU.mult, [PBKs[i], SINK], [BbKs[i]])
                    for g4_ in range(4):
                        g = gs + g4_
                        g8 = g % 8
                        mm(by5[:TT, g * 16:(g + 1) * 16], uT[:, q, :TT], diagD[:, q, g8 * 16:(g8 + 1) * 16], True, False, [uTK, diagDK], [by5K])
                        mm(by5[:TT, g * 16:(g + 1) * 16], Ab[:, 4 * i + g4_, :TT], Md1[:, g * 16:(g + 1) * 16], False, False, [AbKs[i], Md1K], [by5K])
                        mm(by5[:TT, g * 16:(g + 1) * 16], Bb[:, 4 * i + g4_, :TT], Md2[:, g * 16:(g + 1) * 16], False, True, [BbKs[i], Md2K], [by5K])

                s1(0)
                for e in range(8):
                    s2(e, 0)
                    if e + 1 < 8:
                        s1(e + 1)
                    s2(e, 1)
                    if full:
                        s3(e)
            else:
                for q in range(4):
                    PAv = PA[:, :, :TT].rearrange("p g (s l) -> p g s l", s=nsg)
                    PBv = PB[:, :, :TT].rearrange("p g (s l) -> p g s l", s=nsg)
                    for half in range(2):
                        b1, b1K = bank(); b2, b2K = bank()
                        b1v = b1[:].rearrange("p (g t) -> p g t", g=4)
                        b2v = b2[:].rearrange("p (g t) -> p g t", g=4)
                        hs = slice(64 * half, 64 * half + 64)
                        for j in range(4):
                            mm(b1v[:, j, :TT], BBn[hs, q, j, :], uT[hs, q, :TT], True, True, [BBnK, uTK], [b1K])
                            mm(b2v[:, j, :TT], BBs[hs, q, j, :], uT[hs, q, :TT], True, True, [BBsK, uTK], [b2K])
                        gs = 8 * q + 4 * half
                        cosb = COS[:, gs:gs + 4, 0:SL].unsqueeze(2).to_broadcast([128, 4, nsg, SL])
                        sinb = SIN[:, gs:gs + 4, 0:SL].unsqueeze(2).to_broadcast([128, 4, nsg, SL])
                        tt("dve", PAv[:, 4 * half:4 * half + 4], b1v[:, :, :TT].rearrange("p g (s l) -> p g s l", s=nsg), cosb, ALU.mult, [b1K, COSK], PAKs)
                        tt("dve", PBv[:, 4 * half:4 * half + 4], b2v[:, :, :TT].rearrange("p g (s l) -> p g s l", s=nsg), sinb, ALU.mult, [b2K, SINK], PBKs)
                    tt("pool", PA[:, :, :TT], PA[:, :, :TT], PB[:, :, :TT], ALU.add, PAKs + PBKs, PAKs)
                    for (s, st) in segs:
                        xs_, xsK = XS[st]
                        for g in range(8):
                            op("dve", lambda e, g=g, s=s, xs_=xs_, q=q: e.tensor_tensor_scan(
                                out=PB[:, g, s * SL:(s + 1) * SL], data0=rho[:, 8 * q + g:8 * q + g + 1].to_broadcast([128, SL]),
                                data1=PA[:, g, s * SL:(s + 1) * SL], initial=xs_[:, 8 * q + g:8 * q + g + 1], op0=ALU.mult, op1=ALU.add),
                               r=PAKs + [rhoK, xsK], w=PBKs)
                        e_ = (s + 1) * SL - 1
                        tt("dve", AL[:], PB[:, :, e_], COS[:, 8 * q:8 * q + 8, SL - 1], ALU.mult, PBKs + [COSK], [ALK])
                        tt("dve", BL[:], PB[:, :, e_], SIN[:, 8 * q:8 * q + 8, SL - 1], ALU.mult, PBKs + [SINK], [BLK])
                        bk, bkK = bank()
                        mm(bk[:, 0:8], ident[:, :], AL[:], True, False, [identK, ALK], [bkK])
                        mm(bk[:, 0:8], psw[:, :], BL[:], False, True, [pswK, BLK], [bkK])
                        cp("act", xs_[:, 8 * q:8 * q + 8], bk[:, 0:8], [bkK], [xsK])
                    if not full:
                        continue
                    cosf = COS[:, 8 * q:8 * q + 8, 0:SL].unsqueeze(2).to_broadcast([128, 8, nsg, SL])
                    sinf = SIN[:, 8 * q:8 * q + 8, 0:SL].unsqueeze(2).to_broadcast([128, 8, nsg, SL])
                    tt("pool", Ab[:, :, :TT].rearrange("p g (s l) -> p g s l", s=nsg), PBv, cosf, ALU.mult, PBKs + [COSK], AbKs)
                    tt("dve", Bb[:, :, :TT].rearrange("p g (s l) -> p g s l", s=nsg), PBv, sinf, ALU.mult, PBKs + [SINK], BbKs)
                    for g8 in range(8):
                        g = 8 * q + g8
                        mm(by5[:TT, g * 16:(g + 1) * 16], uT[:, q, :TT], diagD[:, q, g8 * 16:(g8 + 1) * 16], True, False, [uTK, diagDK], [by5K])
                        mm(by5[:TT, g * 16:(g + 1) * 16], Ab[:, g8, :TT], Md1[:, g * 16:(g + 1) * 16], False, False, AbKs + [Md1K], [by5K])
                        mm(by5[:TT, g * 16:(g + 1) * 16], Bb[:, g8, :TT], Md2[:, g * 16:(g + 1) * 16], False, True, BbKs + [Md2K], [by5K])
            end_stream()
            if part == "rest_nomerge":
                return [stG, stS]
            merge_streams([stG, stS])
            if not full:
                return None
            tile_tail(kind, ti, ydst, b, TT)
            return None

        def tile_tail(kind, ti, ydst, b, TT):
            x, xK = xt, xtK
            SZS, SZSK = SZSs[b]
            by5, by5K = y5bank
            cp("act", Y5[:TT, :], by5[:TT, :], [by5K], [Y5K])
            tt("dve", G1[:TT, :], Y5[:TT, :], Y5[:TT, :], ALU.mult, [Y5K], [G1K])
            ts("dve", G1[:TT, :], G1[:TT, :], 0.044715, 1.0, ALU.mult, ALU.add, [G1K], [G1K])
            tt("dve", G1[:TT, :], G1[:TT, :], Y5[:TT, :], ALU.mult, [G1K, Y5K], [G1K])
            act(G2[:TT, :], G1[:TT, :], AF.Sigmoid, [G1K], [G2K], scale=2.0 * math.sqrt(2.0 / PI))
            tt("dve", GY[:TT, :], Y5[:TT, :], G2[:TT, :], ALU.mult, [Y5K, G2K], [GYK])
            cp("pool", GYb[:TT, :], GY[:TT, :], [GYK], [GYbK])
            pg, pgK = bbank()
            pgv = pg[:].rearrange("p (k t) -> p k t", k=8)
            for q in range(4):
                tr(pgv[:, q, :TT], GYb[:TT, q * 128:(q + 1) * 128], identb[:TT, :TT], [GYbK, identbK], [pgK])
            cp("act", GYT[:, :, :TT], pgv[:, 0:4, :TT], [pgK], [GYTK])
            bg, bgK = bank()
            for q in range(4):
                mm(bg[:TT, :], GYT[:, q, :TT], glub[:, q, :], q == 0, q == 3, [GYTK, glubK], [bgK])
            tt("dve", G1[:TT, :], bg[:TT, :], glbb[:TT, :], ALU.add, [bgK, glbbK], [G1K])
            act(G2[:TT, :], G1[:TT, :], AF.Sigmoid, [G1K], [G2K])
            tt("dve", G1[:TT, :], GY[:TT, :], G2[:TT, :], ALU.mult, [GYK, G2K], [G1K])
            tt("dve", MIX[:TT, 512:1024], G1[:TT, :], SZS[:TT, :], ALU.mult, [G1K, SZSK], [MIXK])
            if dbg and kind == "p" and ti == TP // 128 - 1:
                dma(dbg_mix[:, :], MIX[:TT, :], r=[MIXK])
            pm, pmK = bbank()
            pmv = pm[:].rearrange("p (k t) -> p k t", k=8)
            for k in range(8):
                tr(pmv[:, k, :TT], MIX[:TT, k * 128:(k + 1) * 128], identb[:TT, :TT], [MIXK, identbK], [pmK])
            cp("act", MIXT[:, :, :TT], pmv[:, :, :TT], [pmK], [MIXTK])
            for half in range(2):
                bo, boK = bank()
                for k in range(8):
                    mm(bo[:TT, :], MIXT[:, k, :TT], woutb[:, k, half * 512:(half + 1) * 512], k == 0, k == 7, [MIXTK, woutbK], [boK])
                tt("dve", XN[:TT, half * 512:(half + 1) * 512], XN[:TT, half * 512:(half + 1) * 512], bo[:TT, :], ALU.add, [XNK, boK], [XNK])
            ss2, ss2K = sc["ss2"]; r2, r2K = sc["r2"]
            act(MIXT[:TT].rearrange("p k t -> p (k t)"), XN[:TT, :], AF.Square, [XNK], [MIXTK, ss2K], accum_out=ss2[:TT, :])
            rsqrt_chain((r2[:TT, :], r2K), (ss2[:TT, :], ss2K), 1.0 / 1024, 1e-6)
            ts("dve", XN[:TT, :], XN[:TT, :], r2[:TT, 0:1], None, ALU.mult, None, [XNK, r2K], [XNK])
            tt("pool", XN[:TT, :], XN[:TT, :], fnwb[:TT, :], ALU.mult, [XNK, fnwbK], [XNK])
            dma(ydst, XN[:TT, :], r=[XNK])

        def write_states(st, conv_dst, dn_dst, re_dst, im_dst, RAW, RAWK, s, L):
            for t in range(12):
                dma(conv_dst[:, t * 128:(t + 1) * 128].rearrange("r c -> c r"), RAW[:, t, s, L:L + 3], r=[RAWK], slow=True)
            dma(dn_dst.rearrange("h d e -> d h e"), Sst[st][0][:], r=[Sst[st][1]])
            if st == 0:
                op("dve", lambda e: e.reciprocal(out=RC[:], in_=rho[:]), r=[rhoK, RCK], w=[RCK])
                tt("dve", XS[0][0][:], XS[0][0][:], RC[:], ALU.mult, XS0K + [RCK], XS0K)
            bk, bkK = bank()
            tr(bk[0:32, 0:128], XS[st][0][:, :], ident[:, :], (XS0K if st == 0 else [XS[st][1]]) + [identK], [bkK])
            cp("dve", xso, bk[0:32, 0:128], [bkK], [xsoK])
            dma(re_dst, xso[:, 0:64], r=[xsoK])
            dma(im_dst, xso[:, 64:128], r=[xsoK])

        do_tile("s", 0, xs[:, :], ys[:, :])
        for s in range(NS):
            write_states(1 + s, o_convs[s], o_dns[s], o_res[s], o_ims[s], RAWs, RAWsK, s, 32)
        op("pool", lambda e: e.memset(RAWp[:], 0.0), w=[RAWpK])
        npre = NPRE // 128
        tiles = []
        for ti in range(npre):
            tiles.append((xpre[ti * 128:(ti + 1) * 128, :], None, ("stateq" if ti == npre - 1 else "state")))
        for ti in range(TP // 128):
            tiles.append((xp[ti * 128:(ti + 1) * 128, :], yp[ti * 128:(ti + 1) * 128, :], "full"))
        front_done = False
        for t, (xsrc_, ydst_, mode_) in enumerate(tiles):
            b_ = t % 2
            if not front_done:
                do_tile("p", t, xsrc_, ydst_, mode=mode_, b=b_, part="front")
            front_done = False
            if t + 1 < len(tiles):
                sts = do_tile("p", t, xsrc_, ydst_, mode=mode_, b=b_, part="rest_nomerge")
                nx = tiles[t + 1]
                stF = begin_stream([4], bb=0)
                do_tile("p", t + 1, nx[0], nx[1], mode=nx[2], b=(t + 1) % 2, part="front")
                end_stream()
                merge_streams(sts + [stF])
                front_done = True
                if mode_ == "full":
                    do_tile("p", t, xsrc_, ydst_, mode=mode_, b=b_, part="tail")
            else:
                do_tile("p", t, xsrc_, ydst_, mode=mode_, b=b_, part="rest")
        write_states(0, o_convp, o_dnp, o_rep, o_imp, RAWp, RAWpK, 0, 128)
        P.emit()
    return nc


_CACHE = {}


DBG = ()
LAST = {}
_CACHE = {}


def run(inputs, T, n_cores=8):
    NS = 2
    f = lambda a: np.ascontiguousarray(np.asarray(a, dtype=np.float32))
    xpf = f(inputs["x_prompt"])[:, :T]
    xsf = f(inputs["x_sample"])
    nb = xpf.shape[0]
    nseg = max(1, n_cores // nb)
    TP = T // nseg
    NPRE = (nseg - 1) * TP
    key = (TP, NPRE)
    if key not in _CACHE:
        _CACHE[key] = build(TP, NS, dbg=DBG, NPRE=NPRE)
    nc = _CACHE[key]
    in_maps = []
    wmap = {k: f(inputs[k]).reshape(WSHAPES[k]) for k in WSHAPES}
    cmap = {"c_" + k: v for k, v in CONSTS.items()}
    for c in range(n_cores):
        b, k = c // nseg, c % nseg
        m = {}
        if b < nb:
            m["xp"] = f(xpf[b, k * TP:(k + 1) * TP])
            if NPRE:
                pre = np.zeros((NPRE, 1024), np.float32)
                if k > 0:
                    pre[NPRE - k * TP:] = xpf[b, :k * TP]
                m["xpre"] = pre
        else:
            m["xp"] = np.zeros((TP, 1024), np.float32)
            if NPRE:
                m["xpre"] = np.zeros((NPRE, 1024), np.float32)
        sl = slice(NS * c, NS * c + NS)
        m["xs"] = f(xsf[sl]).reshape(NS * 32, 1024)
        m["cconv"] = f(inputs["cache_conv"][0, sl])
        m["sdn"] = f(inputs["state_dn"][0, sl])
        m["sre"] = f(inputs["state_s5_re"][0, sl])
        m["sim"] = f(inputs["state_s5_im"][0, sl])
        m.update(wmap); m.update(cmap)
        in_maps.append(m)
    res = run_bass_kernel_spmd(nc, in_maps, core_ids=list(range(n_cores)))
    R = res.results
    LAST['R'] = R
    y_prompt = np.stack([np.concatenate([R[b * nseg + k]["yp"] for k in range(nseg)]) for b in range(nb)])
    y_sample = np.concatenate([R[c]["ys"].reshape(NS, 32, 1024) for c in range(n_cores)])
    last = [b * nseg + nseg - 1 for b in range(nb)]
    convp = np.stack([R[c]["convp"] for c in last])[None]
    dnp = np.stack([R[c]["dnp"] for c in last])[None]
    rep = np.stack([R[c]["rep"] for c in last])[None]
    imp = np.stack([R[c]["imp"] for c in last])[None]
    convs = np.concatenate([R[c]["convs"] for c in range(n_cores)])[None]
    dns = np.concatenate([R[c]["dns"] for c in range(n_cores)])[None]
    res_ = np.concatenate([R[c]["res"] for c in range(n_cores)])[None]
    ims = np.concatenate([R[c]["ims"] for c in range(n_cores)])[None]
    outs = (y_prompt, y_sample, convp, dnp, rep, imp, convs, dns, res_, ims)
    return tuple(np.ascontiguousarray(o, dtype=np.float32) for o in outs)


def kernel(**inputs):
    return run(inputs, T=16384, n_cores=8)
```

```python
import math
import numpy as np
from contextlib import ExitStack
import concourse.bass as bass
import concourse.mybir as mybir
from concourse.bass_utils import run_bass_kernel_spmd

F32 = mybir.dt.float32
BF16 = mybir.dt.bfloat16
I32 = mybir.dt.int32
ALU = mybir.AluOpType
AF = mybir.ActivationFunctionType

EPOCH = 30000
NDMASEM = 24
PI = math.pi


class Prog:
    ENGS = ["pe", "act", "dve", "pool", "sp"]

    def __init__(self, nc, es):
        self.nc = nc
        self.es = es
        self.ops = {e: [] for e in self.ENGS}
        self.cnt = {e: 0 for e in self.ENGS}
        self.sems = {e: [] for e in self.ENGS}
        self.dma_sems = [es.enter_context(nc.semaphore(f"dq{i}")) for i in range(NDMASEM)]
        self.dma_n = 0
        self.dma_tokens = []
        self.lastw = {}
        self.readers = {}
        self.waited = {e: {} for e in self.ENGS}

    def _sem(self, e, idx):
        ep = idx // EPOCH
        while len(self.sems[e]) <= ep:
            self.sems[e].append(self.es.enter_context(self.nc.semaphore(f"s_{e}_{len(self.sems[e])}")))
        return self.sems[e][ep], idx % EPOCH + 1

    def op(self, eng, fn, r=(), w=(), dma=False):
        deps = []
        for k in r:
            if k in self.lastw:
                deps.append(self.lastw[k])
        for k in w:
            if k in self.lastw:
                deps.append(self.lastw[k])
            deps.extend(self.readers.get(k, []))
        if dma:
            i = self.dma_n
            self.dma_n += 1
            sem = self.dma_sems[i % NDMASEM]
            val = 16 * (i // NDMASEM + 1)
            if i >= NDMASEM:
                deps.append(self.dma_tokens[i - NDMASEM])
            tok = [sem, val, "dma", None]
            self.dma_tokens.append(tok)
            inc = (sem, 16)
        else:
            idx = self.cnt[eng]
            self.cnt[eng] += 1
            sem, val = self._sem(eng, idx)
            tok = [sem, val, eng, None]
            inc = (sem, 1)
        waits = []
        wd = self.waited[eng]
        for (s, v, te, vc) in deps:
            if te == "pe" and eng == "pe" and not dma:
                continue
            key = id(s)
            if wd.get(key, 0) >= v:
                continue
            wd[key] = v
            waits.append((s, v))
            if vc:
                for k2, v2 in vc.items():
                    if wd.get(k2, 0) < v2:
                        wd[k2] = v2
        tok[3] = dict(wd)
        tok = tuple(tok)
        if dma:
            self.dma_tokens[-1] = tok
        self.ops[eng].append((fn, waits, inc))
        for k in r:
            self.readers.setdefault(k, []).append(tok)
        for k in w:
            self.lastw[k] = tok
            self.readers[k] = []
        return tok

    def emit(self):
        nc = self.nc
        final = []
        for t in self.dma_tokens[-NDMASEM:]:
            final.append((t[0], t[1]))
        for e in self.ENGS:
            if self.cnt[e] > 0:
                final.append(self._sem(e, self.cnt[e] - 1))
        ops = self.ops

        def run(engh, lst, extra=None):
            for fn, waits, inc in lst:
                for (s, v) in waits:
                    engh.wait_ge(s, v)
                ins = fn(engh)
                ins.then_inc(inc[0], inc[1])
            if extra:
                for (s, v) in extra:
                    engh.wait_ge(s, v)

        with nc.Block() as block:
            @block.tensor
            def _(e):
                run(e, ops["pe"])

            @block.scalar
            def _(e):
                run(e, ops["act"])

            @block.vector
            def _(e):
                run(e, ops["dve"])

            @block.gpsimd
            def _(e):
                run(e, ops["pool"])

            @block.sync
            def _(e):
                run(e, ops["sp"], final)


def make_consts():
    c = {}
    c["ident"] = np.eye(128, dtype=np.float32)
    c["lst"] = np.tril(np.ones((128, 128), np.float32), -1)
    c["ones"] = np.ones((128, 128), np.float32)
    for name, TT, C in (("p", 128, 64), ("s", 64, 32)):
        idx = np.arange(TT)
        same = (idx[:, None] // C) == (idx[None, :] // C)
        tri = same & (idx[:, None] <= idx[None, :])
        us = same & (idx[None, :] > idx[:, None])
        ui = same & (idx[None, :] >= idx[:, None])
        ls = same & (idx[:, None] > idx[None, :])
        f = lambda a: a.astype(np.float32).copy()
        c["tri_" + name] = f(tri)
        c["blk_" + name] = f(same)
        c["us_" + name] = f(us)
        c["ui_" + name] = f(ui)
        c["ls_" + name] = f(ls)
        c["id_" + name] = f(np.eye(TT))
        c["tw_" + name] = f(2 * np.eye(TT))
        nch = TT // C
        ind = np.zeros((TT, nch, 128), np.float32)
        for ch in range(nch):
            ind[ch * C:(ch + 1) * C, ch, :] = 1.0
        c["ind_" + name] = ind
    c["tv"] = np.tile(np.arange(1, 65, dtype=np.float32)[None, :], (128, 1))
    m64 = np.ones((128, 64), np.float32)
    m64[:, 0] = 0.0
    c["m64"] = m64
    sg = np.ones((128, 4), np.float32)
    sg[:64, 0] = -1.0
    sg[64:, 1] = -1.0
    sg[:, 2] = -1.0
    c["sgn"] = sg
    psw = np.zeros((128, 128), np.float32)
    for n in range(64):
        psw[64 + n, n] = -1.0
        psw[n, 64 + n] = 1.0
    c["psw"] = psw
    rm = np.zeros((128, 8), np.float32)
    for g in range(8):
        rm[g * 16:(g + 1) * 16, g] = 1.0
    c["rowmask"] = rm
    return c


CONSTS = make_consts()

WSHAPES = {
    "norm_w": [1024], "w_in": [1024, 3080], "conv_w": [4, 1536], "dn_A_log": [4], "dn_dt_bias": [4],
    "dn_norm_w": [128], "s5_A_re": [32, 64], "s5_A_im": [32, 64], "s5_log_dt": [32],
    "s5_B_re": [32, 64, 16], "s5_B_im": [32, 64, 16], "s5_C_re": [32, 16, 64], "s5_C_im": [32, 16, 64],
    "s5_D": [512], "glu_w": [512, 512], "glu_b": [512], "w_out": [1024, 1024], "final_norm_w": [1024],
}


def build(TP, NS=2, dbg=(), NPRE=0):
    nc = bass.Bass("TRN2", target_bir_lowering=False)
    din = lambda n, s: nc.dram_tensor(n, list(s), F32, kind="ExternalInput").ap()
    dout = lambda n, s: nc.dram_tensor(n, list(s), F32, kind="ExternalOutput").ap()
    xp = din("xp", [TP, 1024]); xs = din("xs", [NS * 32, 1024])
    xpre = din("xpre", [NPRE, 1024]) if NPRE else None
    cconv = din("cconv", [NS, 3, 1536]); sdn = din("sdn", [NS, 4, 128, 128])
    sre = din("sre", [NS, 32, 64]); sim = din("sim", [NS, 32, 64])
    W = {k: din(k, v) for k, v in WSHAPES.items()}
    CD = {k: din("c_" + k, v.shape) for k, v in CONSTS.items()}
    yp = dout("yp", [TP, 1024]); ys = dout("ys", [NS * 32, 1024])
    o_convp = dout("convp", [3, 1536]); o_dnp = dout("dnp", [4, 128, 128])
    o_rep = dout("rep", [32, 64]); o_imp = dout("imp", [32, 64])
    o_convs = dout("convs", [NS, 3, 1536]); o_dns = dout("dns", [NS, 4, 128, 128])
    o_res = dout("res", [NS, 32, 64]); o_ims = dout("ims", [NS, 32, 64])
    dbg_mix = nc.dram_tensor("dbg_mix", [128, 1024], BF16, kind="ExternalOutput").ap() if dbg else None

    with ExitStack() as es:
        P = Prog(nc, es)
        _sbn = [0]

        def sb(shape, dt=F32, name=None):
            _sbn[0] += 1
            nm = name or f"t{_sbn[0]}"
            t = es.enter_context(nc.sbuf_tensor(nm, list(shape), dt))
            return t, nm

        banks = []
        for i in range(6):
            banks.append((es.enter_context(nc.psum_tensor(f"pb{i}", [128, 512], F32)), f"pb{i}"))
        bbanks = []
        for i in range(2):
            bbanks.append((es.enter_context(nc.psum_tensor(f"pbb{i}", [128, 1024], BF16)), f"pbb{i}"))
        _bi = [0, 0]
        _stream = [None]

        y5bank = banks.pop()

        def bank():
            st_ = _stream[0]
            if st_ is not None:
                st_["bi"] = (st_["bi"] + 1) % len(st_["banks"])
                return st_["banks"][st_["bi"]]
            _bi[0] = (_bi[0] + 1) % len(banks)
            return banks[_bi[0]]

        def begin_stream(bank_ids, bb=None):
            bl_ = [y5bank if i == "y5" else banks[i] for i in bank_ids]
            _stream[0] = dict(ops=[], banks=bl_, bi=0, bb=bb)
            return _stream[0]

        def end_stream():
            _stream[0] = None

        def merge_streams(sts):
            lists = [st_["ops"] for st_ in sts]
            idx = [0] * len(lists)
            total = sum(len(l) for l in lists)
            for _ in range(total):
                best, bv = None, None
                for i, l in enumerate(lists):
                    if idx[i] < len(l):
                        frac = idx[i] / len(l)
                        if bv is None or frac < bv:
                            best, bv = i, frac
                a = lists[best][idx[best]]
                idx[best] += 1
                P.op(a[0], a[1], r=a[2], w=a[3], dma=a[4])

        def bbank():
            st_ = _stream[0]
            if st_ is not None and st_.get("bb") is not None:
                return bbanks[st_["bb"]]
            _bi[1] = (_bi[1] + 1) % len(bbanks)
            return bbanks[_bi[1]]

        def op(eng, fn, r=(), w=(), dma=False):
            if _stream[0] is not None:
                _stream[0]["ops"].append((eng, fn, tuple(r), tuple(w), dma))
                return None
            return P.op(eng, fn, r=r, w=w, dma=dma)

        def dma(out, in_, r=(), w=(), slow=False):
            if slow:
                return op("sp", lambda e: e.dma_start(out=out, in_=in_, allow_slow_non_contiguous=True), r=r, w=w, dma=True)
            return op("sp", lambda e: e.dma_start(out=out, in_=in_), r=r, w=w, dma=True)

        def mm(out, lhsT, rhs, start, stop, r, w):
            return op("pe", lambda e: e.matmul(out, lhsT=lhsT, rhs=rhs, start=start, stop=stop), r=r, w=w)

        def tr(out, in_, ident, r, w):
            return op("pe", lambda e: e.transpose(out=out, in_=in_, identity=ident), r=r, w=w)

        def act(out, in_, func, r, w, **kw):
            return op("act", lambda e: e.activation(out=out, in_=in_, func=func, **kw), r=r, w=w)

        def tt(eng, out, in0, in1, o, r, w):
            return op(eng, lambda e: e.tensor_tensor(out=out, in0=in0, in1=in1, op=o), r=r, w=w)

        def ts(eng, out, in0, s1, s2, o0, o1, r, w):
            if o1 is None:
                return op(eng, lambda e: e.tensor_scalar(out=out, in0=in0, scalar1=s1, scalar2=None, op0=o0), r=r, w=w)
            return op(eng, lambda e: e.tensor_scalar(out=out, in0=in0, scalar1=s1, scalar2=s2, op0=o0, op1=o1), r=r, w=w)

        def stt(eng, out, in0, sc, in1, o0, o1, r, w):
            return op(eng, lambda e: e.scalar_tensor_tensor(out=out, in0=in0, scalar=sc, in1=in1, op0=o0, op1=o1), r=r, w=w)

        def cp(eng, out, in_, r, w):
            if eng == "act":
                return op("act", lambda e: e.copy(out=out, in_=in_), r=r, w=w)
            return op(eng, lambda e: e.tensor_copy(out=out, in_=in_), r=r, w=w)

        TA, TAK = sb([128, 512], F32, "TA")
        TB, TBK = sb([128, 512], F32, "TB")
        TC, TCK = sb([128, 512], F32, "TC")
        TD, TDK = sb([128, 512], F32, "TD")
        TE, TEK = sb([128, 512], F32, "TE")
        TF, TFK = sb([128, 512], F32, "TF")
        xt, xtK = sb([128, 1024], F32, "xt")
        XN, XNK = sb([128, 1024], F32, "XN")
        v4 = lambda t: t[:].rearrange("p (h e) -> p h e", h=4)

        C = {}
        for k, v in CONSTS.items():
            if k in ("ones",):
                dma(XN[:, 0:128], CD[k], w=[XNK])
                C[k] = (XN[:, 0:128], XNK)
                continue
            t, nm = sb(v.shape, F32, "k_" + k)
            dma(t[:], CD[k], w=[nm])
            C[k] = (t, nm)
        ident, identK = C["ident"]
        identb, identbK = sb([128, 128], BF16, "identb")
        cp("dve", identb[:], ident[:], [identK], [identbK])
        onesb, onesbK = sb([128, 128], BF16, "onesb")
        cp("dve", onesb[:], C["ones"][0], [C["ones"][1]], [onesbK])
        sgn, sgnK = C["sgn"]

        nwc, nwcK = sb([128, 8], F32, "nwc")
        for k in range(8):
            dma(nwc[:, k:k + 1], W["norm_w"][k * 128:(k + 1) * 128].rearrange("(p o) -> p o", o=1), w=[nwcK])
        winb, winbK = sb([128, 8, 3080], BF16, "winb")
        n_ = 0
        for k in range(8):
            for pc in range(4):
                c0, c1 = pc * 770, (pc + 1) * 770
                dma(XN[:, 0:770], W["w_in"][k * 128:(k + 1) * 128, c0:c1], w=[XNK])
                ts("dve" if n_ % 2 == 0 else "pool", winb[:, k, c0:c1], XN[:, 0:770], nwc[:, k:k + 1], None, ALU.mult, None, [XNK, nwcK], [winbK])
                n_ += 1
        woutb, woutbK = sb([128, 8, 1024], BF16, "woutb")
        for k in range(8):
            dma(XN[:, :], W["w_out"][k * 128:(k + 1) * 128, :], w=[XNK])
            cp("dve" if k % 2 == 0 else "pool", woutb[:, k, :], XN[:, :], [XNK], [woutbK])
        glub, glubK = sb([128, 4, 512], BF16, "glub")
        for k in range(4):
            dma(XN[:, 0:512], W["glu_w"][k * 128:(k + 1) * 128, :], w=[XNK])
            cp("dve", glub[:, k, :], XN[:, 0:512], [XNK], [glubK])
        fnwb, fnwbK = sb([128, 1024], F32, "fnwb")
        dma(fnwb[:], W["final_norm_w"].partition_broadcast(128), w=[fnwbK])
        glbb, glbbK = sb([128, 512], BF16, "glbb")
        dma(XN[:, 0:512], W["glu_b"].partition_broadcast(128), w=[XNK])
        cp("dve", glbb[:], XN[:, 0:512], [XNK], [glbbK])
        dnwb, dnwbK = sb([128, 128], F32, "dnwb")
        dma(dnwb[:], W["dn_norm_w"].partition_broadcast(128), w=[dnwbK])
        dtb, dtbK = sb([128, 4], F32, "dtb")
        dma(dtb[:], W["dn_dt_bias"].partition_broadcast(128), w=[dtbK])
        negA, negAK = sb([128, 4], F32, "negA")
        dma(negA[:], W["dn_A_log"].partition_broadcast(128), w=[negAK])
        act(negA[:], negA[:], AF.Exp, [negAK], [negAK])
        ts("dve", negA[:], negA[:], -1.0, None, ALU.mult, None, [negAK], [negAK])
        acc, accK = sb([128, 4, 128], F32, "acc")
        accKs = [accK + str(t) for t in range(4)]
        cw4, cw4K = None, None
        CW, CWK = sb([128, 12, 4], F32, "CW")
        for t3 in range(3):
            dma(XN[0:4, 0:512], W["conv_w"][:, t3 * 512:(t3 + 1) * 512], w=[XNK])
            for t4 in range(4):
                t = t3 * 4 + t4
                bk, bkK = bank()
                tr(bk[:, 0:4], XN[0:4, t4 * 128:(t4 + 1) * 128], ident[0:4, 0:4], [XNK, identK], [bkK])
                cp("dve", CW[:, t, :], bk[:, 0:4], [bkK], [CWK])

        PA, PAK = sb([128, 8, 128], F32, "PA")
        PB, PBK = sb([128, 8, 128], F32, "PB")
        KI, KIK = sb([128, 512], I32, "KI")
        COS, COSK = sb([128, 32, 64], F32, "COS")
        SIN, SINK = sb([128, 32, 64], F32, "SIN")

        def reduce_sin(out, arg, tmp, ki, keys, shift):
            (oK, aK, tK, kK) = keys
            ts("dve", tmp, arg, shift + 64 * PI, 1.0 / (2 * PI), ALU.add, ALU.mult, [aK], [tK])
            cp("dve", ki, tmp, [tK], [kK])
            cp("dve", tmp, ki, [kK], [tK])
            ts("dve", tmp, tmp, -2 * PI, shift + 64 * PI, ALU.mult, ALU.add, [tK], [tK])
            tt("dve", tmp, tmp, arg, ALU.add, [tK, aK], [tK])
            ts("dve", tmp, tmp, -3.1415925, 3.1415925, ALU.max, ALU.min, [tK], [tK])
            act(out, tmp, AF.Sin, [tK], [oK])

        ar2, ar2K = sb([32, 128], F32, "ar2")
        ai2, ai2K = sb([32, 128], F32, "ai2")
        for h in range(2):
            dma(ar2[:, h * 64:(h + 1) * 64], W["s5_A_re"], w=[ar2K])
            dma(ai2[:, h * 64:(h + 1) * 64], W["s5_A_im"], w=[ai2K])
        sm = {}
        for nm in ["lre", "lim", "dt", "ldr", "th", "rho", "c0", "s0", "t0", "t1", "lbr", "lbi", "den", "fre", "fim",
                   "FIMS", "FRES", "t2"]:
            sm[nm] = sb([128, 32], F32, "s5_" + nm)
        bk, bkK = bank()
        tr(bk[:, 0:32], ar2[0:32, :], ident[0:32, 0:32], [ar2K, identK], [bkK])
        ts("dve", sm["lre"][0][:], bk[:, 0:32], -1e-4, None, ALU.min, None, [bkK], [sm["lre"][1]])
        bk, bkK = bank()
        tr(bk[:, 0:32], ai2[0:32, :], ident[0:32, 0:32], [ai2K, identK], [bkK])
        cp("dve", sm["lim"][0][:], bk[:, 0:32], [bkK], [sm["lim"][1]])
        dma(sm["dt"][0][:], W["s5_log_dt"].partition_broadcast(128), w=[sm["dt"][1]])
        act(sm["dt"][0][:], sm["dt"][0][:], AF.Exp, [sm["dt"][1]], [sm["dt"][1]])
        S = lambda n: sm[n][0][:]
        K_ = lambda n: sm[n][1]
        tt("dve", S("ldr"), S("lre"), S("dt"), ALU.mult, [K_("lre"), K_("dt")], [K_("ldr")])
        tt("dve", S("th"), S("lim"), S("dt"), ALU.mult, [K_("lim"), K_("dt")], [K_("th")])
        act(S("rho"), S("ldr"), AF.Exp, [K_("ldr")], [K_("rho")])
        reduce_sin(S("s0"), S("th"), S("t0"), KI[:, 0:32], (K_("s0"), K_("th"), K_("t0"), KIK), 0.0)
        reduce_sin(S("c0"), S("th"), S("t0"), KI[:, 0:32], (K_("c0"), K_("th"), K_("t0"), KIK), PI / 2)
        tt("dve", S("lbr"), S("rho"), S("c0"), ALU.mult, [K_("rho"), K_("c0")], [K_("lbr")])
        tt("dve", S("lbi"), S("rho"), S("s0"), ALU.mult, [K_("rho"), K_("s0")], [K_("lbi")])
        tt("dve", S("den"), S("lre"), S("lre"), ALU.mult, [K_("lre")], [K_("den")])
        tt("dve", S("t0"), S("lim"), S("lim"), ALU.mult, [K_("lim")], [K_("t0")])
        tt("dve", S("den"), S("den"), S("t0"), ALU.add, [K_("den"), K_("t0")], [K_("den")])
        op("dve", lambda e: e.reciprocal(out=S("den"), in_=S("den")), r=[K_("den")], w=[K_("den")])
        ts("dve", S("t1"), S("lbr"), -1.0, None, ALU.add, None, [K_("lbr")], [K_("t1")])
        tt("dve", S("t0"), S("t1"), S("lre"), ALU.mult, [K_("t1"), K_("lre")], [K_("t0")])
        tt("dve", S("t2"), S("lbi"), S("lim"), ALU.mult, [K_("lbi"), K_("lim")], [K_("t2")])
        tt("dve", S("t0"), S("t0"), S("t2"), ALU.add, [K_("t0"), K_("t2")], [K_("t0")])
        tt("dve", S("fre"), S("t0"), S("den"), ALU.mult, [K_("t0"), K_("den")], [K_("fre")])
        tt("dve", S("t0"), S("lbi"), S("lre"), ALU.mult, [K_("lbi"), K_("lre")], [K_("t0")])
        tt("dve", S("t2"), S("t1"), S("lim"), ALU.mult, [K_("t1"), K_("lim")], [K_("t2")])
        tt("dve", S("t0"), S("t0"), S("t2"), ALU.subtract, [K_("t0"), K_("t2")], [K_("t0")])
        tt("dve", S("fim"), S("t0"), S("den"), ALU.mult, [K_("t0"), K_("den")], [K_("fim")])
        ts("dve", S("FIMS"), S("fim"), sgn[:, 0:1], None, ALU.mult, None, [K_("fim"), sgnK], [K_("FIMS")])
        ts("dve", S("FRES"), S("fre"), sgn[:, 1:2], None, ALU.mult, None, [K_("fre"), sgnK], [K_("FRES")])
        g16 = lambda t: t[:].rearrange("p (g c) -> p g c", g=32)
        Bst, BstK, Bsw, BswK, bb, bbK, bbs, bbsK, btmp, btmpK = g16(TA), TAK, g16(TB), TBK, g16(TC), TCK, g16(TD), TDK, g16(TE), TEK
        bre = W["s5_B_re"].rearrange("g n c -> n g c")
        bim = W["s5_B_im"].rearrange("g n c -> n g c")
        dma(Bst[0:64], bre, w=[BstK]); dma(Bst[64:128], bim, w=[BstK])
        dma(Bsw[0:64], bim, w=[BswK]); dma(Bsw[64:128], bre, w=[BswK])
        bc = lambda nm: sm[nm][0][:].unsqueeze(2).to_broadcast([128, 32, 16])
        tt("dve", bb, Bst, bc("fre"), ALU.mult, [BstK, K_("fre")], [bbK])
        tt("dve", btmp, Bsw, bc("FIMS"), ALU.mult, [BswK, K_("FIMS")], [btmpK])
        tt("dve", bb, bb, btmp, ALU.add, [bbK, btmpK], [bbK])
        tt("dve", bbs, Bsw, bc("FRES"), ALU.mult, [BswK, K_("FRES")], [bbsK])
        tt("dve", btmp, Bst, bc("fim"), ALU.mult, [BstK, K_("fim")], [btmpK])
        tt("dve", bbs, bbs, btmp, ALU.add, [bbsK, btmpK], [bbsK])
        BBn, BBnK = sb([128, 4, 4, 128], BF16, "BBn")
        BBs, BBsK = sb([128, 4, 4, 128], BF16, "BBs")
        rmask, rmaskK = C["rowmask"]
        for (src, srcK, dst, dstK) in ((bb, bbK, BBn, BBnK), (bbs, bbsK, BBs, BBsK)):
            for q in range(4):
                bk, bkK = bank()
                tr(bk[:, 0:128], src[:, 8 * q:8 * q + 8, :].rearrange("p g c -> p (g c)"), ident[:, :], [srcK, identK], [bkK])
                for g in range(8):
                    hf, j = g // 4, g % 4
                    ts("dve", dst[64 * hf:64 * hf + 64, q, j, :], bk[64 * hf:64 * hf + 64, 0:128], rmask[64 * hf:64 * hf + 64, g:g + 1], None, ALU.mult, None, [bkK, rmaskK], [dstK])
        tv, tvK = C["tv"]
        PAf = PA[:].rearrange("p g t -> p (g t)")
        PBf = PB[:].rearrange("p g t -> p (g t)")
        for q4 in range(4):
            argv = PAf[:, 0:512].rearrange("p (g t) -> p g t", g=8)
            tt("dve", argv, sm["th"][0][:, 8 * q4:8 * q4 + 8].unsqueeze(2).to_broadcast([128, 8, 64]),
               tv[:].unsqueeze(1).to_broadcast([128, 8, 64]), ALU.mult, [K_("th"), tvK], [PAK])
            reduce_sin(SIN[:, 8 * q4:8 * q4 + 8, :].rearrange("p g t -> p (g t)"), PAf[:, 0:512], PBf[:, 0:512], KI[:, :], (SINK, PAK, PBK, KIK), 0.0)
            reduce_sin(COS[:, 8 * q4:8 * q4 + 8, :].rearrange("p g t -> p (g t)"), PAf[:, 0:512], PBf[:, 0:512], KI[:, :], (COSK, PAK, PBK, KIK), PI / 2)
        Md1, Md1K = sb([128, 512], BF16, "Md1")
        Md2, Md2K = sb([128, 512], BF16, "Md2")
        cri, criK = xt[:].rearrange("p (q n) -> p q n", q=8), xtK
        cre = W["s5_C_re"].rearrange("g c n -> (g c) n")
        cim = W["s5_C_im"].rearrange("g c n -> (g c) n")
        for q in range(4):
            dma(cri[:, q, 0:64], cre[q * 128:(q + 1) * 128, :], w=[criK])
            dma(cri[:, q, 64:128], cim[q * 128:(q + 1) * 128, :], w=[criK])
            dma(cri[:, 4 + q, 0:64], cim[q * 128:(q + 1) * 128, :], w=[criK])
            dma(cri[:, 4 + q, 64:128], cre[q * 128:(q + 1) * 128, :], w=[criK])
        for q in range(4):
            bk, bkK = bank()
            tr(bk[:, 0:128], cri[:, q, :], ident[:, :], [criK, identK], [bkK])
            ts("dve", Md1[:, q * 128:(q + 1) * 128], bk[:, 0:128], sgn[:, 1:2], None, ALU.mult, None, [bkK, sgnK], [Md1K])
            bk, bkK = bank()
            tr(bk[:, 0:128], cri[:, 4 + q, :], ident[:, :], [criK, identK], [bkK])
            ts("dve", Md2[:, q * 128:(q + 1) * 128], bk[:, 0:128], -1.0, None, ALU.mult, None, [bkK], [Md2K])
        dcol, dcolK = sb([128, 4], F32, "dcol")
        for q in range(4):
            dma(dcol[:, q:q + 1], W["s5_D"][q * 128:(q + 1) * 128].rearrange("(p o) -> p o", o=1), w=[dcolK])
        diagD, diagDK = sb([128, 4, 128], BF16, "diagD")
        for q in range(4):
            ts("dve", diagD[:, q, :], ident[:, :], dcol[:, q:q + 1], None, ALU.mult, None, [identK, dcolK], [diagDK])
        psw, pswK = C["psw"]
        rho, rhoK = sm["rho"]
        RC, RCK = sb([128, 32], F32, "RC"); RS, RSK = sb([128, 32], F32, "RS")
        tt("dve", RC[:], COS[:, :, 63], rho[:], ALU.mult, [COSK, rhoK], [RCK])
        tt("dve", RS[:], SIN[:, :, 63], rho[:], ALU.mult, [SINK, rhoK], [RSK])
        RHOT, RHOTK = sb([128, 32, 64], F32, "RHOT")
        tt("dve", RHOT[:], rho[:].unsqueeze(2).to_broadcast([128, 32, 64]), C["m64"][0][:].unsqueeze(1).to_broadcast([128, 32, 64]), ALU.mult, [rhoK, C["m64"][1]], [RHOTK])

        NST = 1 + NS
        Sst = [sb([128, 4, 128], F32, f"Sst{i}") for i in range(NST)]
        Sbf = [sb([128, 4, 128], BF16, f"Sbf{i}") for i in range(NST)]
        XS = [sb([128, 32], F32, f"XS{i}") for i in range(NST)]
        op("pool", lambda e: e.memset(Sst[0][0][:], 0.0), w=[Sst[0][1]])
        op("pool", lambda e: e.memset(Sbf[0][0][:], 0.0), w=[Sbf[0][1]])
        XS0K = [f"XS0_{e}" for e in range(8)]
        op("pool", lambda e: e.memset(XS[0][0][:], 0.0), w=XS0K)
        xs2, xs2K = sb([32, 128], F32, "xs2")
        for s in range(NS):
            dma(Sst[1 + s][0][:], sdn[s].rearrange("h d e -> d h e"), w=[Sst[1 + s][1]])
            cp("pool", Sbf[1 + s][0][:], Sst[1 + s][0][:], [Sst[1 + s][1]], [Sbf[1 + s][1]])
            dma(xs2[:, 0:64], sre[s], w=[xs2K]); dma(xs2[:, 64:128], sim[s], w=[xs2K])
            bk, bkK = bank()
            tr(bk[:, 0:32], xs2[0:32, :], ident[0:32, 0:32], [xs2K, identK], [bkK])
            cp("dve", XS[1 + s][0][:], bk[:, 0:32], [bkK], [XS[1 + s][1]])

        sc = {}
        for nm in ["ss", "rstd", "ss2", "r2"]:
            sc[nm] = sb([128, 1], F32, "sc_" + nm)
        hb, hbK = sb([128, 1024], BF16, "hb")
        MIX, MIXK = KI[:].bitcast(BF16), KIK
        hT, hTK = sb([128, 8, 128], BF16, "hT")
        MIXT, MIXTK = sb([128, 8, 128], BF16, "MIXT")
        RAWp, RAWpK = sb([128, 12, 1, 131], F32, "RAWp")
        RAWs = RAWp[:].rearrange("p a b c -> p (a b c)")[:, 0:12 * NS * 35].rearrange("p (a b c) -> p a b c", a=12, b=NS)
        RAWsK = RAWpK
        for s in range(NS):
            for t in range(12):
                dma(RAWs[:, t, s, 0:3], cconv[s, :, t * 128:(t + 1) * 128].rearrange("r c -> c r"), w=[RAWsK], slow=True)
        QKVs = [sb([128, 12, 128], BF16, f"QKV{i}") for i in range(2)]
        uTs = [sb([128, 4, 128], BF16, f"uT{i}") for i in range(2)]
        GT8s = [sb([128, 8], F32, f"GT8_{i}") for i in range(2)]
        SZDs = [sb([128, 512], BF16, f"SZD{i}") for i in range(2)]
        SZSs = [sb([128, 512], BF16, f"SZS{i}") for i in range(2)]
        g4 = {}
        for nm in ["GT8", "BETA", "LNB", "SQB", "RSB", "XA", "G", "GC", "GD", "EGC", "EDEC", "SSK", "RK", "S1", "S2", "S3", "SSO", "RO"]:
            g4[nm] = sb([128, 8 if nm == "GT8" else 4], F32, "g_" + nm)
        EGL, EGLK = sb([128, 2, 4], F32, "EGL")
        Rg, RgK = v4(TA), TAK
        RQ, RQK = v4(TA), TAK
        OT, OTK = v4(TA), TAK
        EU, EUK = v4(TB), TBK
        X32, X32K = v4(TB), TBK
        Y5, Y5K = TB, TBK
        EL, ELK = v4(TC), TCK
        Z32, Z32K = v4(TC), TCK
        G1, G1K = TC, TCK
        EUI, EUIK = v4(TD), TDK
        OO, OOK = v4(TD), TDK
        Mt, MtK = v4(TE), TEK
        G2, G2K = TE, TEK
        VT, VTK = v4(TF), TFK
        GY, GYK = TF, TFK
        QHT, QHTK = sb([128, 4, 128], BF16, "QHT")
        KTk, KTkK = sb([128, 4, 128], BF16, "KTk")
        QSQ, QSQK = KTk, KTkK
        KG, KGK = sb([128, 4, 128], BF16, "KG"); KDEC, KDECK = sb([128, 4, 128], BF16, "KDEC")
        KTT, KTTK = sb([128, 4, 128], BF16, "KTT"); KGT, KGTK = sb([128, 4, 128], BF16, "KGT")
        Xb, XbK = sb([128, 4, 128], BF16, "Xb"); Mtb, MtbK = sb([128, 4, 128], BF16, "Mtb")
        Zb, ZbK = sb([128, 4, 128], BF16, "Zb")
        XF, XFK = sb([128, 4, 128], BF16, "XF")
        ATT, ATTK = sb([128, 4, 128], BF16, "ATT")
        Rr, RrK = sb([128, 4, 128], BF16, "Rr"); VN, VNK = sb([128, 4, 128], BF16, "VN")
        GYb, GYbK = Rr[:].rearrange("p h e -> p (h e)"), RrK
        GYT, GYTK = VN, VNK
        Ab, AbK = sb([128, 8, 128], BF16, "Ab"); Bb, BbK = sb([128, 8, 128], BF16, "Bb")
        AL, ALK = sb([128, 8], F32, "AL"); BL, BLK = sb([128, 8], F32, "BL")
        AL2, _ = sb([128, 4, 4], F32, "AL2"); BL2, _ = sb([128, 4, 4], F32, "BL2"); FX2, _ = sb([128, 4, 4], F32, "FX2")
        xso, xsoK = XN[0:32, 0:128], XNK
        def rsqrt_chain(outk, ink, scale, eps):
            o, oK = outk
            i_, iK = ink
            ts("dve", o, i_, scale, eps, ALU.mult, ALU.add, [iK], [oK])
            act(o, o, AF.Ln, [oK], [oK])
            act(o, o, AF.Exp, [oK], [oK], scale=-0.5)

        def do_tile(kind, ti, xsrc, ydst, mode="full", b=0, part="all"):
            if part == "tail":
                tile_tail(kind, ti, ydst, b, 128 if kind == "p" else NS * 32)
                return None
            full = mode == "full"
            needq = mode != "state"
            QKV, QKVK = QKVs[b]
            uT, uTK = uTs[b]
            SZD, SZDK = SZDs[b]
            SZS, SZSK = SZSs[b]
            if kind == "p":
                TT, Cc, nseg, L, RAW, RAWK = 128, 64, 1, 128, RAWp, RAWpK
                chunks = [(0, 0), (1, 0)]
                segs = [(0, 0), (1, 0)]
                SL = 64
            else:
                TT, Cc, nseg, L, RAW, RAWK = NS * 32, 32, NS, 32, RAWs, RAWsK
                chunks = [(s, 1 + s) for s in range(NS)]
                segs = [(s, 1 + s) for s in range(NS)]
                SL = 32
            nsg = TT // SL
            cn = lambda n: C[n + "_" + kind]
            m4 = lambda n: cn(n)[0][:TT, :TT].unsqueeze(1).to_broadcast([TT, 4, TT])
            x, xK = xt, xtK
            if part in ("all", "front"):
                x, xK = xt, xtK
                dma(x[:TT, :], xsrc, w=[xK])
                ss, ssK = sc["ss"]; rstd, rstdK = sc["rstd"]
                act(hb[:TT, :], x[:TT, :], AF.Square, [xK], [hbK, ssK], accum_out=ss[:TT, :])
                rsqrt_chain((rstd[:TT, :], rstdK), (ss[:TT, :], ssK), 1.0 / 1024, 1e-6)
                ts("dve", hb[:TT, :], x[:TT, :], rstd[:TT, 0:1], None, ALU.mult, None, [xK, rstdK], [hbK])
                pb, pbK = bbank()
                pbv = pb[:].rearrange("p (k t) -> p k t", k=8)
                for k in range(8):
                    tr(pbv[:, k, :TT], hb[:TT, k * 128:(k + 1) * 128], identb[:TT, :TT], [hbK, identbK], [pbK])
                cp("act", hT[:, :, :TT], pbv[:, :, :TT], [pbK], [hTK])
                for grp in range(4):
                    if grp == 0 and not needq:
                        continue
                    bk, bkK = bank()
                    bv = bk[:].rearrange("p (j t) -> p j t", j=4)
                    c0 = grp * 512 if grp < 3 else 2056
                    for j in range(4):
                        for k in range(8):
                            mm(bv[:, j, :TT], winb[:, k, c0 + j * 128:c0 + (j + 1) * 128], hT[:, k, :TT], k == 0, k == 7, [winbK, hTK], [bkK])
                    if grp < 3:
                        for s in range(nseg):
                            cp("act", RAW[:, grp * 4:(grp + 1) * 4, s, 3:3 + L], bv[:, :, s * L:(s + 1) * L], [bkK], [RAWK])
                    else:
                        cp("act", uT[:, :, :TT], bv[:, :, :TT], [bkK], [uTK])
                if full:
                    bk, bkK = bank()
                    for k in range(8):
                        mm(bk[:TT, :], hT[:, k, :TT], winb[:, k, 1544:2056], k == 0, k == 7, [winbK, hTK], [bkK])
                    act(SZD[:TT, :], bk[:TT, :], AF.Silu, [bkK], [SZDK])
                    bk, bkK = bank()
                    for k in range(8):
                        mm(bk[:TT, :], hT[:, k, :TT], winb[:, k, 2568:3080], k == 0, k == 7, [winbK, hTK], [bkK])
                    act(SZS[:TT, :], bk[:TT, :], AF.Silu, [bkK], [SZSK])
                bk, bkK = bank()
                for k in range(8):
                    mm(bk[:TT, 0:8], hT[:, k, :TT], winb[:, k, 1536:1544], k == 0, k == 7, [winbK, hTK], [bkK])
                GT8, GT8K = GT8s[b]
                cp("dve", GT8[:TT, :], bk[:TT, 0:8], [bkK], [GT8K])
                for t3 in range(3):
                    if t3 == 0 and not full:
                        continue
                    for t4 in range(4):
                        t = t3 * 4 + t4
                        av = acc[:, t4, :TT].rearrange("p (s l) -> p s l", s=nseg)
                        ts("dve", av, RAW[:, t, :, 0:L], CW[:, t, 0:1], None, ALU.mult, None, [RAWK, CWK], [accKs[t4]])
                        for j in range(1, 4):
                            stt("dve", av, RAW[:, t, :, j:j + L], CW[:, t, j:j + 1], av, ALU.mult, ALU.add, [RAWK, CWK, accKs[t4]], [accKs[t4]])
                    act(QKV[:, t3 * 4:(t3 + 1) * 4, :TT], acc[:, :, :TT], AF.Silu, accKs, [QKVK])
                if kind == "p":
                    cp("pool", RAW[:, :, 0, 0:3], RAW[:, :, 0, L:L + 3], [RAWK], [RAWK])
            if part == "front":
                return None
            if full:
                cp("pool", XN[:TT, :], x[:TT, :], [xK], [XNK])
            GT8, GT8K = GT8s[b]
            stG = begin_stream([0, 1], bb=1)
            G_ = lambda n: g4[n][0][:TT, :]
            GK = lambda n: g4[n][1]
            act(G_("BETA"), GT8[:TT, 4:8], AF.Exp, [GT8K], [GK("BETA")], scale=-1.0)
            act(G_("LNB"), G_("BETA"), AF.Ln, [GK("BETA")], [GK("LNB")], bias=1.0)
            act(G_("SQB"), G_("LNB"), AF.Exp, [GK("LNB")], [GK("SQB")], scale=-0.5)
            act(G_("RSB"), G_("LNB"), AF.Exp, [GK("LNB")], [GK("RSB")], scale=0.5)
            tt("dve", G_("XA"), GT8[:TT, 0:4], dtb[:TT, :], ALU.add, [GT8K, dtbK], [GK("XA")])
            act(G_("XA"), G_("XA"), AF.Exp, [GK("XA")], [GK("XA")])
            act(G_("XA"), G_("XA"), AF.Ln, [GK("XA")], [GK("XA")], bias=1.0)
            tt("dve", G_("G"), G_("XA"), negA[:TT, :], ALU.mult, [GK("XA"), negAK], [GK("G")])
            bk, bkK = bank()
            mm(bk[:TT, 0:4], cn("tri")[0][:TT, :TT], G_("G"), True, True, [cn("tri")[1], GK("G")], [bkK])
            mm(bk[:TT, 4:8], cn("blk")[0][:TT, :TT], G_("G"), True, True, [cn("blk")[1], GK("G")], [bkK])
            nch = len(chunks)
            for c in range(nch):
                mm(bk[:, 8 + 4 * c:12 + 4 * c], cn("ind")[0][:TT, c, :], G_("G"), True, True, [cn("ind")[1], GK("G")], [bkK])
            cp("dve", G_("GC"), bk[:TT, 0:4], [bkK], [GK("GC")])
            tt("dve", G_("GD"), bk[:TT, 4:8], G_("GC"), ALU.subtract, [bkK, GK("GC")], [GK("GD")])
            act(G_("EGC"), G_("GC"), AF.Exp, [GK("GC")], [GK("EGC")])
            act(G_("EDEC"), G_("GD"), AF.Exp, [GK("GD")], [GK("EDEC")])
            act(EGL[:, 0:nch, :], bk[:, 8:8 + 4 * nch].rearrange("p (c h) -> p c h", c=nch), AF.Exp, [bkK], [EGLK])
            tt("dve", Rg[:TT, :, :TT], m4("tri"), G_("G").unsqueeze(2).to_broadcast([TT, 4, TT]), ALU.mult, [cn("tri")[1], GK("G")], [RgK])
            bu, buK = bank()
            bl, blK = bank()
            buv = bu[:].rearrange("p (h t) -> p h t", h=4)
            blv = bl[:].rearrange("p (h t) -> p h t", h=4)
            lst, lstK = C["lst"]
            for h in range(4):
                mm(buv[:TT, h, :TT], lst[:TT, :TT], Rg[:TT, h, :TT], True, True, [lstK, RgK], [buK])
                mm(blv[:TT, h, :TT], Rg[:TT, h, :TT], lst[:TT, :TT], True, True, [lstK, RgK], [blK])
            act(EU[:TT, :, :TT], buv[:TT, :, :TT], AF.Exp, [buK], [EUK])
            act(EL[:TT, :, :TT], blv[:TT, :, :TT], AF.Exp, [blK], [ELK])
            if full:
                tt("pool", EUI[:TT, :, :TT], EU[:TT, :, :TT], m4("ui"), ALU.mult, [EUK, cn("ui")[1]], [EUIK])
            tt("pool", EU[:TT, :, :TT], EU[:TT, :, :TT], m4("us"), ALU.mult, [EUK, cn("us")[1]], [EUK])
            tt("pool", EL[:TT, :, :TT], EL[:TT, :, :TT], m4("ls"), ALU.mult, [ELK, cn("ls")[1]], [ELK])
            if full:
                act(QSQ[:, :, :TT], QKV[:, 0:4, :TT], AF.Square, [QKVK], [QSQK])
                bk, bkK = bank()
                bv = bk[:].rearrange("p (h t) -> p h t", h=4)
                for h in range(4):
                    mm(bv[:, h, :TT], onesb[:, :], QSQ[:, h, :TT], True, True, [onesbK, QSQK], [bkK])
                ts("dve", RQ[:, :, :TT], bv[:, :, :TT], 1e-6, None, ALU.add, None, [bkK], [RQK])
                act(RQ[:, :, :TT], RQ[:, :, :TT], AF.Ln, [RQK], [RQK])
                act(RQ[:, :, :TT], RQ[:, :, :TT], AF.Exp, [RQK], [RQK], scale=-0.5)
                stt("dve", QHT[:, :, :TT], QKV[:, 0:4, :TT], 128.0 ** -0.5, RQ[:, :, :TT], ALU.mult, ALU.mult, [QKVK, RQK], [QHTK])
            pk, pkK = bbank()
            pkv = pk[:].rearrange("p (h d) -> p h d", h=8)
            for h in range(4):
                tr(pkv[:TT, h, :], QKV[:, 4 + h, :TT], identb[:, :], [QKVK, identbK], [pkK])
                tr(pkv[:TT, 4 + h, :], QKV[:, 8 + h, :TT], identb[:, :], [QKVK, identbK], [pkK])
            SSK, SSKK = g4["SSK"]
            for h in range(4):
                act(Rr[:TT, 0, :], pkv[:TT, h, :], AF.Square, [pkK], [RrK, SSKK], accum_out=SSK[:TT, h:h + 1])
            rsqrt_chain((G_("RK"), GK("RK")), (G_("SSK"), SSKK), 1.0, 1e-6)
            tt("dve", G_("S1"), G_("RK"), G_("SQB"), ALU.mult, [GK("RK"), GK("SQB")], [GK("S1")])
            tt("dve", G_("S2"), G_("S1"), G_("EGC"), ALU.mult, [GK("S1"), GK("EGC")], [GK("S2")])
            tt("dve", G_("S3"), G_("RK"), G_("EDEC"), ALU.mult, [GK("RK"), GK("EDEC")], [GK("S3")])
            b4 = lambda n: g4[n][0][:TT, :].unsqueeze(2).to_broadcast([TT, 4, 128])
            tt("dve", KTk[:TT], pkv[:TT, 0:4, :], b4("S1"), ALU.mult, [pkK, GK("S1")], [KTkK])
            tt("dve", KG[:TT], pkv[:TT, 0:4, :], b4("S2"), ALU.mult, [pkK, GK("S2")], [KGK])
            tt("dve", KDEC[:TT], pkv[:TT, 0:4, :], b4("S3"), ALU.mult, [pkK, GK("S3")], [KDECK])
            tt("dve", VT[:TT], pkv[:TT, 4:8, :], b4("SQB"), ALU.mult, [pkK, GK("SQB")], [VTK])
            pk2, pk2K = bbank()
            pk2v = pk2[:].rearrange("p (h t) -> p h t", h=8)
            for h in range(4):
                tr(pk2v[:, h, :TT], KTk[:TT, h, :], identb[:TT, :TT], [KTkK, identbK], [pk2K])
                tr(pk2v[:, 4 + h, :TT], KG[:TT, h, :], identb[:TT, :TT], [KGK, identbK], [pk2K])
            cp("act", KTT[:, :, :TT], pk2v[:, 0:4, :TT], [pk2K], [KTTK])
            cp("act", KGT[:, :, :TT], pk2v[:, 4:8, :TT], [pk2K], [KGTK])
            bkk, bkkK = bank()
            bqk, bqkK = bank()
            kkv = bkk[:].rearrange("p (h t) -> p h t", h=4)
            qkv_ = bqk[:].rearrange("p (h t) -> p h t", h=4)
            for h in range(4):
                mm(kkv[:TT, h, :TT], KTT[:, h, :TT], KTT[:, h, :TT], True, True, [KTTK], [bkkK])
                if full:
                    mm(qkv_[:TT, h, :TT], KTT[:, h, :TT], QHT[:, h, :TT], True, True, [KTTK, QHTK], [bqkK])
            tt("dve", Mt[:TT, :, :TT], kkv[:TT, :, :TT], EL[:TT, :, :TT], ALU.mult, [bkkK, ELK], [MtK])
            tt("pool", Mt[:TT, :, :TT], Mt[:TT, :, :TT], m4("id"), ALU.add, [MtK, cn("id")[1]], [MtK])
            cp("pool", Mtb[:TT, :, :TT], Mt[:TT, :, :TT], [MtK], [MtbK])
            tt("dve", EU[:TT, :, :TT], kkv[:TT, :, :TT], EU[:TT, :, :TT], ALU.mult, [bkkK, EUK], [EUK])
            tt("pool", Xb[:TT, :, :TT], m4("id"), EU[:TT, :, :TT], ALU.subtract, [cn("id")[1], EUK], [XbK])
            RSB, RSBK = g4["RSB"]
            for h in range(4 if full else 0):
                stt("dve", ATT[:TT, h, :TT], qkv_[:TT, h, :TT], RSB[:TT, h:h + 1], EUI[:TT, h, :TT], ALU.mult, ALU.mult, [bqkK, RSBK, EUIK], [ATTK])
            for it in range(5):
                last = it == 4
                Xo, XoK, Mo, MoK = (X32, X32K, Mt, MtK) if last else (Xb, XbK, Mtb, MtbK)
                Zo, ZoK = (Z32, Z32K) if last else (Zb, ZbK)
                by, byK = bank()
                byv = by[:].rearrange("p (h t) -> p h t", h=4)
                for h in range(4):
                    mm(byv[:TT, h, :TT], Xo[:TT, h, :TT], Mo[:TT, h, :TT], True, True, [XoK, MoK], [byK])
                stt("dve", Zo[:TT, :, :TT], byv[:TT, :, :TT], -1.0, cn("tw")[0][:TT, :TT].unsqueeze(1).to_broadcast([TT, 4, TT]), ALU.mult, ALU.add, [byK, cn("tw")[1]], [ZoK])
                bx, bxK = bank()
                bxv = bx[:].rearrange("p (h t) -> p h t", h=4)
                for h in range(4):
                    mm(bxv[:TT, h, :TT], Zo[:TT, h, :TT], Xo[:TT, h, :TT], True, True, [ZoK, XoK], [bxK])
                if it < 3:
                    cp("act", Xb[:TT, :, :TT], bxv[:TT, :, :TT], [bxK], [XbK])
                elif it == 3:
                    cp("act", X32[:TT, :, :TT], bxv[:TT, :, :TT], [bxK], [X32K])
                else:
                    cp("act", XF[:TT, :, :TT], bxv[:TT, :, :TT], [bxK], [XFK])
            for (c, st) in chunks:
                r0, r1 = c * Cc, (c + 1) * Cc
                Sm, SmK = Sst[st]
                Sb_, SbK = Sbf[st]
                ba, baK = bank(); bq, bqK = bank()
                bav = ba[:].rearrange("p (h e) -> p h e", h=4)
                bqv = bq[:].rearrange("p (h e) -> p h e", h=4)
                for h in range(4):
                    mm(bav[:TT, h, :], KGT[:, h, :TT], Sb_[:, h, :], True, True, [KGTK, SbK], [baK])
                    if full:
                        mm(bqv[:TT, h, :], QHT[:, h, :TT], Sb_[:, h, :], True, True, [QHTK, SbK], [bqK])
                tt("dve", Rr[r0:r1], VT[r0:r1], bav[r0:r1], ALU.subtract, [VTK, baK], [RrK])
                if full:
                    tt("dve", OO[r0:r1], bqv[r0:r1], g4["EGC"][0][r0:r1, :].unsqueeze(2).to_broadcast([Cc, 4, 128]), ALU.mult, [bqK, GK("EGC")], [OOK])
                bc_, bcK = bank()
                bcv = bc_[:].rearrange("p (h e) -> p h e", h=4)
                for h in range(4):
                    mm(bcv[:TT, h, :], XF[r0:r1, h, :TT], Rr[r0:r1, h, :], True, True, [XFK, RrK], [bcK])
                tt("dve", VN[r0:r1], bcv[r0:r1], g4["SQB"][0][r0:r1, :].unsqueeze(2).to_broadcast([Cc, 4, 128]), ALU.mult, [bcK, GK("SQB")], [VNK])
                bd, bdK = bank(); be, beK = bank()
                bdv = bd[:].rearrange("p (h e) -> p h e", h=4)
                bev = be[:].rearrange("p (h e) -> p h e", h=4)
                for h in range(4):
                    if full:
                        mm(bdv[:TT, h, :], ATT[r0:r1, h, :TT], VN[r0:r1, h, :], True, True, [ATTK, VNK], [bdK])
                    mm(bev[:, h, :], KDEC[r0:r1, h, :], VN[r0:r1, h, :], True, True, [KDECK, VNK], [beK])
                if full:
                    tt("dve", OO[r0:r1], OO[r0:r1], bdv[r0:r1], ALU.add, [OOK, bdK], [OOK])
                tt("dve", Sm[:], Sm[:], EGL[:, c, :].unsqueeze(2).to_broadcast([128, 4, 128]), ALU.mult, [SmK, EGLK], [SmK])
                tt("dve", Sm[:], Sm[:], bev[:, :, :], ALU.add, [SmK, beK], [SmK])
                cp("act", Sb_[:], Sm[:], [SmK], [SbK])
            if full:
                SSO, SSOK = g4["SSO"]
                for h in range(4):
                    act(Rr[:TT, 0, :], OO[:TT, h, :], AF.Square, [OOK], [RrK, SSOK], accum_out=SSO[:TT, h:h + 1])
                rsqrt_chain((G_("RO"), GK("RO")), (G_("SSO"), SSOK), 1.0 / 128, 1e-6)
                tt("dve", OO[:TT], OO[:TT], b4("RO"), ALU.mult, [OOK, GK("RO")], [OOK])
                tt("pool", OO[:TT], OO[:TT], dnwb[:TT, :].unsqueeze(1).to_broadcast([TT, 4, 128]), ALU.mult, [OOK, dnwbK], [OOK])
                tt("dve", MIX[:TT, 0:512].rearrange("p (h e) -> p h e", h=4), OO[:TT], SZD[:TT, :].rearrange("p (h e) -> p h e", h=4), ALU.mult, [OOK, SZDK], [MIXK])
            end_stream()
            stS = begin_stream([2, 3])
            by5, by5K = y5bank
            PAKs = [PAK + "0", PAK + "1"]; PBKs = [PBK + "0", PBK + "1"]
            AbKs = [AbK + "0", AbK + "1"]; BbKs = [BbK + "0", BbK + "1"]
            if kind == "p":
                PAf = PA[:].rearrange("p g t -> p (g t)"); PBf = PB[:].rearrange("p g t -> p (g t)")
                xs_ = XS[0][0]

                def s1(e):
                    i, q, hf, gs = e % 2, e // 2, e % 2, 4 * e
                    hs = slice(64 * hf, 64 * hf + 64)
                    b1, b1K = bank(); b2, b2K = bank()
                    b1v = b1[:].rearrange("p (g t) -> p g t", g=4)
                    b2v = b2[:].rearrange("p (g t) -> p g t", g=4)
                    for j in range(4):
                        mm(b1v[:, j, :], BBn[hs, q, j, :], uT[hs, q, :], True, True, [BBnK, uTK], [b1K])
                        mm(b2v[:, j, :], BBs[hs, q, j, :], uT[hs, q, :], True, True, [BBsK, uTK], [b2K])
                    pa = PAf[:, i * 512:(i + 1) * 512].rearrange("p (h g t) -> p g h t", h=2, g=4)
                    pb = PBf[:, i * 512:(i + 1) * 512].rearrange("p (h g t) -> p g h t", h=2, g=4)
                    cosb = COS[:, gs:gs + 4, :].unsqueeze(2).to_broadcast([128, 4, 2, 64])
                    sinb = SIN[:, gs:gs + 4, :].unsqueeze(2).to_broadcast([128, 4, 2, 64])
                    tt("dve", pa, b1v[:, :, :].rearrange("p g (h t) -> p g h t", h=2), cosb, ALU.mult, [b1K, COSK], [PAKs[i]])
                    tt("dve", pb, b2v[:, :, :].rearrange("p g (h t) -> p g h t", h=2), sinb, ALU.mult, [b2K, SINK], [PBKs[i]])
                    tt("pool", PAf[:, i * 512:(i + 1) * 512], PAf[:, i * 512:(i + 1) * 512], PBf[:, i * 512:(i + 1) * 512], ALU.add, [PAKs[i], PBKs[i]], [PAKs[i]])

                def s2(e, h):
                    i, gs = e % 2, 4 * e
                    k2 = (e % 2) * 2 + h
                    kk = f"s5sm{k2}"
                    pa3 = PAf[:, i * 512:(i + 1) * 512].rearrange("p (h g t) -> p h g t", h=2, g=4)
                    pb3 = PBf[:, i * 512:(i + 1) * 512].rearrange("p (h g t) -> p h g t", h=2, g=4)
                    if h == 0:
                        tt("dve", pa3[:, h, :, 0], pa3[:, h, :, 0], xs_[:, gs:gs + 4], ALU.add, [PAKs[i], XS0K[e]], [PAKs[i]])
                    c0 = i * 512 + h * 256
                    op("dve", lambda en, c0=c0, gs=gs: en.tensor_tensor_scan(
                        out=PBf[:, c0:c0 + 256], data0=RHOT[:, gs:gs + 4, :].rearrange("p g t -> p (g t)"),
                        data1=PAf[:, c0:c0 + 256], initial=0.0, op0=ALU.mult, op1=ALU.add),
                       r=[PAKs[i], RHOTK], w=[PBKs[i]])
                    tt("dve", AL2[:, k2, :], pb3[:, h, :, 63], RC[:, gs:gs + 4], ALU.mult, [PBKs[i], RCK], [kk + "a"])
                    tt("dve", BL2[:, k2, :], pb3[:, h, :, 63], RS[:, gs:gs + 4], ALU.mult, [PBKs[i], RSK], [kk + "b"])
                    bk, bkK = bank()
                    mm(bk[:, 0:4], psw[:, :], BL2[:, k2, :], True, True, [pswK, kk + "b"], [bkK])
                    if h == 0:
                        tt("dve", pa3[:, 1, :, 0], pa3[:, 1, :, 0], AL2[:, k2, :], ALU.add, [PAKs[i], kk + "a"], [PAKs[i]])
                        tt("dve", pa3[:, 1, :, 0], pa3[:, 1, :, 0], bk[:, 0:4], ALU.add, [PAKs[i], bkK], [PAKs[i]])
                    else:
                        tt("dve", xs_[:, gs:gs + 4], bk[:, 0:4], AL2[:, k2, :], ALU.add, [bkK, kk + "a"], [XS0K[e]])

                def s3(e):
                    i, q, gs = e % 2, e // 2, 4 * e
                    pb = PBf[:, i * 512:(i + 1) * 512].rearrange("p (h g t) -> p g h t", h=2, g=4)
                    cosb = COS[:, gs:gs + 4, :].unsqueeze(2).to_broadcast([128, 4, 2, 64])
                    sinb = SIN[:, gs:gs + 4, :].unsqueeze(2).to_broadcast([128, 4, 2, 64])
                    ab = Ab[:, 4 * i:4 * i + 4, :].rearrange("p g (h t) -> p g h t", h=2)
                    bbv = Bb[:, 4 * i:4 * i + 4, :].rearrange("p g (h t) -> p g h t", h=2)
                    tt("pool", ab, pb, cosb, ALU.mult, [PBKs[i], COSK], [AbKs[i]])
                    tt("dve", bbv, pb, sinb, ALU.mult, [PBKs[i], SINK], [BbKs[i]])
                    for g4_ in range(4):
                        g = gs + g4_
                        g8 = g % 8
                        mm(by5[:TT, g * 16:(g + 1) * 16], uT[:, q, :TT], diagD[:, q, g8 * 16:(g8 + 1) * 16], True, False, [uTK, diagDK], [by5K])
                        mm(by5[:TT, g * 16:(g + 1) * 16], Ab[:, 4 * i + g4_, :TT], Md1[:, g * 16:(g + 1) * 16], False, False, [AbKs[i], Md1K], [by5K])
                        mm(by5[:TT, g * 16:(g + 1) * 16], Bb[:, 4 * i + g4_, :TT], Md2[:, g * 16:(g + 1) * 16], False, True, [BbKs[i], Md2K], [by5K])

                s1(0)
                for e in range(8):
                    s2(e, 0)
                    if e + 1 < 8:
                        s1(e + 1)
                    s2(e, 1)
                    if full:
                        s3(e)
            else:
                for q in range(4):
                    PAv = PA[:, :, :TT].rearrange("p g (s l) -> p g s l", s=nsg)
                    PBv = PB[:, :, :TT].rearrange("p g (s l) -> p g s l", s=nsg)
                    for half in range(2):
                        b1, b1K = bank(); b2, b2K = bank()
                        b1v = b1[:].rearrange("p (g t) -> p g t", g=4)
                        b2v = b2[:].rearrange("p (g t) -> p g t", g=4)
                        hs = slice(64 * half, 64 * half + 64)
                        for j in range(4):
                            mm(b1v[:, j, :TT], BBn[hs, q, j, :], uT[hs, q, :TT], True, True, [BBnK, uTK], [b1K])
                            mm(b2v[:, j, :TT], BBs[hs, q, j, :], uT[hs, q, :TT], True, True, [BBsK, uTK], [b2K])
                        gs = 8 * q + 4 * half
                        cosb = COS[:, gs:gs + 4, 0:SL].unsqueeze(2).to_broadcast([128, 4, nsg, SL])
                        sinb = SIN[:, gs:gs + 4, 0:SL].unsqueeze(2).to_broadcast([128, 4, nsg, SL])
                        tt("dve", PAv[:, 4 * half:4 * half + 4], b1v[:, :, :TT].rearrange("p g (s l) -> p g s l", s=nsg), cosb, ALU.mult, [b1K, COSK], PAKs)
                        tt("dve", PBv[:, 4 * half:4 * half + 4], b2v[:, :, :TT].rearrange("p g (s l) -> p g s l", s=nsg), sinb, ALU.mult, [b2K, SINK], PBKs)
                    tt("pool", PA[:, :, :TT], PA[:, :, :TT], PB[:, :, :TT], ALU.add, PAKs + PBKs, PAKs)
                    for (s, st) in segs:
                        xs_, xsK = XS[st]
                        for g in range(8):
                            op("dve", lambda e, g=g, s=s, xs_=xs_, q=q: e.tensor_tensor_scan(
                                out=PB[:, g, s * SL:(s + 1) * SL], data0=rho[:, 8 * q + g:8 * q + g + 1].to_broadcast([128, SL]),
                                data1=PA[:, g, s * SL:(s + 1) * SL], initial=xs_[:, 8 * q + g:8 * q + g + 1], op0=ALU.mult, op1=ALU.add),
                               r=PAKs + [rhoK, xsK], w=PBKs)
                        e_ = (s + 1) * SL - 1
                        tt("dve", AL[:], PB[:, :, e_], COS[:, 8 * q:8 * q + 8, SL - 1], ALU.mult, PBKs + [COSK], [ALK])
                        tt("dve", BL[:], PB[:, :, e_], SIN[:, 8 * q:8 * q + 8, SL - 1], ALU.mult, PBKs + [SINK], [BLK])
                        bk, bkK = bank()
                        mm(bk[:, 0:8], ident[:, :], AL[:], True, False, [identK, ALK], [bkK])
                        mm(bk[:, 0:8], psw[:, :], BL[:], False, True, [pswK, BLK], [bkK])
                        cp("act", xs_[:, 8 * q:8 * q + 8], bk[:, 0:8], [bkK], [xsK])
                    if not full:
                        continue
                    cosf = COS[:, 8 * q:8 * q + 8, 0:SL].unsqueeze(2).to_broadcast([128, 8, nsg, SL])
                    sinf = SIN[:, 8 * q:8 * q + 8, 0:SL].unsqueeze(2).to_broadcast([128, 8, nsg, SL])
                    tt("pool", Ab[:, :, :TT].rearrange("p g (s l) -> p g s l", s=nsg), PBv, cosf, ALU.mult, PBKs + [COSK], AbKs)
                    tt("dve", Bb[:, :, :TT].rearrange("p g (s l) -> p g s l", s=nsg), PBv, sinf, ALU.mult, PBKs + [SINK], BbKs)
                    for g8 in range(8):
                        g = 8 * q + g8
                        mm(by5[:TT, g * 16:(g + 1) * 16], uT[:, q, :TT], diagD[:, q, g8 * 16:(g8 + 1) * 16], True, False, [uTK, diagDK], [by5K])
                        mm(by5[:TT, g * 16:(g + 1) * 16], Ab[:, g8, :TT], Md1[:, g * 16:(g + 1) * 16], False, False, AbKs + [Md1K], [by5K])
                        mm(by5[:TT, g * 16:(g + 1) * 16], Bb[:, g8, :TT], Md2[:, g * 16:(g + 1) * 16], False, True, BbKs + [Md2K], [by5K])
            end_stream()
            if part == "rest_nomerge":
                return [stG, stS]
            merge_streams([stG, stS])
            if not full:
                return None
            tile_tail(kind, ti, ydst, b, TT)
            return None

        def tile_tail(kind, ti, ydst, b, TT):
            x, xK = xt, xtK
            SZS, SZSK = SZSs[b]
            by5, by5K = y5bank
            cp("act", Y5[:TT, :], by5[:TT, :], [by5K], [Y5K])
            tt("dve", G1[:TT, :], Y5[:TT, :], Y5[:TT, :], ALU.mult, [Y5K], [G1K])
            ts("dve", G1[:TT, :], G1[:TT, :], 0.044715, 1.0, ALU.mult, ALU.add, [G1K], [G1K])
            tt("dve", G1[:TT, :], G1[:TT, :], Y5[:TT, :], ALU.mult, [G1K, Y5K], [G1K])
            act(G2[:TT, :], G1[:TT, :], AF.Sigmoid, [G1K], [G2K], scale=2.0 * math.sqrt(2.0 / PI))
            tt("dve", GY[:TT, :], Y5[:TT, :], G2[:TT, :], ALU.mult, [Y5K, G2K], [GYK])
            cp("pool", GYb[:TT, :], GY[:TT, :], [GYK], [GYbK])
            pg, pgK = bbank()
            pgv = pg[:].rearrange("p (k t) -> p k t", k=8)
            for q in range(4):
                tr(pgv[:, q, :TT], GYb[:TT, q * 128:(q + 1) * 128], identb[:TT, :TT], [GYbK, identbK], [pgK])
            cp("act", GYT[:, :, :TT], pgv[:, 0:4, :TT], [pgK], [GYTK])
            bg, bgK = bank()
            for q in range(4):
                mm(bg[:TT, :], GYT[:, q, :TT], glub[:, q, :], q == 0, q == 3, [GYTK, glubK], [bgK])
            tt("dve", G1[:TT, :], bg[:TT, :], glbb[:TT, :], ALU.add, [bgK, glbbK], [G1K])
            act(G2[:TT, :], G1[:TT, :], AF.Sigmoid, [G1K], [G2K])
            tt("dve", G1[:TT, :], GY[:TT, :], G2[:TT, :], ALU.mult, [GYK, G2K], [G1K])
            tt("dve", MIX[:TT, 512:1024], G1[:TT, :], SZS[:TT, :], ALU.mult, [G1K, SZSK], [MIXK])
            if dbg and kind == "p" and ti == TP // 128 - 1:
                dma(dbg_mix[:, :], MIX[:TT, :], r=[MIXK])
            pm, pmK = bbank()
            pmv = pm[:].rearrange("p (k t) -> p k t", k=8)
            for k in range(8):
                tr(pmv[:, k, :TT], MIX[:TT, k * 128:(k + 1) * 128], identb[:TT, :TT], [MIXK, identbK], [pmK])
            cp("act", MIXT[:, :, :TT], pmv[:, :, :TT], [pmK], [MIXTK])
            for half in range(2):
                bo, boK = bank()
                for k in range(8):
                    mm(bo[:TT, :], MIXT[:, k, :TT], woutb[:, k, half * 512:(half + 1) * 512], k == 0, k == 7, [MIXTK, woutbK], [boK])
                tt("dve", XN[:TT, half * 512:(half + 1) * 512], XN[:TT, half * 512:(half + 1) * 512], bo[:TT, :], ALU.add, [XNK, boK], [XNK])
            ss2, ss2K = sc["ss2"]; r2, r2K = sc["r2"]
            act(MIXT[:TT].rearrange("p k t -> p (k t)"), XN[:TT, :], AF.Square, [XNK], [MIXTK, ss2K], accum_out=ss2[:TT, :])
            rsqrt_chain((r2[:TT, :], r2K), (ss2[:TT, :], ss2K), 1.0 / 1024, 1e-6)
            ts("dve", XN[:TT, :], XN[:TT, :], r2[:TT, 0:1], None, ALU.mult, None, [XNK, r2K], [XNK])
            tt("pool", XN[:TT, :], XN[:TT, :], fnwb[:TT, :], ALU.mult, [XNK, fnwbK], [XNK])
            dma(ydst, XN[:TT, :], r=[XNK])

        def write_states(st, conv_dst, dn_dst, re_dst, im_dst, RAW, RAWK, s, L):
            for t in range(12):
                dma(conv_dst[:, t * 128:(t + 1) * 128].rearrange("r c -> c r"), RAW[:, t, s, L:L + 3], r=[RAWK], slow=True)
            dma(dn_dst.rearrange("h d e -> d h e"), Sst[st][0][:], r=[Sst[st][1]])
            if st == 0:
                op("dve", lambda e: e.reciprocal(out=RC[:], in_=rho[:]), r=[rhoK, RCK], w=[RCK])
                tt("dve", XS[0][0][:], XS[0][0][:], RC[:], ALU.mult, XS0K + [RCK], XS0K)
            bk, bkK = bank()
            tr(bk[0:32, 0:128], XS[st][0][:, :], ident[:, :], (XS0K if st == 0 else [XS[st][1]]) + [identK], [bkK])
            cp("dve", xso, bk[0:32, 0:128], [bkK], [xsoK])
            dma(re_dst, xso[:, 0:64], r=[xsoK])
            dma(im_dst, xso[:, 64:128], r=[xsoK])

        do_tile("s", 0, xs[:, :], ys[:, :])
        for s in range(NS):
            write_states(1 + s, o_convs[s], o_dns[s], o_res[s], o_ims[s], RAWs, RAWsK, s, 32)
        op("pool", lambda e: e.memset(RAWp[:], 0.0), w=[RAWpK])
        npre = NPRE // 128
        tiles = []
        for ti in range(npre):
            tiles.append((xpre[ti * 128:(ti + 1) * 128, :], None, ("stateq" if ti == npre - 1 else "state")))
        for ti in range(TP // 128):
            tiles.append((xp[ti * 128:(ti + 1) * 128, :], yp[ti * 128:(ti + 1) * 128, :], "full"))
        front_done = False
        for t, (xsrc_, ydst_, mode_) in enumerate(tiles):
            b_ = t % 2
            if not front_done:
                do_tile("p", t, xsrc_, ydst_, mode=mode_, b=b_, part="front")
            front_done = False
            if t + 1 < len(tiles):
                sts = do_tile("p", t, xsrc_, ydst_, mode=mode_, b=b_, part="rest_nomerge")
                nx = tiles[t + 1]
                stF = begin_stream([4, "y5"] if mode_ != "full" else [4], bb=0)
                do_tile("p", t + 1, nx[0], nx[1], mode=nx[2], b=(t + 1) % 2, part="front")
                end_stream()
                merge_streams(sts + [stF])
                front_done = True
                if mode_ == "full":
                    do_tile("p", t, xsrc_, ydst_, mode=mode_, b=b_, part="tail")
            else:
                do_tile("p", t, xsrc_, ydst_, mode=mode_, b=b_, part="rest")
        write_states(0, o_convp, o_dnp, o_rep, o_imp, RAWp, RAWpK, 0, 128)
        P.emit()
    return nc


_CACHE = {}


DBG = ()
LAST = {}
_CACHE = {}


def run(inputs, T, n_cores=8):
    NS = 2
    f = lambda a: np.ascontiguousarray(np.asarray(a, dtype=np.float32))
    xpf = f(inputs["x_prompt"])[:, :T]
    xsf = f(inputs["x_sample"])
    nb = xpf.shape[0]
    nseg = max(1, n_cores // nb)
    TP = T // nseg
    NPRE = (nseg - 1) * TP
    key = (TP, NPRE)
    if key not in _CACHE:
        _CACHE[key] = build(TP, NS, dbg=DBG, NPRE=NPRE)
    nc = _CACHE[key]
    in_maps = []
    wmap = {k: f(inputs[k]).reshape(WSHAPES[k]) for k in WSHAPES}
    cmap = {"c_" + k: v for k, v in CONSTS.items()}
    for c in range(n_cores):
        b, k = c // nseg, c % nseg
        m = {}
        if b < nb:
            m["xp"] = f(xpf[b, k * TP:(k + 1) * TP])
            if NPRE:
                pre = np.zeros((NPRE, 1024), np.float32)
                if k > 0:
                    pre[NPRE - k * TP:] = xpf[b, :k * TP]
                m["xpre"] = pre
        else:
            m["xp"] = np.zeros((TP, 1024), np.float32)
            if NPRE:
                m["xpre"] = np.zeros((NPRE, 1024), np.float32)
        sl = slice(NS * c, NS * c + NS)
        m["xs"] = f(xsf[sl]).reshape(NS * 32, 1024)
        m["cconv"] = f(inputs["cache_conv"][0, sl])
        m["sdn"] = f(inputs["state_dn"][0, sl])
        m["sre"] = f(inputs["state_s5_re"][0, sl])
        m["sim"] = f(inputs["state_s5_im"][0, sl])
        m.update(wmap); m.update(cmap)
        in_maps.append(m)
    res = run_bass_kernel_spmd(nc, in_maps, core_ids=list(range(n_cores)))
    R = res.results
    LAST['R'] = R
    y_prompt = np.stack([np.concatenate([R[b * nseg + k]["yp"] for k in range(nseg)]) for b in range(nb)])
    y_sample = np.concatenate([R[c]["ys"].reshape(NS, 32, 1024) for c in range(n_cores)])
    last = [b * nseg + nseg - 1 for b in range(nb)]
    convp = np.stack([R[c]["convp"] for c in last])[None]
    dnp = np.stack([R[c]["dnp"] for c in last])[None]
    rep = np.stack([R[c]["rep"] for c in last])[None]
    imp = np.stack([R[c]["imp"] for c in last])[None]
    convs = np.concatenate([R[c]["convs"] for c in range(n_cores)])[None]
    dns = np.concatenate([R[c]["dns"] for c in range(n_cores)])[None]
    res_ = np.concatenate([R[c]["res"] for c in range(n_cores)])[None]
    ims = np.concatenate([R[c]["ims"] for c in range(n_cores)])[None]
    outs = (y_prompt, y_sample, convp, dnp, rep, imp, convs, dns, res_, ims)
    return tuple(np.ascontiguousarray(o, dtype=np.float32) for o in outs)


def kernel(**inputs):
    return run(inputs, T=16384, n_cores=8)
```
